# Optimizing a Trainium2 kernel written in Bass

```python
import jax, jax.numpy as jnp
from jax import lax
import numpy as np

D_MODEL = 4096
BATCH = 2
SEQ = 8192
DEPTH = 1

CHUNK = 64
D_MIX = D_MODEL
GDN_HEAD_DIM = 128
GDN_HEADS = (D_MIX // 2) // GDN_HEAD_DIM
GDN_WIDTH = GDN_HEADS * GDN_HEAD_DIM
SSM_HEAD_DIM = 64
SSM_WIDTH = D_MIX - GDN_WIDTH
SSM_HEADS = SSM_WIDTH // SSM_HEAD_DIM
SSM_GROUPS = 8
SSM_STATE = 128
SSM_GN = SSM_GROUPS * SSM_STATE
SHORT_CONV = 4
FFN_CONV = 3
D_FF = 256 * ((8 * D_MODEL // 3 + 255) // 256)
EPS = 1e-6

PROJ_SIZES = (3 * GDN_WIDTH, GDN_WIDTH, GDN_HEADS, GDN_HEADS,
              SSM_WIDTH, SSM_WIDTH + 2 * SSM_GN, SSM_HEADS)
D_IN_PROJ = sum(PROJ_SIZES)
SPLIT_POINTS = tuple(int(s) for s in np.cumsum(PROJ_SIZES)[:-1])

kernel_name = "hybrid_gdn_ssd_convffn_block"


def rmsnorm(x, w):
    xf = x.astype(jnp.float32)
    y = xf * lax.rsqrt(jnp.mean(xf * xf, axis=-1, keepdims=True) + EPS)
    return (y * w.astype(jnp.float32)).astype(x.dtype)


def causal_dwconv(x, w, b=None):
    K, C = w.shape
    y = lax.conv_general_dilated(
        x, w[:, None, :].astype(x.dtype), window_strides=(1,), padding=[(K - 1, 0)],
        dimension_numbers=("NWC", "WIO", "NWC"), feature_group_count=C)
    if b is not None:
        y = y + b.astype(x.dtype)
    return y


def l2norm(x):
    return x * lax.rsqrt(jnp.sum(x * x, axis=-1, keepdims=True) + EPS)


def gated_delta_chunked(q, k, v, g, beta):
    Bsz, L, H, Dk = q.shape
    Dv = v.shape[-1]
    N = L // CHUNK

    def to_chunks(t):
        return jnp.moveaxis(t.reshape(Bsz, N, CHUNK, H, *t.shape[3:]), 3, 1)

    q, k, v, g, beta = (to_chunks(t) for t in (q, k, v, g, beta))
    q = q * (Dk ** -0.5)
    gc = jnp.cumsum(g, axis=-1)
    idx = jnp.arange(CHUNK)
    causal = idx[:, None] >= idx[None, :]
    strict = idx[:, None] > idx[None, :]
    gamma = jnp.exp(jnp.where(causal, gc[..., :, None] - gc[..., None, :], -jnp.inf))
    kb = k * beta[..., None]
    A = jnp.where(strict, jnp.einsum("bhncd,bhnsd->bhncs", kb, k) * gamma, 0.0)
    eye = jnp.eye(CHUNK, dtype=A.dtype)
    T = lax.linalg.triangular_solve(A + eye, jnp.broadcast_to(eye, A.shape),
                                    left_side=True, lower=True, unit_diagonal=True)
    u = jnp.einsum("bhncs,bhnsv->bhncv", T, v * beta[..., None])
    w = jnp.einsum("bhncs,bhnsk->bhnck", T, kb * jnp.exp(gc)[..., None])
    attn = jnp.einsum("bhncd,bhnsd->bhncs", q, k) * gamma
    q_dec = q * jnp.exp(gc)[..., None]
    k_dec = k * jnp.exp(gc[..., -1:] - gc)[..., None]
    chunk_decay = jnp.exp(gc[..., -1])

    def step(S, inp):
        q_c, w_c, u_c, a_c, k_c, d_c = inp
        v_new = u_c - jnp.einsum("bhck,bhkv->bhcv", w_c, S)
        o = jnp.einsum("bhck,bhkv->bhcv", q_c, S) + jnp.einsum("bhcs,bhsv->bhcv", a_c, v_new)
        S = S * d_c[..., None, None] + jnp.einsum("bhck,bhcv->bhkv", k_c, v_new)
        return S, o

    xs = tuple(jnp.moveaxis(t, 2, 0) for t in (q_dec, w, u, attn, k_dec, chunk_decay))
    S0 = jnp.zeros((Bsz, H, Dk, Dv), q.dtype)
    _, o = lax.scan(step, S0, xs)
    return jnp.transpose(o, (1, 0, 3, 2, 4)).reshape(Bsz, L, H, Dv)


def ssd_chunked(x, dt, A, Bm, Cm):
    Bsz, L, H, P = x.shape
    G, Ns = Bm.shape[2], Bm.shape[3]
    R = H // G
    Nc = L // CHUNK
    x = x.reshape(Bsz, Nc, CHUNK, G, R, P)
    dt = dt.reshape(Bsz, Nc, CHUNK, G, R)
    Bm = Bm.reshape(Bsz, Nc, CHUNK, G, Ns)
    Cm = Cm.reshape(Bsz, Nc, CHUNK, G, Ns)
    acs = jnp.cumsum(dt * A.reshape(G, R), axis=2)
    idx = jnp.arange(CHUNK)
    causal = (idx[:, None] >= idx[None, :])[:, :, None, None]
    Lmat = jnp.exp(jnp.where(causal, acs[:, :, :, None] - acs[:, :, None, :], -jnp.inf))
    xdt = x * dt[..., None]
    CB = jnp.einsum("bclgn,bcsgn->bclsg", Cm, Bm)
    y_diag = jnp.einsum("bclsgr,bcsgrp->bclgrp", CB[..., None] * Lmat, xdt)
    decay_states = jnp.exp(acs[:, :, -1:] - acs)
    states = jnp.einsum("bclgn,bclgrp->bcgrpn", Bm, xdt * decay_states[..., None])
    chunk_decay = jnp.exp(acs[:, :, -1])

    def step(h, inp):
        s_c, d_c = inp
        return h * d_c[..., None, None] + s_c, h

    h0 = jnp.zeros((Bsz, G, R, P, Ns), x.dtype)
    _, prev = lax.scan(step, h0, (jnp.moveaxis(states, 1, 0), jnp.moveaxis(chunk_decay, 1, 0)))
    y_off = jnp.einsum("bclgn,cbgrpn->bclgrp", Cm, prev) * jnp.exp(acs)[..., None]
    return (y_diag + y_off).reshape(Bsz, L, H, P)


def gdn_mixer(qkv, gate, b_raw, a_raw, conv_w, A_log, dt_bias, norm_w):
    Bsz, L, _ = qkv.shape
    f32 = jnp.float32
    qkv = jax.nn.silu(causal_dwconv(qkv, conv_w)).astype(f32)
    qkv = qkv.reshape(Bsz, L, 3, GDN_HEADS, GDN_HEAD_DIM)
    q, k, v = l2norm(qkv[:, :, 0]), l2norm(qkv[:, :, 1]), qkv[:, :, 2]
    beta = jax.nn.sigmoid(b_raw.astype(f32))
    g = -jnp.exp(A_log.astype(f32)) * jax.nn.softplus(a_raw.astype(f32) + dt_bias.astype(f32))
    o = gated_delta_chunked(q, k, v, g, beta)
    o = rmsnorm(o, norm_w) * jax.nn.silu(gate.astype(f32).reshape(Bsz, L, GDN_HEADS, GDN_HEAD_DIM))
    return o.reshape(Bsz, L, GDN_WIDTH).astype(qkv.dtype)


def ssd_mixer(z, xbc, dt_raw, conv_w, conv_b, A_log, dt_bias, D_skip, norm_w):
    Bsz, L, _ = xbc.shape
    f32 = jnp.float32
    xbc = jax.nn.silu(causal_dwconv(xbc, conv_w, conv_b)).astype(f32)
    xs = xbc[..., :SSM_WIDTH].reshape(Bsz, L, SSM_HEADS, SSM_HEAD_DIM)
    Bm = xbc[..., SSM_WIDTH:SSM_WIDTH + SSM_GN].reshape(Bsz, L, SSM_GROUPS, SSM_STATE)
    Cm = xbc[..., SSM_WIDTH + SSM_GN:].reshape(Bsz, L, SSM_GROUPS, SSM_STATE)
    dt = jax.nn.softplus(dt_raw.astype(f32) + dt_bias.astype(f32))
    A = -jnp.exp(A_log.astype(f32))
    y = ssd_chunked(xs, dt, A, Bm, Cm) + xs * D_skip.astype(f32)[:, None]
    y = y.reshape(Bsz, L, SSM_WIDTH) * jax.nn.silu(z.astype(f32))
    yg = y.reshape(Bsz, L, SSM_GROUPS, SSM_WIDTH // SSM_GROUPS)
    yg = yg * lax.rsqrt(jnp.mean(yg * yg, axis=-1, keepdims=True) + EPS)
    return (yg.reshape(Bsz, L, SSM_WIDTH) * norm_w.astype(f32)).astype(z.dtype)


def conv_gated_mlp(h, w_gate, w_up, conv_w, conv_b, w_down):
    g = causal_dwconv(h @ w_gate, conv_w, conv_b)
    return (jax.nn.silu(g) * (h @ w_up)) @ w_down


def setup_inputs(seed: int = 0) -> dict:
    key = jax.random.key(seed)
    ks = jax.random.split(key, 24)
    f32 = jnp.float32

    def nrm(k, shape, scale):
        return jax.random.normal(k, shape, f32) * scale

    def gain(k, shape):
        return 1.0 + 0.05 * jax.random.normal(k, shape, f32)

    def dt_bias(k, n):
        dt = jnp.exp(jax.random.uniform(k, (DEPTH, n), f32, np.log(1e-3), np.log(1e-1)))
        return dt + jnp.log(-jnp.expm1(-dt))

    def a_log(k, n):
        return jnp.log(jax.random.uniform(k, (DEPTH, n), f32, 1.0, 16.0))

    ssm_conv_ch = SSM_WIDTH + 2 * SSM_GN
    return {
        "x": jax.random.normal(ks[0], (BATCH, SEQ, D_MODEL), f32),
        "norm1_w": gain(ks[1], (DEPTH, D_MODEL)),
        "w_in": nrm(ks[2], (DEPTH, D_MODEL, D_IN_PROJ), D_MODEL ** -0.5),
        "gdn_conv_w": nrm(ks[3], (DEPTH, SHORT_CONV, 3 * GDN_WIDTH), SHORT_CONV ** -0.5),
        "gdn_A_log": a_log(ks[4], GDN_HEADS),
        "gdn_dt_bias": dt_bias(ks[5], GDN_HEADS),
        "gdn_norm_w": gain(ks[6], (DEPTH, GDN_HEAD_DIM)),
        "ssm_conv_w": nrm(ks[7], (DEPTH, SHORT_CONV, ssm_conv_ch), SHORT_CONV ** -0.5),
        "ssm_conv_b": nrm(ks[8], (DEPTH, ssm_conv_ch), 0.02),
        "ssm_A_log": a_log(ks[9], SSM_HEADS),
        "ssm_dt_bias": dt_bias(ks[10], SSM_HEADS),
        "ssm_D": gain(ks[11], (DEPTH, SSM_HEADS)),
        "ssm_norm_w": gain(ks[12], (DEPTH, SSM_WIDTH)),
        "w_out": nrm(ks[13], (DEPTH, D_MIX, D_MODEL), D_MIX ** -0.5),
        "norm2_w": gain(ks[14], (DEPTH, D_MODEL)),
        "ffn_w_gate": nrm(ks[15], (DEPTH, D_MODEL, D_FF), D_MODEL ** -0.5),
        "ffn_w_up": nrm(ks[16], (DEPTH, D_MODEL, D_FF), D_MODEL ** -0.5),
        "ffn_conv_w": nrm(ks[17], (DEPTH, FFN_CONV, D_FF), FFN_CONV ** -0.5),
        "ffn_conv_b": nrm(ks[18], (DEPTH, D_FF), 0.02),
        "ffn_w_down": nrm(ks[19], (DEPTH, D_FF, D_MODEL), D_FF ** -0.5),
        "norm_f_w": gain(ks[20], (D_MODEL,)),
    }


def reference(x, norm1_w, w_in, gdn_conv_w, gdn_A_log, gdn_dt_bias, gdn_norm_w,
              ssm_conv_w, ssm_conv_b, ssm_A_log, ssm_dt_bias, ssm_D, ssm_norm_w,
              w_out, norm2_w, ffn_w_gate, ffn_w_up, ffn_conv_w, ffn_conv_b, ffn_w_down,
              norm_f_w):
    h = x
    for i in range(DEPTH):
        hn = rmsnorm(h, norm1_w[i])
        proj = hn @ w_in[i]
        qkv, gate, b_raw, a_raw, z, xbc, dt_raw = jnp.split(proj, SPLIT_POINTS, axis=-1)
        y_gdn = gdn_mixer(qkv, gate, b_raw, a_raw, gdn_conv_w[i], gdn_A_log[i],
                          gdn_dt_bias[i], gdn_norm_w[i])
        y_ssm = ssd_mixer(z, xbc, dt_raw, ssm_conv_w[i], ssm_conv_b[i], ssm_A_log[i],
                          ssm_dt_bias[i], ssm_D[i], ssm_norm_w[i])
        h = h + jnp.concatenate([y_gdn, y_ssm], axis=-1) @ w_out[i]
        hn = rmsnorm(h, norm2_w[i])
        h = h + conv_gated_mlp(hn, ffn_w_gate[i], ffn_w_up[i], ffn_conv_w[i],
                               ffn_conv_b[i], ffn_w_down[i])
    return rmsnorm(h, norm_f_w)
```

```python
import bisect
import contextlib
import numpy as np
import concourse.bass as bass
import concourse.mybir as mybir
from concourse.bass_utils import run_bass_kernel_spmd

F32 = mybir.dt.float32
BF16 = mybir.dt.bfloat16
I32 = mybir.dt.int32
AF = mybir.ActivationFunctionType
ALU = mybir.AluOpType
EPS = 1e-6
NEG = -30000.0


class Tl:
    __slots__ = ("a", "name", "w", "r", "dsem", "dn", "multi")

    def __init__(self, a, name, multi=False):
        self.a = a
        self.name = name
        self.w = [] if multi else None
        self.r = {}
        self.dsem = {}
        self.dn = {}
        self.multi = multi


class TlView:
    def __init__(self, parent, a):
        object.__setattr__(self, "p", parent)
        object.__setattr__(self, "a", a)

    def __getattr__(self, n):
        return getattr(object.__getattribute__(self, "p"), n)

    def __setattr__(self, n, v):
        setattr(object.__getattribute__(self, "p"), n, v)


class KB:
    ENG = ("pe", "dve", "act", "pool", "sp")

    def __init__(self, nc, es):
        self.nc = nc
        self.es = es
        self.es_sem = es
        self.dma_tiles = []
        self.q = {"pe": nc.tensor, "dve": nc.vector, "act": nc.scalar, "pool": nc.gpsimd, "sp": nc.sync}
        self.sem = {e: es.enter_context(nc.semaphore("s_" + e)) for e in self.ENG}
        self.emitted = {e: 0 for e in self.ENG}
        self.last = {e: None for e in self.ENG}
        self.sigs = {e: [] for e in self.ENG}
        self.seen = {e: {} for e in self.ENG}
        self.nsem = 0
        self.uid = 0

    def sb(self, name, shape, dt):
        t = self.es.enter_context(self.nc.sbuf_tensor("sb_" + name, list(shape), dt))
        return Tl(t[:], name)

    def ps(self, name, shape, dt):
        t = self.es.enter_context(self.nc.psum_tensor("ps_" + name, list(shape), dt))
        return Tl(t[:], name)

    def dram(self, name, shape, dt, kind="Internal", multi=False):
        t = self.nc.dram_tensor(name, list(shape), dt, kind=kind)
        return Tl(t.ap(), name, multi=multi)

    def view(self, ap, name):
        return Tl(ap, name)

    def newsem(self, name):
        self.nsem += 1
        return self.es_sem.enter_context(self.nc.semaphore(name))

    def _setw(self, t, dep):
        if t.multi:
            t.w = [d for d in t.w if d[1] is not dep[1]] + [dep]
        else:
            t.w = dep
        t.r = {}

    def barrier(self):
        engs = ["pe", "dve", "act"]
        for e in engs:
            if self.last[e] is not None and (not self.sigs[e] or self.sigs[e][-1] < self.emitted[e]):
                self.last[e].then_inc(self.sem[e], 1)
                self.sigs[e].append(self.emitted[e])
        for e in self.ENG:
            for o in engs:
                if o != e and self.sigs[o]:
                    self._wait_sem(e, self.sem[o], len(self.sigs[o]))
            for tl in self.dma_tiles:
                for qc in tl.dsem:
                    self._wait_sem(e, tl.dsem[qc], tl.dn[qc])

    def _wait_sem(self, eng, sem, val):
        d = self.seen[eng]
        if d.get(sem.name, 0) >= val:
            return
        d[sem.name] = val
        self.q[eng].wait_ge(sem, val)

    def _need(self, eng, dep):
        if dep is None:
            return
        if isinstance(dep, list):
            for d in dep:
                self._need(eng, d)
            return
        if dep[0] == "e":
            _, e, idx = dep
            sg = self.sigs[e]
            if not sg or sg[-1] < idx:
                self.last[e].then_inc(self.sem[e], 1)
                sg.append(self.emitted[e])
            pos = bisect.bisect_left(sg, idx)
            self._wait_sem(eng, self.sem[e], pos + 1)
        else:
            tl = dep[1]
            for qc in tl.dsem:
                self._wait_sem(eng, tl.dsem[qc], tl.dn[qc])

    def _deps(self, eng, reads, writes, is_dma=False):
        for t in reads:
            w = t.w
            if w is not None:
                if not isinstance(w, list) and w[0] == "e" and w[1] == eng and eng == "pe":
                    continue
                self._need(eng, w)
        for t in writes:
            w = t.w
            if w is not None and (isinstance(w, list) or not (w[0] == "e" and w[1] == eng and eng == "pe")):
                self._need(eng, w)
            for key, dep in t.r.items():
                if dep[0] == "e" and dep[1] == eng and eng == "pe":
                    continue
                self._need(eng, dep)

    def op(self, eng, reads, writes, fn):
        self._deps(eng, reads, writes)
        ins = fn(self.q[eng])
        self.emitted[eng] += 1
        idx = self.emitted[eng]
        self.last[eng] = ins
        for t in reads:
            t.r[eng] = ("e", eng, idx)
        for t in writes:
            self._setw(t, ("e", eng, idx))
        return ins

    def V(self, reads, writes, fn):
        return self.op("dve", reads, writes, fn)

    def A(self, reads, writes, fn):
        return self.op("act", reads, writes, fn)

    def P(self, reads, writes, fn):
        return self.op("pe", reads, writes, fn)

    def dma(self, qe, dst, dst_ap, src, src_ap, semtl=None, kind="dma", in_off=None):
        reads = [src] if src is not None else []
        writes = [dst] if dst is not None else []
        self._deps(qe, reads, writes, is_dma=True)
        if semtl is None:
            semtl = dst if dst is not None else src
        qc = "sw" if qe == "pool" else "hw"
        if qc not in semtl.dsem:
            if not semtl.dsem:
                self.dma_tiles.append(semtl)
            semtl.dsem[qc] = self.newsem("d%s_%s" % (qc, semtl.name))
            semtl.dn[qc] = 0
        e = self.q[qe]
        if kind == "dma":
            ins = e.dma_start(out=dst_ap, in_=src_ap)
            inc = 16
        elif kind == "ind":
            ins = e.indirect_dma_start(out=dst_ap, out_offset=None, in_=src_ap, in_offset=in_off)
            inc = 16
        else:
            raise ValueError(kind)
        semtl.dn[qc] += inc
        ins.then_inc(semtl.dsem[qc], inc)
        self.emitted[qe] += 1
        self.last[qe] = None
        dep = ("d", semtl)
        if src is not None:
            src.r["d_" + semtl.name] = dep
        if dst is not None:
            self._setw(dst, dep)
        return ins

    def collective(self, src, dst, src_ap, dst_ap, groups, cctl):
        self._deps("pool", [src], [dst], is_dma=True)
        if "cc" not in cctl.dsem:
            if not cctl.dsem:
                self.dma_tiles.append(cctl)
            cctl.dsem["cc"] = self.newsem("dcc_" + cctl.name)
            cctl.dn["cc"] = 0
        ins = self.q["pool"].collective_compute("AllGather", ALU.bypass, replica_groups=groups, ins=[src_ap], outs=[dst_ap])
        cctl.dn["cc"] += 1
        ins.then_inc(cctl.dsem["cc"], 1)
        self.emitted["pool"] += 1
        self.last["pool"] = None
        dep = ("d", cctl)
        src.r["d_" + cctl.name] = dep
        self._setw(dst, dep)

    def finish(self, tls):
        for t in tls:
            if t.w is not None:
                self._need("sp", t.w)


class RR:
    def __init__(self, items):
        self.items = items
        self.i = 0

    def get(self):
        x = self.items[self.i % len(self.items)]
        self.i += 1
        return x


KM = 32
TT = 512
WBE = 8192


LOCK = [True]


def lockstep(gens):
    gens = list(gens)
    if not LOCK[0]:
        for g in gens:
            for _ in g:
                pass
        return
    while gens:
        nxt = []
        for g in gens:
            try:
                next(g)
                nxt.append(g)
            except StopIteration:
                pass
        gens = nxt


def build_program(cfg):
    L, D, DFF = cfg["L"], cfg["D"], cfg["DFF"]
    dbg = cfg.get("debug", False)
    LOCK[0] = cfg.get("lock", True)
    KD = D // 128
    NT = L // TT
    LQ = L // 4
    NT3 = LQ // TT
    FB = DFF // 128
    FH = FB // 2
    NG = D // 512
    assert FB % 2 == 0 and LQ % TT == 0 and D % 512 == 0 and FH <= 44 and 2 * KD * 128 <= WBE

    nc = bass.Bass("TRN2", target_bir_lowering=False)
    es = contextlib.ExitStack()
    with es:
        k = KB(nc, es)

        def din(name, shape, dt=F32):
            return nc.dram_tensor(name, list(shape), dt, kind="ExternalInput").ap()

        dump_on = cfg.get("dump", False)
        dumped = []

        def dump(name, tl, ap, shape, dt=F32):
            if not dump_on:
                return
            d = k.dram("D_" + name, list(shape), dt, kind="ExternalOutput")
            k.dma("sp", d, d.a, tl, ap)
            dumped.append(d)

        x_in = din("x", [L, D])
        xq_in = din("xq", [128 + LQ, D])
        w1_in = din("w1", [28, 128, KD, 128])
        wsm_in = din("wsm", [128, KD, 16])
        n1c_in = din("n1c", [128, KD])
        n2c_in = din("n2c", [128, KD])
        nfb_in = din("nfb", [D])
        cw_in = din("cw", [128, 20, 4])
        cb_in = din("cb", [128, 20])
        dtb_in = din("dtb", [128, 12])
        alog_in = din("alog", [128, 12])
        dcol_in = din("dcol", [128, 4])
        ssmnw_in = din("ssmnw", [128, 4])
        gdnnw_in = din("gdnnw", [128, 1])
        masks_in = din("masks", [4, 128, 128])
        wo_in = din("wo", [NG, 128, KM, 512])
        wg_in = din("wg", [FB, 128, 2, KD, 128])
        wd_in = din("wd", [DFF, D])
        fcw_in = din("fcw", [128, FB, 3])
        fcb_in = din("fcb", [128, FB])
        idx_in = din("idx", [128, (NT3 + 1) * KM], I32)
        out_d = k.dram("out", [LQ, D], F32, kind="ExternalOutput")
        ybuf = k.dram("ybuf", [NT, 1024, TT], BF16)
        gath = k.dram("gath", [(NT + 1) * 4096, TT], BF16, multi=True)
        hbuf = k.dram("hbuf", [128 + LQ, D], F32)
        WSLc = WBE // 512
        NSL = (FH + WSLc - 1) // WSLc
        w1s = k.dram("w1s", [28, 128, KD * 128], BF16)
        wos = k.dram("wos", [NG * 2, 128, KM * 256], BF16)
        wgs = k.dram("wgs", [FB, 128, 2 * KD * 128], BF16)
        wds = k.dram("wds", [2 * NG * NSL, 128, WBE], BF16)
        if dbg:
            ydbg = k.dram("ydbg", [NT, 1024, TT], BF16, kind="ExternalOutput")
            hdbg = k.dram("hdbg", [128 + LQ, D], F32, kind="ExternalOutput")

        cst = {"sp": Tl(None, "cst_sp"), "pool": Tl(None, "cst_pool")}

        conv_jobs = []
        for g in range(NG):
            for hf in range(2):
                conv_jobs.append((wos, wos.a[g * 2 + hf].rearrange("p (k c) -> p k c", k=KM), wo_in[g][:, :, hf * 256:(hf + 1) * 256]))
        for fb in range(FB):
            conv_jobs.append((wgs, wgs.a[fb], wg_in[fb].rearrange("p u k c -> p (u k c)")))
        for hf in range(2):
            for g in range(NG):
                for sl in range(NSL):
                    f0 = sl * WSLc
                    nf = min(WSLc, FH - f0)
                    r0 = (hf * FH + f0) * 128
                    si = (hf * NG + g) * NSL + sl
                    conv_jobs.append((wds, wds.a[si][:, 0:nf * 512].rearrange("p (f c) -> p f c", c=512),
                                      wd_in[r0:r0 + nf * 128, g * 512:(g + 1) * 512].rearrange("(f p) c -> p f c", p=128)))
        conv_pos = [0]

        def emit_conv(n):
            for _ in range(n):
                if conv_pos[0] < len(conv_jobs):
                    d, dap, sap = conv_jobs[conv_pos[0]]
                    k.dma("pool", d, dap, None, sap)
                    conv_pos[0] += 1

        def cload(name, shape, src_ap, dt=F32, q="sp"):
            t = k.sb(name, shape, dt)
            k.dma(q, t, t.a, None, src_ap, semtl=cst[q])
            return t

        ident = cload("ident", [128, 128], masks_in[0])
        triU = cload("triU", [128, 128], masks_in[1])
        negU = cload("negU", [128, 128], masks_in[2])
        posL = cload("posL", [128, 128], masks_in[3])
        n1c = cload("n1c", [128, KD], n1c_in)
        n2c = cload("n2c", [128, KD], n2c_in)
        cw = cload("cw", [128, 20, 4], cw_in)
        cbias = cload("cbias", [128, 20], cb_in)
        dtb = cload("dtb", [128, 12], dtb_in)
        alog = cload("alog", [128, 12], alog_in)
        dcol = cload("dcol", [128, 4], dcol_in)
        ssmnw = cload("ssmnw", [128, 4], ssmnw_in)
        gdnnw = cload("gdnnw", [128, 1], gdnnw_in)
        fcw = cload("fcw", [128, FB, 3], fcw_in)
        fcb = cload("fcb", [128, FB], fcb_in)
        idx = cload("idx", [128, (NT3 + 1) * KM], idx_in, dt=I32, q="pool")
        wsm = cload("wsm", [128, KD, 16], wsm_in, dt=BF16, q="pool")
        identb = k.sb("identb", [128, 128], BF16)
        ones = k.sb("ones", [128, 128], F32)
        epsc = k.sb("epsc", [128, 1], F32)
        eps128 = k.sb("eps128", [128, 1], F32)
        onec = k.sb("onec", [128, 1], F32)
        nA = k.sb("nA", [128, 12], F32)
        k.V([ident], [identb], lambda e: e.tensor_copy(out=identb.a, in_=ident.a))
        k.V([], [ones], lambda e: e.memset(ones.a, 1.0))
        k.V([], [epsc], lambda e: e.memset(epsc.a, EPS))
        k.V([], [eps128], lambda e: e.memset(eps128.a, EPS * 128.0))
        k.V([], [onec], lambda e: e.memset(onec.a, 1.0))
        k.A([alog], [nA], lambda e: e.activation(out=nA.a, in_=alog.a, func=AF.Exp))
        k.V([nA], [nA], lambda e: e.tensor_scalar(out=nA.a, in0=nA.a, scalar1=-1.0, scalar2=None, op0=ALU.mult))

        pbig_l = [k.ps("pb%d" % i, [128, 512], F32) for i in range(2)]
        pbig = RR(pbig_l)
        ptr = RR([k.ps("ptr%d" % i, [128, 1024], BF16) for i in range(2)])
        pqb = [k.ps("pqb%d" % i, [128, 512], F32) for i in range(4)]
        pq = RR([TlView(pqb[i], pqb[i].a[:, qd * 128:(qd + 1) * 128]) for qd in range(4) for i in range(4)])
        pq3 = RR([TlView(pqb[i], pqb[i].a[:, qd * 128:(qd + 1) * 128]) for qd in range(4) for i in range(2)])
        pacc = [pqb[2], pqb[3]]

        xrow = k.sb("xrow", [128, D], F32)
        hnT = k.sb("hnT", [128, KD, TT + 2], BF16)
        wblk = RR([k.sb("wblk%d" % i, [128, WBE], BF16) for i in range(2)])
        ssq = RR([k.sb("ssq%d" % i, [128, 1], F32) for i in range(4)])
        rstd = RR([k.sb("rstd%d" % i, [128, 1], F32) for i in range(4)])
        big = k.sb("big", [128, 22528], BF16)
        t512 = RR([k.sb("t512_%d" % i, [128, 516], F32) for i in range(5)])

        def norm_transpose(xr, hb, hba, col0, nwc, tgt, ncols=128, srccol=0):
            sq = ssq.get()
            rs = rstd.get()
            k.V([], [sq], lambda e: e.memset(sq.a, 0.0))
            k.A([xr, sq], [hb, sq], lambda e: e.activation(out=hba, in_=xr.a, func=AF.Square, accum_out=sq.a))
            k.A([sq, epsc], [rs], lambda e: e.activation(out=rs.a, in_=sq.a, func=AF.Sqrt, bias=epsc.a[:, 0:1], scale=1.0 / D))
            k.V([rs], [rs], lambda e: e.reciprocal(out=rs.a, in_=rs.a))
            k.A([xr, rs], [hb], lambda e: e.activation(out=hba, in_=xr.a, func=AF.Copy, scale=rs.a[:, 0:1]))
            for kg in range(KD // 4):
                pt = ptr.get()
                for qd in range(4):
                    kd = kg * 4 + qd
                    k.P([hb, identb], [pt], lambda e: e.transpose(pt.a[:, qd * 128:(qd + 1) * 128], hba[:, kd * 128:(kd + 1) * 128], identb.a))
                src = pt.a[:, 0:512].rearrange("p (q c) -> p q c", q=4)[:, :, srccol:srccol + ncols]
                dstv = tgt.a[:, kg * 4:kg * 4 + 4, col0:col0 + ncols]
                sc = nwc.a[:, kg * 4:kg * 4 + 4].unsqueeze(2).to_broadcast([128, 4, ncols])
                k.V([pt, nwc], [tgt], lambda e: e.tensor_tensor(out=dstv, in0=src, in1=sc, op=ALU.mult))

        def conv_silu(pb, ci, dst_tl, dst_ap, K, hal, cwt, cbt):
            pre = t512.get()
            acc = t512.get()
            H = K - 1
            W = TT
            k.A([hal], [pre], lambda e: e.activation(out=pre.a[:, 0:H], in_=hal.a[:, ci, 0:H], func=AF.Copy))
            k.A([pb], [pre], lambda e: e.activation(out=pre.a[:, H:H + W], in_=pb.a[:, 0:W], func=AF.Copy))
            k.V([pre, cwt, cbt], [acc], lambda e: e.tensor_scalar(out=acc.a[:, 0:W], in0=pre.a[:, H:H + W], scalar1=cwt.a[:, ci, K - 1:K], scalar2=cbt.a[:, ci:ci + 1], op0=ALU.mult, op1=ALU.add))
            for j in range(K - 1):
                k.V([pre, cwt, acc], [acc], lambda e: e.scalar_tensor_tensor(out=acc.a[:, 0:W], in0=pre.a[:, j:j + W], scalar=cwt.a[:, ci, j:j + 1], in1=acc.a[:, 0:W], op0=ALU.mult, op1=ALU.add))
            k.A([pre], [hal], lambda e: e.activation(out=hal.a[:, ci, 0:H], in_=pre.a[:, W:W + H], func=AF.Copy))
            k.A([acc], [dst_tl], lambda e: e.activation(out=dst_ap, in_=acc.a[:, 0:W], func=AF.Silu))

        def mm(out_tl, out_ap, l_tl, l_ap, r_tl, r_ap, start=True, stop=True):
            k.P([l_tl, r_tl], [out_tl], lambda e: e.matmul(out_ap, lhsT=l_ap, rhs=r_ap, start=start, stop=stop))

        es1 = contextlib.ExitStack()
        with es1:
            k.es = es1
            sv = big.a.bitcast(F32)

            def slot(i):
                return sv[:, i * TT:(i + 1) * TT]
            sstore = k.sb("sstore", [128, 2, TT], F32)
            zst = k.sb("zst", [128, 4, TT], BF16)
            halo = k.sb("halo", [128, 20, 3], F32)
            k.V([], [halo], lambda e: e.memset(halo.a, 0.0))
            o_g = k.sb("o_g", [128, 4, TT], BF16)
            o_s = k.sb("o_s", [128, 4, TT], BF16)
            yT = k.sb("yT", [128, 8, TT], BF16)
            hb1_ap = yT.a.rearrange("p b c -> p (b c)")[:, 0:D]
            S_g = [k.sb("S_g%d" % h, [128, 128], F32) for h in range(4)]
            S_s = [k.sb("S_s%d" % h, [128, 64], F32) for h in range(8)]
            for s_ in S_g + S_s:
                k.V([], [s_], lambda e: e.memset(s_.a, 0.0))
            mhp = [RR([k.sb("mh%d_%d" % (h, i), [128, 128], F32) for i in range(8)]) for h in range(4)]
            mlp = [RR([k.sb("ml%d_%d" % (h, i), [128, 128], F32) for i in range(5)]) for h in range(4)]
            mcb = [k.sb("mcb%d" % i, [128, 128], F32) for i in range(2)]
            sm12 = RR([k.sb("sm%d" % i, [128, 16], F32) for i in range(24)])
            ktok = [k.sb("ktok%d" % h, [128, 128], F32) for h in range(4)]
            vtok = [k.sb("vtok%d" % h, [128, 128], F32) for h in range(4)]
            btok = [k.sb("btok%d" % g, [128, 128], F32) for g in range(2)]
            xtok = k.sb("xtok", [128, 512], F32)

            def proj_block(cb, pb):
                wb = wblk.get()
                wv = wb.a[:, 0:KD * 128].rearrange("p (k c) -> p k c", k=KD)
                if t == 0:
                    k.dma("pool", wb, wv, None, w1_in[cb])
                    k.dma("sp", w1s, w1s.a[cb], wb, wb.a[:, 0:KD * 128])
                else:
                    k.dma("sp", wb, wb.a[:, 0:KD * 128], w1s, w1s.a[cb])
                for kd in range(KD):
                    k.P([wb, hnT], [pb], lambda e: e.matmul(pb.a, lhsT=wv[:, kd, :], rhs=hnT.a[:, kd, 0:TT], start=(kd == 0), stop=(kd == KD - 1)))
                return pb

            def l2norm(src_tl, src_ap, dst_tl, dst_ap, scale, bias_tl, pb):
                sq = t512.get()
                k.A([src_tl], [sq], lambda e: e.activation(out=sq.a[:, 0:TT], in_=src_ap, func=AF.Square))
                mm(pb, pb.a, ones, ones.a, sq, sq.a[:, 0:TT])
                rt = t512.get()
                k.A([pb, bias_tl], [rt], lambda e: e.activation(out=rt.a[:, 0:TT], in_=pb.a, func=AF.Sqrt, bias=bias_tl.a[:, 0:1], scale=scale))
                k.V([rt], [rt], lambda e: e.reciprocal(out=rt.a[:, 0:TT], in_=rt.a[:, 0:TT]))
                k.V([src_tl, rt], [dst_tl], lambda e: e.tensor_tensor(out=dst_ap, in0=src_ap, in1=rt.a[:, 0:TT], op=ALU.mult))

            def transpose_f(dst_tl, dst_ap, src_tl, src_ap):
                p = pq.get()
                k.P([src_tl, ident], [p], lambda e: e.transpose(p.a, src_ap, ident.a))
                k.A([p], [dst_tl], lambda e: e.activation(out=dst_ap, in_=p.a, func=AF.Copy))

            for t in range(NT):
                for s in range(4):
                    k.dma("sp", xrow, xrow.a, None, x_in[t * TT + s * 128: t * TT + (s + 1) * 128, :])
                    norm_transpose(xrow, yT, hb1_ap, s * 128, n1c, hnT)
                jobs = []
                for hh in range(4):
                    for r in range(4):
                        jobs.append(("g", hh, r))
                for sbk in range(12):
                    jobs.append(("s", sbk, 0))

                def post(job, pb, ji):
                    kind, a_, r = job
                    if kind == "g":
                        hh = a_
                        if r == 3:
                            k.A([pb], [big], lambda e: e.activation(out=slot(hh * 4 + 3), in_=pb.a, func=AF.Silu))
                        elif r == 2:
                            conv_silu(pb, hh * 3 + r, big, slot(hh * 4 + 2), 4, halo, cw, cbias)
                        else:
                            tmp = t512.get()
                            conv_silu(pb, hh * 3 + r, tmp, tmp.a[:, 0:TT], 4, halo, cw, cbias)
                            if r == 0:
                                l2norm(tmp, tmp.a[:, 0:TT], big, slot(hh * 4 + 0), 128.0, eps128, pqb[ji % 2])
                            else:
                                l2norm(tmp, tmp.a[:, 0:TT], big, slot(hh * 4 + 1), 1.0, epsc, pqb[ji % 2])
                    else:
                        sbk = a_
                        if sbk < 4:
                            k.A([pb], [zst], lambda e: e.activation(out=zst.a[:, sbk, :], in_=pb.a, func=AF.Silu))
                        elif sbk < 8:
                            conv_silu(pb, 12 + (sbk - 4), big, slot(16 + sbk - 4), 4, halo, cw, cbias)
                        elif sbk < 10:
                            conv_silu(pb, 12 + (sbk - 4), big, slot(20 + sbk - 8), 4, halo, cw, cbias)
                        else:
                            conv_silu(pb, 12 + (sbk - 4), sstore, sstore.a[:, sbk - 10, :], 4, halo, cw, cbias)

                prev = None
                for ji, job in enumerate(jobs):
                    cb = (job[1] * 4 + job[2]) if job[0] == "g" else 16 + job[1]
                    pb = proj_block(cb, pbig_l[ji % 2])
                    if prev is not None:
                        post(*prev)
                    prev = (job, pb, ji)
                post(*prev)
                if t >= 1:
                    emit_conv((len(conv_jobs) + NT - 2) // max(NT - 1, 1))
                for c4 in range(4):
                    cs = slice(c4 * 128, (c4 + 1) * 128)
                    psm = pq.get()
                    for kd in range(KD):
                        k.P([hnT, wsm], [psm], lambda e: e.matmul(psm.a[:, 0:16], lhsT=hnT.a[:, kd, cs], rhs=wsm.a[:, kd, :], start=(kd == 0), stop=(kd == KD - 1)))
                    sm = sm12.get()
                    k.A([psm], [sm], lambda e: e.activation(out=sm.a, in_=psm.a[:, 0:16], func=AF.Copy))
                    xs_ = sm12.get(); ax = sm12.get(); ex = sm12.get(); sp = sm12.get(); G = sm12.get(); beta = sm12.get()
                    k.V([sm, dtb], [xs_], lambda e: e.tensor_tensor(out=xs_.a[:, 0:12], in0=sm.a[:, 4:16], in1=dtb.a, op=ALU.add))
                    k.A([xs_], [ax], lambda e: e.activation(out=ax.a[:, 0:12], in_=xs_.a[:, 0:12], func=AF.Abs))
                    k.A([ax], [ex], lambda e: e.activation(out=ex.a[:, 0:12], in_=ax.a[:, 0:12], func=AF.Exp, scale=-1.0))
                    k.A([ex, onec], [ex], lambda e: e.activation(out=ex.a[:, 0:12], in_=ex.a[:, 0:12], func=AF.Ln, bias=onec.a[:, 0:1], scale=1.0))
                    k.V([xs_, ex], [sp], lambda e: e.scalar_tensor_tensor(out=sp.a[:, 0:12], in0=xs_.a[:, 0:12], scalar=0.0, in1=ex.a[:, 0:12], op0=ALU.max, op1=ALU.add))
                    k.V([sp, nA], [G], lambda e: e.tensor_tensor(out=G.a[:, 0:12], in0=sp.a[:, 0:12], in1=nA.a, op=ALU.mult))
                    k.A([sm], [beta], lambda e: e.activation(out=beta.a[:, 0:4], in_=sm.a[:, 0:4], func=AF.Sigmoid))
                    pc = pq.get()
                    mm(pc, pc.a[:, 0:12], triU, triU.a, G, G.a[:, 0:12])
                    mm(pc, pc.a[:, 16:28], ones, ones.a, G, G.a[:, 0:12])
                    gc = sm12.get(); gl = sm12.get(); eg = sm12.get(); kdc = sm12.get(); dbc = sm12.get(); bg = sm12.get(); dif = sm12.get()
                    k.A([pc], [gc], lambda e: e.activation(out=gc.a[:, 0:12], in_=pc.a[:, 0:12], func=AF.Copy))
                    k.A([pc], [gl], lambda e: e.activation(out=gl.a[:, 0:12], in_=pc.a[:, 16:28], func=AF.Copy))
                    k.A([gc], [eg], lambda e: e.activation(out=eg.a[:, 0:12], in_=gc.a[:, 0:12], func=AF.Exp))
                    k.V([gl, gc], [dif], lambda e: e.tensor_tensor(out=dif.a[:, 0:12], in0=gl.a[:, 0:12], in1=gc.a[:, 0:12], op=ALU.subtract))
                    k.A([dif], [kdc], lambda e: e.activation(out=kdc.a[:, 0:12], in_=dif.a[:, 0:12], func=AF.Exp))
                    k.A([gl], [dbc], lambda e: e.activation(out=dbc.a[:, 0:12], in_=gl.a[:, 0:12], func=AF.Exp))
                    k.V([beta, eg], [bg], lambda e: e.tensor_tensor(out=bg.a[:, 0:4], in0=beta.a[:, 0:4], in1=eg.a[:, 0:4], op=ALU.mult))
                    DMP = (t == 0 and c4 < 2)
                    if DMP:
                        tg = "_c%d" % c4
                        dump("sm" + tg, sm, sm.a, [128, 16]); dump("sp" + tg, sp, sp.a[:, 0:12], [128, 12]); dump("G" + tg, G, G.a[:, 0:12], [128, 12])
                        dump("gc" + tg, gc, gc.a[:, 0:12], [128, 12]); dump("gl" + tg, gl, gl.a[:, 0:12], [128, 12]); dump("beta" + tg, beta, beta.a[:, 0:4], [128, 4])
                        dump("kdc" + tg, kdc, kdc.a[:, 0:12], [128, 12]); dump("dbc" + tg, dbc, dbc.a[:, 0:12], [128, 12])
                        if c4 == 0:
                            for i_ in range(22):
                                dump("slot%d" % i_, big, slot(i_), [128, TT])
                            dump("sstore", sstore, sstore.a, [128, 2, TT])

                    def gdn_head(hh):
                        mh, ml = mhp[hh], mlp[hh]
                        qT = slot(hh * 4 + 0)[:, cs]
                        kT = slot(hh * 4 + 1)[:, cs]
                        vT = slot(hh * 4 + 2)[:, cs]
                        p1 = pq.get()
                        k.P([big, ident], [p1], lambda e: e.transpose(p1.a, kT, ident.a))
                        yield
                        k.A([p1], [ktok[hh]], lambda e: e.activation(out=ktok[hh].a, in_=p1.a, func=AF.Copy))
                        p2 = pq.get()
                        k.P([big, ident], [p2], lambda e: e.transpose(p2.a, vT, ident.a))
                        yield
                        k.A([p2], [vtok[hh]], lambda e: e.activation(out=vtok[hh].a, in_=p2.a, func=AF.Copy))
                        gb = mh.get()
                        k.V([ones, G], [gb], lambda e: e.tensor_scalar(out=gb.a, in0=ones.a, scalar1=G.a[:, hh:hh + 1], scalar2=None, op0=ALU.mult))
                        prow = pq.get()
                        mm(prow, prow.a, gb, gb.a, triU, triU.a)
                        yield
                        X = mh.get(); GT = mh.get(); XL = mh.get(); GL = mh.get(); Eg = mh.get()
                        k.V([prow, gc, negU], [X], lambda e: e.scalar_tensor_tensor(out=X.a, in0=prow.a, scalar=gc.a[:, hh:hh + 1], in1=negU.a, op0=ALU.subtract, op1=ALU.min))
                        k.A([X], [GT], lambda e: e.activation(out=GT.a, in_=X.a, func=AF.Exp))
                        k.V([prow, gc, posL], [XL], lambda e: e.scalar_tensor_tensor(out=XL.a, in0=prow.a, scalar=gc.a[:, hh:hh + 1], in1=posL.a, op0=ALU.subtract, op1=ALU.max))
                        k.A([XL], [GL], lambda e: e.activation(out=GL.a, in_=XL.a, func=AF.Exp, scale=-1.0))
                        k.A([prow], [Eg], lambda e: e.activation(out=Eg.a, in_=prow.a, func=AF.Exp))
                        pqk = pq.get()
                        mm(pqk, pqk.a, big, kT, big, qT)
                        yield
                        attnT = ml.get()
                        k.V([pqk, GT], [attnT], lambda e: e.tensor_tensor(out=attnT.a, in0=pqk.a, in1=GT.a, op=ALU.mult))
                        vb = ml.get(); kbg = ml.get(); kdec = ml.get(); qdec = ml.get()
                        k.V([vtok[hh], beta], [vb], lambda e: e.tensor_scalar(out=vb.a, in0=vtok[hh].a, scalar1=beta.a[:, hh:hh + 1], scalar2=None, op0=ALU.mult))
                        k.V([ktok[hh], bg], [kbg], lambda e: e.tensor_scalar(out=kbg.a, in0=ktok[hh].a, scalar1=bg.a[:, hh:hh + 1], scalar2=None, op0=ALU.mult))
                        k.V([ktok[hh], kdc], [kdec], lambda e: e.tensor_scalar(out=kdec.a, in0=ktok[hh].a, scalar1=kdc.a[:, hh:hh + 1], scalar2=None, op0=ALU.mult))
                        k.V([big, Eg], [qdec], lambda e: e.tensor_tensor(out=qdec.a, in0=qT, in1=Eg.a, op=ALU.mult))
                        pkk = pq.get()
                        mm(pkk, pkk.a, big, kT, big, kT)
                        yield
                        Am = mh.get()
                        k.V([pkk, beta, GL], [Am], lambda e: e.scalar_tensor_tensor(out=Am.a, in0=pkk.a, scalar=beta.a[:, hh:hh + 1], in1=GL.a, op0=ALU.mult, op1=ALU.mult))
                        pN = pq.get()
                        k.P([Am, ident], [pN], lambda e: e.transpose(pN.a, Am.a, ident.a))
                        yield
                        Nm = mh.get()
                        k.A([pN], [Nm], lambda e: e.activation(out=Nm.a, in_=pN.a, func=AF.Copy))
                        R = mh.get()
                        k.V([ident, Nm], [R], lambda e: e.tensor_tensor(out=R.a, in0=ident.a, in1=Nm.a, op=ALU.subtract))
                        Pm, PTm = Nm, Am
                        for lvl in range(1, 7):
                            pPT = pq.get()
                            mm(pPT, pPT.a, Pm, Pm.a, PTm, PTm.a)
                            yield
                            nPT = mh.get()
                            k.A([pPT], [nPT], lambda e: e.activation(out=nPT.a, in_=pPT.a, func=AF.Copy))
                            if lvl < 6:
                                pP = pq.get()
                                mm(pP, pP.a, PTm, PTm.a, Pm, Pm.a)
                                yield
                                nP = mh.get()
                                k.V([pP], [nP], lambda e: e.tensor_copy(out=nP.a, in_=pP.a))
                            pR = pq.get()
                            mm(pR, pR.a, nPT, nPT.a, R, R.a)
                            yield
                            nR = mh.get()
                            k.V([pR, R], [nR], lambda e: e.tensor_tensor(out=nR.a, in0=pR.a, in1=R.a, op=ALU.add))
                            R = nR
                            PTm = nPT
                            if lvl < 6:
                                Pm = nP
                        pw = pq.get()
                        mm(pw, pw.a, kbg, kbg.a, R, R.a)
                        yield
                        wTn = mh.get()
                        k.A([pw], [wTn], lambda e: e.activation(out=wTn.a, in_=pw.a, func=AF.Copy, scale=-1.0))
                        pv = pq.get()
                        mm(pv, pv.a, R, R.a, vb, vb.a, start=True, stop=False)
                        mm(pv, pv.a, wTn, wTn.a, S_g[hh], S_g[hh].a, start=False, stop=True)
                        yield
                        vnew = mh.get()
                        k.V([pv], [vnew], lambda e: e.tensor_copy(out=vnew.a, in_=pv.a))
                        po = pq.get()
                        mm(po, po.a, S_g[hh], S_g[hh].a, qdec, qdec.a, start=True, stop=False)
                        mm(po, po.a, vnew, vnew.a, attnT, attnT.a, start=False, stop=True)
                        yield
                        k.A([po], [o_g], lambda e: e.activation(out=o_g.a[:, hh, cs], in_=po.a, func=AF.Copy))
                        pS = pq.get()
                        mm(pS, pS.a, kdec, kdec.a, vnew, vnew.a)
                        yield
                        k.V([S_g[hh], dbc, pS], [S_g[hh]], lambda e: e.scalar_tensor_tensor(out=S_g[hh].a, in0=S_g[hh].a, scalar=dbc.a[:, hh:hh + 1], in1=pS.a, op0=ALU.mult, op1=ALU.add))

                    lockstep([gdn_head(hh) for hh in range(4)])

                    for g2 in range(2):
                        transpose_f(btok[g2], btok[g2].a, big, slot(20 + g2)[:, cs])
                    for blk in range(4):
                        transpose_f(xtok, xtok.a[:, blk * 128:(blk + 1) * 128], big, slot(16 + blk)[:, cs])
                    for g2 in range(2):
                        p = pq.get()
                        mm(p, p.a, big, slot(20 + g2)[:, cs], sstore, sstore.a[:, g2, cs])
                        k.A([p], [mcb[g2]], lambda e: e.activation(out=mcb[g2].a, in_=p.a, func=AF.Copy))

                    def ssd_head(h, pos_):
                        g2 = h // 4
                        col = 4 + h
                        mh = mhp[h % 4]
                        gb = mh.get()
                        k.V([ones, G], [gb], lambda e: e.tensor_scalar(out=gb.a, in0=ones.a, scalar1=G.a[:, col:col + 1], scalar2=None, op0=ALU.mult))
                        prow = pq.get()
                        mm(prow, prow.a, gb, gb.a, triU, triU.a)
                        yield
                        X = mh.get(); GT = mh.get(); Eg = mh.get()
                        k.V([prow, gc, negU], [X], lambda e: e.scalar_tensor_tensor(out=X.a, in0=prow.a, scalar=gc.a[:, col:col + 1], in1=negU.a, op0=ALU.subtract, op1=ALU.min))
                        k.A([X], [GT], lambda e: e.activation(out=GT.a, in_=X.a, func=AF.Exp))
                        k.A([prow], [Eg], lambda e: e.activation(out=Eg.a, in_=prow.a, func=AF.Exp))
                        attnT = mh.get(); qdec = mh.get(); kdec = mh.get(); vh = mh.get()
                        k.V([mcb[g2], GT], [attnT], lambda e: e.tensor_tensor(out=attnT.a, in0=mcb[g2].a, in1=GT.a, op=ALU.mult))
                        k.V([sstore, Eg], [qdec], lambda e: e.tensor_tensor(out=qdec.a, in0=sstore.a[:, g2, cs], in1=Eg.a, op=ALU.mult))
                        k.V([btok[g2], kdc], [kdec], lambda e: e.tensor_scalar(out=kdec.a, in0=btok[g2].a, scalar1=kdc.a[:, col:col + 1], scalar2=None, op0=ALU.mult))
                        k.V([xtok, sp], [vh], lambda e: e.tensor_scalar(out=vh.a[:, 0:64], in0=xtok.a[:, h * 64:(h + 1) * 64], scalar1=sp.a[:, col:col + 1], scalar2=None, op0=ALU.mult))
                        half = slice((h % 2) * 64, (h % 2) * 64 + 64)
                        mm(pos_, pos_.a[half, :], S_s[h], S_s[h].a, qdec, qdec.a, start=True, stop=False)
                        mm(pos_, pos_.a[half, :], vh, vh.a[:, 0:64], attnT, attnT.a, start=False, stop=True)
                        pS = pq.get()
                        mm(pS, pS.a[:, 0:64], kdec, kdec.a, vh, vh.a[:, 0:64])
                        yield
                        k.V([S_s[h], dbc, pS], [S_s[h]], lambda e: e.scalar_tensor_tensor(out=S_s[h].a, in0=S_s[h].a, scalar=dbc.a[:, col:col + 1], in1=pS.a[:, 0:64], op0=ALU.mult, op1=ALU.add))

                    for g2 in range(2):
                        pp = [pq.get(), pq.get()]
                        lockstep([ssd_head(g2 * 4 + i, pp[i // 2]) for i in range(4)])
                        for i2 in range(2):
                            k.A([pp[i2]], [o_s], lambda e: e.activation(out=o_s.a[:, g2 * 2 + i2, cs], in_=pp[i2].a, func=AF.Copy))

                for hh in range(4):
                    sq = t512.get()
                    k.A([o_g], [sq], lambda e: e.activation(out=sq.a[:, 0:TT], in_=o_g.a[:, hh, :], func=AF.Square))
                    pb = pbig.get()
                    mm(pb, pb.a, ones, ones.a, sq, sq.a[:, 0:TT])
                    rt = t512.get()
                    k.A([pb, epsc], [rt], lambda e: e.activation(out=rt.a[:, 0:TT], in_=pb.a, func=AF.Sqrt, bias=epsc.a[:, 0:1], scale=1.0 / 128))
                    k.V([rt], [rt], lambda e: e.reciprocal(out=rt.a[:, 0:TT], in_=rt.a[:, 0:TT]))
                    k.V([o_g, rt], [rt], lambda e: e.tensor_tensor(out=rt.a[:, 0:TT], in0=o_g.a[:, hh, :], in1=rt.a[:, 0:TT], op=ALU.mult))
                    k.V([rt, gdnnw, big], [yT], lambda e: e.scalar_tensor_tensor(out=yT.a[:, hh, :], in0=rt.a[:, 0:TT], scalar=gdnnw.a[:, 0:1], in1=slot(hh * 4 + 3), op0=ALU.mult, op1=ALU.mult))
                for g2 in range(2):
                    ys = []
                    pb = pbig.get()
                    for bi in range(2):
                        blk = g2 * 2 + bi
                        y0 = t512.get()
                        k.V([big, dcol, o_s], [y0], lambda e: e.scalar_tensor_tensor(out=y0.a[:, 0:TT], in0=slot(16 + blk), scalar=dcol.a[:, blk:blk + 1], in1=o_s.a[:, blk, :], op0=ALU.mult, op1=ALU.add))
                        k.V([y0, zst], [y0], lambda e: e.tensor_tensor(out=y0.a[:, 0:TT], in0=y0.a[:, 0:TT], in1=zst.a[:, blk, :], op=ALU.mult))
                        sq = t512.get()
                        k.A([y0], [sq], lambda e: e.activation(out=sq.a[:, 0:TT], in_=y0.a[:, 0:TT], func=AF.Square))
                        mm(pb, pb.a, ones, ones.a, sq, sq.a[:, 0:TT], start=(bi == 0), stop=(bi == 1))
                        ys.append(y0)
                    rt = t512.get()
                    k.A([pb, epsc], [rt], lambda e: e.activation(out=rt.a[:, 0:TT], in_=pb.a, func=AF.Sqrt, bias=epsc.a[:, 0:1], scale=1.0 / 256))
                    k.V([rt], [rt], lambda e: e.reciprocal(out=rt.a[:, 0:TT], in_=rt.a[:, 0:TT]))
                    for bi in range(2):
                        blk = g2 * 2 + bi
                        y0 = ys[bi]
                        k.V([y0, ssmnw, rt], [yT], lambda e: e.scalar_tensor_tensor(out=yT.a[:, 4 + blk, :], in0=y0.a[:, 0:TT], scalar=ssmnw.a[:, blk:blk + 1], in1=rt.a[:, 0:TT], op0=ALU.mult, op1=ALU.mult))
                if t == 0:
                    dump("o_g", o_g, o_g.a, [128, 4, TT]); dump("o_s", o_s, o_s.a, [128, 4, TT])
                k.dma("sp", ybuf, ybuf.a[t].rearrange("(b p) c -> p b c", p=128), yT, yT.a)
                if dbg:
                    k.dma("sp", ydbg, ydbg.a[t].rearrange("(b p) c -> p b c", p=128), yT, yT.a)
                k.collective(ybuf, gath, ybuf.a[t], gath.a[t * 4096:(t + 1) * 4096, :], cfg["groups"], gath)
            emit_conv(len(conv_jobs))
            zt = t512.get()
            k.V([], [zt], lambda e: e.memset(zt.a, 0.0))
            zb16 = zt.a.bitcast(BF16)[:, 0:TT]
            for i in range(32):
                k.dma("sp", gath, gath.a[NT * 4096 + i * 128: NT * 4096 + (i + 1) * 128, :], zt, zb16, semtl=zt)
            k.barrier()
        k.es = es

        es3 = contextlib.ExitStack()
        if cfg.get("stop_after", 3) == 1:
            k.finish([ydbg, gath] + dumped)
            k.barrier()
            return nc
        with es3:
            k.es = es3
            rowb = k.sb("rowb", [128, D], F32)
            hb3 = k.sb("hb3", [128, D], BF16)
            halo2 = k.sb("halo2", [128, FB, 2], F32)
            piece = RR([k.sb("piece%d" % i, [128, 512], F32) for i in range(4)])
            ssq3 = k.sb("ssqf", [128, 4, NG], F32)
            actT = big.a.rearrange("p (f c) -> p f c", c=TT)
            y_sb = big.a[:, 0:KM * 640].rearrange("p (k c) -> p k c", k=KM)
            accs = [pbig_l[0], pbig_l[1], pacc[0], pacc[1]]
            pbB = RR([pbig_l[0], pbig_l[1], pacc[0], pacc[1]])
            WSL = WBE // 512
            for m in range(NT3):
                g0 = 0 if m == 0 else 1
                for kd in range(KM):
                    if m == 0:
                        k.dma("pool", big, y_sb[:, kd, 0:128], gath, gath.a.rearrange("r (q c) -> (r q) c", c=128), kind="ind",
                              in_off=bass.IndirectOffsetOnAxis(ap=idx.a[:, kd:kd + 1], axis=0))
                    k.dma("pool", big, y_sb[:, kd, 128:640], gath, gath.a[:, :], kind="ind",
                          in_off=bass.IndirectOffsetOnAxis(ap=idx.a[:, (m + 1) * KM + kd:(m + 1) * KM + kd + 1], axis=0))
                for g in range(NG):
                    for hf in range(2):
                        wb = wblk.get()
                        wv = wb.a.rearrange("p (k c) -> p k c", k=KM)
                        k.dma("sp", wb, wb.a[:, 0:KM * 256], wos, wos.a[g * 2 + hf])
                        for gi in range(g0, 5):
                            pb = pbig.get()
                            for kd in range(KM):
                                k.P([big, wb], [pb], lambda e: e.matmul(pb.a[:, 0:256], lhsT=y_sb[:, kd, gi * 128:(gi + 1) * 128], rhs=wv[:, kd, :], start=(kd == 0), stop=(kd == KM - 1)))
                            xp = piece.get()
                            r0 = (0 if gi == 0 else 128 + m * TT + (gi - 1) * 128)
                            c0 = g * 512 + hf * 256
                            k.dma("sp", xp, xp.a[:, 0:256], None, xq_in[r0:r0 + 128, c0:c0 + 256])
                            k.V([pb, xp], [xp], lambda e: e.tensor_tensor(out=xp.a[:, 0:256], in0=pb.a[:, 0:256], in1=xp.a[:, 0:256], op=ALU.add))
                            k.dma("sp", hbuf, hbuf.a[r0:r0 + 128, c0:c0 + 256], xp, xp.a[:, 0:256])
                            if dbg:
                                k.dma("sp", hdbg, hdbg.a[r0:r0 + 128, c0:c0 + 256], xp, xp.a[:, 0:256])
                for gi in range(g0, 5):
                    r0 = (0 if gi == 0 else 128 + m * TT + (gi - 1) * 128)
                    k.dma("sp", xrow, xrow.a, hbuf, hbuf.a[r0:r0 + 128, :])
                    if gi == 0:
                        norm_transpose(xrow, hb3, hb3.a, 0, n2c, hnT, ncols=2, srccol=126)
                    else:
                        norm_transpose(xrow, hb3, hb3.a, 2 + (gi - 1) * 128, n2c, hnT)
                k.V([], [ssq3], lambda e: e.memset(ssq3.a, 0.0))
                for hf in range(2):
                    for fl in range(FH):
                        fb = hf * FH + fl
                        wb = wblk.get()
                        wv = wb.a[:, 0:2 * KD * 128].rearrange("p (u k c) -> p u k c", u=2, k=KD)
                        k.dma("sp", wb, wb.a[:, 0:2 * KD * 128], wgs, wgs.a[fb])
                        pg = pbB.get()
                        for kd in range(KD):
                            k.P([wb, hnT], [pg], lambda e: e.matmul(pg.a, lhsT=wv[:, 0, kd, :], rhs=hnT.a[:, kd, 2:2 + TT], start=(kd == 0), stop=(kd == KD - 1)))
                        if m == 0:
                            ph = pq3.get()
                            for kd in range(KD):
                                k.P([wb, hnT], [ph], lambda e: e.matmul(ph.a[:, 0:2], lhsT=wv[:, 0, kd, :], rhs=hnT.a[:, kd, 0:2], start=(kd == 0), stop=(kd == KD - 1)))
                            k.A([ph], [halo2], lambda e: e.activation(out=halo2.a[:, fb, :], in_=ph.a[:, 0:2], func=AF.Copy))
                        pu = pbB.get()
                        for kd in range(KD):
                            k.P([wb, hnT], [pu], lambda e: e.matmul(pu.a, lhsT=wv[:, 1, kd, :], rhs=hnT.a[:, kd, 2:2 + TT], start=(kd == 0), stop=(kd == KD - 1)))
                        sg = t512.get()
                        conv_silu(pg, fb, sg, sg.a[:, 0:TT], 3, halo2, fcw, fcb)
                        k.V([sg, pu], [big], lambda e: e.tensor_tensor(out=actT[:, fl, :], in0=sg.a[:, 0:TT], in1=pu.a, op=ALU.mult))
                    for g in range(NG):
                        nsl = (FH + WSL - 1) // WSL
                        for sl in range(nsl):
                            f0 = sl * WSL
                            nf = min(WSL, FH - f0)
                            wb = wblk.get()
                            wv = wb.a.rearrange("p (f c) -> p f c", c=512)
                            r0 = (hf * FH + f0) * 128
                            si = (hf * NG + g) * NSL + sl
                            k.dma("sp", wb, wb.a[:, 0:nf * 512], wds, wds.a[si][:, 0:nf * 512])
                            for i in range(4):
                                for f in range(nf):
                                    k.P([big, wb], [accs[i]], lambda e: e.matmul(accs[i].a, lhsT=actT[:, f0 + f, i * 128:(i + 1) * 128], rhs=wv[:, f, :], start=(f0 + f == 0), stop=(f0 + f == FH - 1)))
                        for i in range(4):
                            r0 = m * TT + i * 128
                            hp = piece.get()
                            if hf == 0:
                                k.dma("sp", hp, hp.a, hbuf, hbuf.a[128 + r0:128 + r0 + 128, g * 512:(g + 1) * 512])
                            else:
                                k.dma("sp", hp, hp.a, out_d, out_d.a[r0:r0 + 128, g * 512:(g + 1) * 512])
                            k.V([accs[i], hp], [hp], lambda e: e.tensor_tensor(out=hp.a, in0=accs[i].a, in1=hp.a, op=ALU.add))
                            if hf == 1:
                                jk = t512.get()
                                k.A([hp, ssq3], [jk, ssq3], lambda e: e.activation(out=jk.a[:, 0:512], in_=hp.a, func=AF.Square, accum_out=ssq3.a[:, i, g:g + 1]))
                            k.dma("sp", out_d, out_d.a[r0:r0 + 128, g * 512:(g + 1) * 512], hp, hp.a)
                k.dma("sp", xrow, xrow.a, None, nfb_in.partition_broadcast(128))
                for i in range(4):
                    r0 = m * TT + i * 128
                    tot = ssq.get(); rs = rstd.get()
                    k.V([ssq3], [tot], lambda e: e.tensor_reduce(out=tot.a, in_=ssq3.a[:, i, :], axis=mybir.AxisListType.X, op=ALU.add))
                    k.A([tot, epsc], [rs], lambda e: e.activation(out=rs.a, in_=tot.a, func=AF.Sqrt, bias=epsc.a[:, 0:1], scale=1.0 / D))
                    k.V([rs], [rs], lambda e: e.reciprocal(out=rs.a, in_=rs.a))
                    k.dma("sp", rowb, rowb.a, out_d, out_d.a[r0:r0 + 128, :])
                    k.V([rowb, rs, xrow], [rowb], lambda e: e.scalar_tensor_tensor(out=rowb.a, in0=rowb.a, scalar=rs.a[:, 0:1], in1=xrow.a, op0=ALU.mult, op1=ALU.mult))
                    k.dma("sp", out_d, out_d.a[r0:r0 + 128, :], rowb, rowb.a)
            fin = [out_d]
            if dbg:
                fin += [ydbg, hdbg]
            k.finish(fin)
            k.barrier()
        k.es = es
    return nc


Q0, K0, V0, GATE0, BETA0, ALPHA0, Z0, XBC0, DT0 = 0, 2048, 4096, 6144, 8192, 8208, 8224, 10272, 14368
GROUPS = [[0, 1, 2, 3], [4, 5, 6, 7]]


def _masks():
    p = np.arange(128)[:, None]
    f = np.arange(128)[None, :]
    ident = (p == f).astype(np.float32)
    triU = (p <= f).astype(np.float32)
    negU = np.where(f >= p, 0.0, NEG).astype(np.float32)
    posL = np.where(p > f, 0.0, -NEG).astype(np.float32)
    return np.ascontiguousarray(np.stack([ident, triU, negU, posL]))


def prep_inputs(inp, cfg):
    L, D, DFF = cfg["L"], cfg["D"], cfg["DFF"]
    KD, FB, NG = D // 128, DFF // 128, D // 512
    LQ = L // 4
    NT, NT3 = L // TT, LQ // TT
    f32 = np.float32
    x = np.asarray(inp["x"], f32)
    w_in = np.asarray(inp["w_in"], f32)[0]
    w_out = np.asarray(inp["w_out"], f32)[0]
    gcw = np.asarray(inp["gdn_conv_w"], f32)[0]
    scw = np.asarray(inp["ssm_conv_w"], f32)[0]
    scb = np.asarray(inp["ssm_conv_b"], f32)[0]
    ar = np.arange(128)
    masks = _masks()
    wgate = np.asarray(inp["ffn_w_gate"], f32)[0].reshape(KD, 128, FB, 128).transpose(2, 1, 0, 3)
    wup = np.asarray(inp["ffn_w_up"], f32)[0].reshape(KD, 128, FB, 128).transpose(2, 1, 0, 3)
    wg = np.ascontiguousarray(np.stack([wgate, wup], axis=2))
    wd = np.ascontiguousarray(np.asarray(inp["ffn_w_down"], f32)[0])
    fcw = np.ascontiguousarray(np.asarray(inp["ffn_conv_w"], f32)[0].reshape(3, FB, 128).transpose(2, 1, 0))
    fcb = np.ascontiguousarray(np.asarray(inp["ffn_conv_b"], f32)[0].reshape(FB, 128).T)
    n1c = np.ascontiguousarray(np.asarray(inp["norm1_w"], f32)[0].reshape(KD, 128).T)
    n2c = np.ascontiguousarray(np.asarray(inp["norm2_w"], f32)[0].reshape(KD, 128).T)
    nfb = np.ascontiguousarray(np.asarray(inp["norm_f_w"], f32))
    gdnnw = np.ascontiguousarray(np.asarray(inp["gdn_norm_w"], f32)[0].reshape(128, 1))
    perm = []
    for r in range(4):
        for hh in range(4):
            perm.append(128 * (4 * r + hh) + ar)
        perm.append(2048 + 512 * r + np.arange(512))
    perm = np.concatenate(perm)
    wo = np.ascontiguousarray(w_out[perm].reshape(KM, 128, NG, 512).transpose(2, 1, 0, 3))
    maps = []
    for c in range(8):
        b, j = c // 4, c % 4
        cols = []
        for hh in range(4):
            H = 4 * j + hh
            cols += [Q0 + 128 * H + ar, K0 + 128 * H + ar, V0 + 128 * H + ar, GATE0 + 128 * H + ar]
        for zb in range(4):
            cols.append(Z0 + 512 * j + zb * 128 + ar)
        for blk in range(4):
            cols.append(XBC0 + 512 * j + blk * 128 + ar)
        for g in range(2):
            cols.append(XBC0 + 2048 + 128 * (2 * j + g) + ar)
        for g in range(2):
            cols.append(XBC0 + 3072 + 128 * (2 * j + g) + ar)
        cols = np.concatenate(cols)
        w1 = np.ascontiguousarray(w_in[:, cols].reshape(KD, 128, 28, 128).transpose(2, 1, 0, 3))
        small = np.concatenate([BETA0 + 4 * j + np.arange(4), ALPHA0 + 4 * j + np.arange(4), DT0 + 8 * j + np.arange(8)])
        wsm = np.ascontiguousarray(w_in[:, small].reshape(KD, 128, 16).transpose(1, 0, 2))
        cw = np.zeros((128, 20, 4), f32)
        cb = np.zeros((128, 20), f32)
        for hh in range(4):
            H = 4 * j + hh
            for r in range(3):
                cw[:, hh * 3 + r, :] = gcw[:, r * 2048 + 128 * H + ar].T
        for i in range(8):
            if i < 4:
                ch = 512 * j + i * 128 + ar
            elif i < 6:
                ch = 2048 + 128 * (2 * j + (i - 4)) + ar
            else:
                ch = 3072 + 128 * (2 * j + (i - 6)) + ar
            cw[:, 12 + i, :] = scw[:, ch].T
            cb[:, 12 + i] = scb[ch]
        dtb = np.concatenate([np.asarray(inp["gdn_dt_bias"], f32)[0][4 * j:4 * j + 4], np.asarray(inp["ssm_dt_bias"], f32)[0][8 * j:8 * j + 8]])
        alog = np.concatenate([np.asarray(inp["gdn_A_log"], f32)[0][4 * j:4 * j + 4], np.asarray(inp["ssm_A_log"], f32)[0][8 * j:8 * j + 8]])
        ssd_D = np.asarray(inp["ssm_D"], f32)[0]
        dcol = np.zeros((128, 4), f32)
        for blk in range(4):
            dcol[:64, blk] = ssd_D[8 * j + 2 * blk]
            dcol[64:, blk] = ssd_D[8 * j + 2 * blk + 1]
        ssmnw = np.ascontiguousarray(np.asarray(inp["ssm_norm_w"], f32)[0][512 * j:512 * (j + 1)].reshape(4, 128).T)
        xq = np.zeros((128 + LQ, D), f32)
        if j > 0:
            xq[:] = x[b, j * LQ - 128:(j + 1) * LQ]
        else:
            xq[128:] = x[b, 0:LQ]
        idx = np.zeros((128, (NT3 + 1) * KM), np.int32)
        for blk in range(NT3 + 1):
            if blk == 0:
                tile = j * NT3 - 1 if j > 0 else NT
            else:
                tile = j * NT3 + (blk - 1)
            for kd in range(KM):
                idx[:, blk * KM + kd] = tile * 4096 + kd * 128 + ar
                if blk == 0:
                    idx[:, kd] = idx[:, kd] * 4 + 3
        maps.append({
            "x": np.ascontiguousarray(x[b]), "xq": xq, "w1": w1, "wsm": wsm, "n1c": n1c, "n2c": n2c, "nfb": nfb,
            "cw": cw, "cb": cb, "dtb": np.ascontiguousarray(np.tile(dtb[None, :], (128, 1))),
            "alog": np.ascontiguousarray(np.tile(alog[None, :], (128, 1))), "dcol": dcol, "ssmnw": ssmnw, "gdnnw": gdnnw,
            "masks": masks, "wo": wo, "wg": wg, "wd": wd, "fcw": fcw, "fcb": fcb, "idx": idx,
        })
    return maps


def run(inp, cfg):
    nc = build_program(cfg)
    maps = prep_inputs(inp, cfg)
    res = run_bass_kernel_spmd(nc, maps, core_ids=list(range(8)))
    return res


def kernel(**inputs):
    cfg = {"L": 8192, "D": 4096, "DFF": 11008, "groups": GROUPS}
    res = run(inputs, cfg)
    L, D = cfg["L"], cfg["D"]
    LQ = L // 4
    out = np.zeros((2, L, D), np.float32)
    for c in range(8):
        b, j = c // 4, c % 4
        out[b, j * LQ:(j + 1) * LQ] = res.results[c]["out"]
    return out
```

```python
import bisect
import contextlib
import numpy as np
import concourse.bass as bass
import concourse.mybir as mybir
from concourse.bass_utils import run_bass_kernel_spmd

F32 = mybir.dt.float32
BF16 = mybir.dt.bfloat16
I32 = mybir.dt.int32
AF = mybir.ActivationFunctionType
ALU = mybir.AluOpType
EPS = 1e-6
NEG = -30000.0


class Tl:
    __slots__ = ("a", "name", "w", "r", "dsem", "dn", "multi")

    def __init__(self, a, name, multi=False):
        self.a = a
        self.name = name
        self.w = [] if multi else None
        self.r = {}
        self.dsem = {}
        self.dn = {}
        self.multi = multi


class TlView:
    def __init__(self, parent, a):
        object.__setattr__(self, "p", parent)
        object.__setattr__(self, "a", a)

    def __getattr__(self, n):
        return getattr(object.__getattribute__(self, "p"), n)

    def __setattr__(self, n, v):
        setattr(object.__getattribute__(self, "p"), n, v)


class KB:
    ENG = ("pe", "dve", "act", "pool", "sp")

    def __init__(self, nc, es):
        self.nc = nc
        self.es = es
        self.es_sem = es
        self.dma_tiles = []
        self.q = {"pe": nc.tensor, "dve": nc.vector, "act": nc.scalar, "pool": nc.gpsimd, "sp": nc.sync}
        self.sem = {e: es.enter_context(nc.semaphore("s_" + e)) for e in self.ENG}
        self.emitted = {e: 0 for e in self.ENG}
        self.last = {e: None for e in self.ENG}
        self.sigs = {e: [] for e in self.ENG}
        self.seen = {e: {} for e in self.ENG}
        self.nsem = 0
        self.uid = 0

    def sb(self, name, shape, dt):
        t = self.es.enter_context(self.nc.sbuf_tensor("sb_" + name, list(shape), dt))
        return Tl(t[:], name)

    def ps(self, name, shape, dt):
        t = self.es.enter_context(self.nc.psum_tensor("ps_" + name, list(shape), dt))
        return Tl(t[:], name)

    def dram(self, name, shape, dt, kind="Internal", multi=False):
        t = self.nc.dram_tensor(name, list(shape), dt, kind=kind)
        return Tl(t.ap(), name, multi=multi)

    def view(self, ap, name):
        return Tl(ap, name)

    def newsem(self, name):
        self.nsem += 1
        return self.es_sem.enter_context(self.nc.semaphore(name))

    def _setw(self, t, dep):
        if t.multi:
            t.w = [d for d in t.w if d[1] is not dep[1]] + [dep]
        else:
            t.w = dep
        t.r = {}

    def barrier(self):
        engs = ["pe", "dve", "act"]
        for e in engs:
            if self.last[e] is not None and (not self.sigs[e] or self.sigs[e][-1] < self.emitted[e]):
                self.last[e].then_inc(self.sem[e], 1)
                self.sigs[e].append(self.emitted[e])
        for e in self.ENG:
            for o in engs:
                if o != e and self.sigs[o]:
                    self._wait_sem(e, self.sem[o], len(self.sigs[o]))
            for tl in self.dma_tiles:
                for qc in tl.dsem:
                    self._wait_sem(e, tl.dsem[qc], tl.dn[qc])

    def _wait_sem(self, eng, sem, val):
        d = self.seen[eng]
        if d.get(sem.name, 0) >= val:
            return
        d[sem.name] = val
        self.q[eng].wait_ge(sem, val)

    def _need(self, eng, dep):
        if dep is None:
            return
        if isinstance(dep, list):
            for d in dep:
                self._need(eng, d)
            return
        if dep[0] == "e":
            _, e, idx = dep
            sg = self.sigs[e]
            if not sg or sg[-1] < idx:
                self.last[e].then_inc(self.sem[e], 1)
                sg.append(self.emitted[e])
            pos = bisect.bisect_left(sg, idx)
            self._wait_sem(eng, self.sem[e], pos + 1)
        else:
            tl = dep[1]
            for qc in tl.dsem:
                self._wait_sem(eng, tl.dsem[qc], tl.dn[qc])

    def _deps(self, eng, reads, writes, is_dma=False):
        for t in reads:
            w = t.w
            if w is not None:
                if not isinstance(w, list) and w[0] == "e" and w[1] == eng and eng == "pe":
                    continue
                self._need(eng, w)
        for t in writes:
            w = t.w
            if w is not None and (isinstance(w, list) or not (w[0] == "e" and w[1] == eng and eng == "pe")):
                self._need(eng, w)
            for key, dep in t.r.items():
                if dep[0] == "e" and dep[1] == eng and eng == "pe":
                    continue
                self._need(eng, dep)

    def op(self, eng, reads, writes, fn):
        self._deps(eng, reads, writes)
        ins = fn(self.q[eng])
        self.emitted[eng] += 1
        idx = self.emitted[eng]
        self.last[eng] = ins
        for t in reads:
            t.r[eng] = ("e", eng, idx)
        for t in writes:
            self._setw(t, ("e", eng, idx))
        return ins

    def V(self, reads, writes, fn):
        return self.op("dve", reads, writes, fn)

    def A(self, reads, writes, fn):
        return self.op("act", reads, writes, fn)

    def P(self, reads, writes, fn):
        return self.op("pe", reads, writes, fn)

    def dma(self, qe, dst, dst_ap, src, src_ap, semtl=None, kind="dma", in_off=None):
        reads = [src] if src is not None else []
        writes = [dst] if dst is not None else []
        self._deps(qe, reads, writes, is_dma=True)
        if semtl is None:
            semtl = dst if dst is not None else src
        qc = "sw" if qe == "pool" else "hw"
        if qc not in semtl.dsem:
            if not semtl.dsem:
                self.dma_tiles.append(semtl)
            semtl.dsem[qc] = self.newsem("d%s_%s" % (qc, semtl.name))
            semtl.dn[qc] = 0
        e = self.q[qe]
        if kind == "dma":
            ins = e.dma_start(out=dst_ap, in_=src_ap)
            inc = 16
        elif kind == "ind":
            ins = e.indirect_dma_start(out=dst_ap, out_offset=None, in_=src_ap, in_offset=in_off)
            inc = 16
        else:
            raise ValueError(kind)
        semtl.dn[qc] += inc
        ins.then_inc(semtl.dsem[qc], inc)
        self.emitted[qe] += 1
        self.last[qe] = None
        dep = ("d", semtl)
        if src is not None:
            src.r["d_" + semtl.name] = dep
        if dst is not None:
            self._setw(dst, dep)
        return ins

    def collective(self, src, dst, src_ap, dst_ap, groups, cctl):
        self._deps("pool", [src], [dst], is_dma=True)
        if "cc" not in cctl.dsem:
            if not cctl.dsem:
                self.dma_tiles.append(cctl)
            cctl.dsem["cc"] = self.newsem("dcc_" + cctl.name)
            cctl.dn["cc"] = 0
        ins = self.q["pool"].collective_compute("AllGather", ALU.bypass, replica_groups=groups, ins=[src_ap], outs=[dst_ap])
        cctl.dn["cc"] += 1
        ins.then_inc(cctl.dsem["cc"], 1)
        self.emitted["pool"] += 1
        self.last["pool"] = None
        dep = ("d", cctl)
        src.r["d_" + cctl.name] = dep
        self._setw(dst, dep)

    def finish(self, tls):
        for t in tls:
            if t.w is not None:
                self._need("sp", t.w)


class RR:
    def __init__(self, items):
        self.items = items
        self.i = 0

    def get(self):
        x = self.items[self.i % len(self.items)]
        self.i += 1
        return x


KM = 32
TT = 512
WBE = 8192


LOCK = [True]


def lockstep(gens):
    gens = list(gens)
    if not LOCK[0]:
        for g in gens:
            for _ in g:
                pass
        return
    while gens:
        nxt = []
        for g in gens:
            try:
                next(g)
                nxt.append(g)
            except StopIteration:
                pass
        gens = nxt


def build_program(cfg):
    L, D, DFF = cfg["L"], cfg["D"], cfg["DFF"]
    dbg = cfg.get("debug", False)
    LOCK[0] = cfg.get("lock", True)
    KD = D // 128
    NT = L // TT
    LQ = L // 4
    NT3 = LQ // TT
    FB = DFF // 128
    FH = FB // 2
    NG = D // 512
    assert FB % 2 == 0 and LQ % TT == 0 and D % 512 == 0 and FH <= 44 and 2 * KD * 128 <= WBE

    nc = bass.Bass("TRN2", target_bir_lowering=False)
    es = contextlib.ExitStack()
    with es:
        k = KB(nc, es)

        def din(name, shape, dt=F32):
            return nc.dram_tensor(name, list(shape), dt, kind="ExternalInput").ap()

        dump_on = cfg.get("dump", False)
        dumped = []

        def dump(name, tl, ap, shape, dt=F32):
            if not dump_on:
                return
            d = k.dram("D_" + name, list(shape), dt, kind="ExternalOutput")
            k.dma("sp", d, d.a, tl, ap)
            dumped.append(d)

        x_in = din("x", [L, D])
        xq_in = din("xq", [128 + LQ, D])
        w1_in = din("w1", [28, 128, KD, 128])
        wsm_in = din("wsm", [128, KD, 16])
        n1c_in = din("n1c", [128, KD])
        n2c_in = din("n2c", [128, KD])
        nfb_in = din("nfb", [D])
        cw_in = din("cw", [128, 20, 4])
        cb_in = din("cb", [128, 20])
        dtb_in = din("dtb", [128, 12])
        alog_in = din("alog", [128, 12])
        dcol_in = din("dcol", [128, 4])
        ssmnw_in = din("ssmnw", [128, 4])
        gdnnw_in = din("gdnnw", [128, 1])
        masks_in = din("masks", [4, 128, 128])
        wo_in = din("wo", [NG, 128, KM, 512])
        wg_in = din("wg", [FB, 128, 2, KD, 128])
        wd_in = din("wd", [DFF, D])
        fcw_in = din("fcw", [128, FB, 3])
        fcb_in = din("fcb", [128, FB])
        idx_in = din("idx", [128, (NT3 + 1) * KM], I32)
        out_d = k.dram("out", [LQ, D], F32, kind="ExternalOutput")
        ybuf = k.dram("ybuf", [NT, 1024, TT], BF16)
        gath = k.dram("gath", [(NT + 1) * 4096, TT], BF16, multi=True)
        hbuf = k.dram("hbuf", [128 + LQ, D], F32)
        WSLc = WBE // 512
        NSL = (FH + WSLc - 1) // WSLc
        w1s = k.dram("w1s", [28, 128, KD * 128], BF16)
        wos = k.dram("wos", [NG * 2, 128, KM * 256], BF16)
        wgs = k.dram("wgs", [FB, 128, 2 * KD * 128], BF16)
        wds = k.dram("wds", [2 * NG * NSL, 128, WBE], BF16)
        if dbg:
            ydbg = k.dram("ydbg", [NT, 1024, TT], BF16, kind="ExternalOutput")
            hdbg = k.dram("hdbg", [128 + LQ, D], F32, kind="ExternalOutput")

        cst = {"sp": Tl(None, "cst_sp"), "pool": Tl(None, "cst_pool")}

        HB = WBE // 2
        conv_jobs = []
        for g in range(NG):
            for hf in range(2):
                for q2 in range(2):
                    ks = slice(q2 * (KM // 2), (q2 + 1) * (KM // 2))
                    conv_jobs.append((wos, wos.a[g * 2 + hf].rearrange("p (k c) -> p k c", k=KM)[:, ks, :],
                                      wo_in[g][:, ks, hf * 256:(hf + 1) * 256], (KM // 2, 256)))
        for fb in range(FB):
            for u in range(2):
                conv_jobs.append((wgs, wgs.a[fb].rearrange("p (u k c) -> p u k c", u=2, k=KD)[:, u], wg_in[fb][:, u], (KD, 128)))
        for hf in range(2):
            for g in range(NG):
                for sl in range(NSL):
                    f0 = sl * WSLc
                    nf = min(WSLc, FH - f0)
                    si = (hf * NG + g) * NSL + sl
                    for q2 in range(2):
                        fa = q2 * (WSLc // 2)
                        fn_ = min(WSLc // 2, nf - fa)
                        if fn_ <= 0:
                            continue
                        r0 = (hf * FH + f0 + fa) * 128
                        conv_jobs.append((wds, wds.a[si][:, fa * 512:(fa + fn_) * 512].rearrange("p (f c) -> p f c", c=512),
                                          wd_in[r0:r0 + fn_ * 128, g * 512:(g + 1) * 512].rearrange("(f p) c -> p f c", p=128), (fn_, 512)))
        conv_pos = [0]
        bnc = []

        def emit_conv(n):
            for _ in range(n):
                if conv_pos[0] < len(conv_jobs):
                    d, dap, sap, (a_, b_) = conv_jobs[conv_pos[0]]
                    bt = bnc[conv_pos[0] % 2]
                    bv = bt.a[:, 0:a_ * b_].rearrange("p (a b) -> p a b", a=a_)
                    k.dma("pool", bt, bv, None, sap)
                    k.dma("pool", d, dap, bt, bv)
                    conv_pos[0] += 1

        def cload(name, shape, src_ap, dt=F32, q="sp"):
            t = k.sb(name, shape, dt)
            k.dma(q, t, t.a, None, src_ap, semtl=cst[q])
            return t

        ident = cload("ident", [128, 128], masks_in[0])
        triU = cload("triU", [128, 128], masks_in[1])
        negU = cload("negU", [128, 128], masks_in[2])
        posL = cload("posL", [128, 128], masks_in[3])
        n1c = cload("n1c", [128, KD], n1c_in)
        n2c = cload("n2c", [128, KD], n2c_in)
        cw = cload("cw", [128, 20, 4], cw_in)
        cbias = cload("cbias", [128, 20], cb_in)
        dtb = cload("dtb", [128, 12], dtb_in)
        alog = cload("alog", [128, 12], alog_in)
        dcol = cload("dcol", [128, 4], dcol_in)
        ssmnw = cload("ssmnw", [128, 4], ssmnw_in)
        gdnnw = cload("gdnnw", [128, 1], gdnnw_in)
        fcw = cload("fcw", [128, FB, 3], fcw_in)
        fcb = cload("fcb", [128, FB], fcb_in)
        idx = cload("idx", [128, (NT3 + 1) * KM], idx_in, dt=I32, q="pool")
        wsm = cload("wsm", [128, KD, 16], wsm_in, dt=BF16, q="pool")
        identb = k.sb("identb", [128, 128], BF16)
        ones = k.sb("ones", [128, 128], F32)
        epsc = k.sb("epsc", [128, 1], F32)
        eps128 = k.sb("eps128", [128, 1], F32)
        onec = k.sb("onec", [128, 1], F32)
        nA = k.sb("nA", [128, 12], F32)
        k.V([ident], [identb], lambda e: e.tensor_copy(out=identb.a, in_=ident.a))
        k.V([], [ones], lambda e: e.memset(ones.a, 1.0))
        k.V([], [epsc], lambda e: e.memset(epsc.a, EPS))
        k.V([], [eps128], lambda e: e.memset(eps128.a, EPS * 128.0))
        k.V([], [onec], lambda e: e.memset(onec.a, 1.0))
        k.A([alog], [nA], lambda e: e.activation(out=nA.a, in_=alog.a, func=AF.Exp))
        k.V([nA], [nA], lambda e: e.tensor_scalar(out=nA.a, in0=nA.a, scalar1=-1.0, scalar2=None, op0=ALU.mult))

        pbig_l = [k.ps("pb%d" % i, [128, 512], F32) for i in range(2)]
        pbig = RR(pbig_l)
        ptr = RR([k.ps("ptr%d" % i, [128, 1024], BF16) for i in range(2)])
        pqb = [k.ps("pqb%d" % i, [128, 512], F32) for i in range(4)]
        pq = RR([TlView(pqb[i], pqb[i].a[:, qd * 128:(qd + 1) * 128]) for qd in range(4) for i in range(4)])
        pq3 = RR([TlView(pqb[i], pqb[i].a[:, qd * 128:(qd + 1) * 128]) for qd in range(4) for i in range(2)])
        pacc = [pqb[2], pqb[3]]

        xrow = k.sb("xrow", [128, D], F32)
        hnT = k.sb("hnT", [128, KD, TT + 2], BF16)
        wblk_l = [k.sb("wblk%d" % i, [128, WBE], BF16) for i in range(2)]
        wblk = RR(wblk_l)
        wlo = RR([Tl(wblk_l[i].a[:, 0:WBE // 2], "wlo%d" % i) for i in range(2)])
        bnc.extend([Tl(wblk_l[i].a[:, WBE // 2:WBE], "bnc%d" % i) for i in range(2)])
        ssq = RR([k.sb("ssq%d" % i, [128, 1], F32) for i in range(4)])
        rstd = RR([k.sb("rstd%d" % i, [128, 1], F32) for i in range(4)])
        big = k.sb("big", [128, 22528], BF16)
        t512 = RR([k.sb("t512_%d" % i, [128, 516], F32) for i in range(5)])

        def norm_transpose(xr, hb, hba, col0, nwc, tgt, ncols=128, srccol=0):
            sq = ssq.get()
            rs = rstd.get()
            k.V([], [sq], lambda e: e.memset(sq.a, 0.0))
            k.A([xr, sq], [hb, sq], lambda e: e.activation(out=hba, in_=xr.a, func=AF.Square, accum_out=sq.a))
            k.A([sq, epsc], [rs], lambda e: e.activation(out=rs.a, in_=sq.a, func=AF.Sqrt, bias=epsc.a[:, 0:1], scale=1.0 / D))
            k.V([rs], [rs], lambda e: e.reciprocal(out=rs.a, in_=rs.a))
            k.A([xr, rs], [hb], lambda e: e.activation(out=hba, in_=xr.a, func=AF.Copy, scale=rs.a[:, 0:1]))
            for kg in range(KD // 4):
                pt = ptr.get()
                for qd in range(4):
                    kd = kg * 4 + qd
                    k.P([hb, identb], [pt], lambda e: e.transpose(pt.a[:, qd * 128:(qd + 1) * 128], hba[:, kd * 128:(kd + 1) * 128], identb.a))
                src = pt.a[:, 0:512].rearrange("p (q c) -> p q c", q=4)[:, :, srccol:srccol + ncols]
                dstv = tgt.a[:, kg * 4:kg * 4 + 4, col0:col0 + ncols]
                sc = nwc.a[:, kg * 4:kg * 4 + 4].unsqueeze(2).to_broadcast([128, 4, ncols])
                k.V([pt, nwc], [tgt], lambda e: e.tensor_tensor(out=dstv, in0=src, in1=sc, op=ALU.mult))

        def conv_silu(pb, ci, dst_tl, dst_ap, K, hal, cwt, cbt):
            pre = t512.get()
            acc = t512.get()
            H = K - 1
            W = TT
            k.A([hal], [pre], lambda e: e.activation(out=pre.a[:, 0:H], in_=hal.a[:, ci, 0:H], func=AF.Copy))
            k.A([pb], [pre], lambda e: e.activation(out=pre.a[:, H:H + W], in_=pb.a[:, 0:W], func=AF.Copy))
            k.V([pre, cwt, cbt], [acc], lambda e: e.tensor_scalar(out=acc.a[:, 0:W], in0=pre.a[:, H:H + W], scalar1=cwt.a[:, ci, K - 1:K], scalar2=cbt.a[:, ci:ci + 1], op0=ALU.mult, op1=ALU.add))
            for j in range(K - 1):
                k.V([pre, cwt, acc], [acc], lambda e: e.scalar_tensor_tensor(out=acc.a[:, 0:W], in0=pre.a[:, j:j + W], scalar=cwt.a[:, ci, j:j + 1], in1=acc.a[:, 0:W], op0=ALU.mult, op1=ALU.add))
            k.A([pre], [hal], lambda e: e.activation(out=hal.a[:, ci, 0:H], in_=pre.a[:, W:W + H], func=AF.Copy))
            k.A([acc], [dst_tl], lambda e: e.activation(out=dst_ap, in_=acc.a[:, 0:W], func=AF.Silu))

        def mm(out_tl, out_ap, l_tl, l_ap, r_tl, r_ap, start=True, stop=True):
            k.P([l_tl, r_tl], [out_tl], lambda e: e.matmul(out_ap, lhsT=l_ap, rhs=r_ap, start=start, stop=stop))

        es1 = contextlib.ExitStack()
        with es1:
            k.es = es1
            sv = big.a.bitcast(F32)

            def slot(i):
                return sv[:, i * TT:(i + 1) * TT]
            sstore = k.sb("sstore", [128, 2, TT], F32)
            zst = k.sb("zst", [128, 4, TT], BF16)
            halo = k.sb("halo", [128, 20, 3], F32)
            k.V([], [halo], lambda e: e.memset(halo.a, 0.0))
            o_g = k.sb("o_g", [128, 4, TT], BF16)
            o_s = k.sb("o_s", [128, 4, TT], BF16)
            yT = k.sb("yT", [128, 8, TT], BF16)
            hb1_ap = yT.a.rearrange("p b c -> p (b c)")[:, 0:D]
            S_g = [k.sb("S_g%d" % h, [128, 128], F32) for h in range(4)]
            S_s = [k.sb("S_s%d" % h, [128, 64], F32) for h in range(8)]
            for s_ in S_g + S_s:
                k.V([], [s_], lambda e: e.memset(s_.a, 0.0))
            mhp = [RR([k.sb("mh%d_%d" % (h, i), [128, 128], F32) for i in range(8)]) for h in range(4)]
            mlp = [RR([k.sb("ml%d_%d" % (h, i), [128, 128], F32) for i in range(5)]) for h in range(4)]
            mcb = [k.sb("mcb%d" % i, [128, 128], F32) for i in range(2)]
            sm12 = RR([k.sb("sm%d" % i, [128, 16], F32) for i in range(24)])
            ktok = [k.sb("ktok%d" % h, [128, 128], F32) for h in range(4)]
            vtok = [k.sb("vtok%d" % h, [128, 128], F32) for h in range(4)]
            btok = [k.sb("btok%d" % g, [128, 128], F32) for g in range(2)]
            xtok = k.sb("xtok", [128, 512], F32)

            def proj_block(cb, pb):
                wb = wlo.get()
                wv = wb.a[:, 0:KD * 128].rearrange("p (k c) -> p k c", k=KD)
                if t == 0:
                    k.dma("pool", wb, wv, None, w1_in[cb])
                    k.dma("sp", w1s, w1s.a[cb], wb, wb.a[:, 0:KD * 128])
                else:
                    k.dma("sp", wb, wb.a[:, 0:KD * 128], w1s, w1s.a[cb])
                for kd in range(KD):
                    k.P([wb, hnT], [pb], lambda e: e.matmul(pb.a, lhsT=wv[:, kd, :], rhs=hnT.a[:, kd, 0:TT], start=(kd == 0), stop=(kd == KD - 1)))
                return pb

            def l2norm(src_tl, src_ap, dst_tl, dst_ap, scale, bias_tl, pb):
                sq = t512.get()
                k.A([src_tl], [sq], lambda e: e.activation(out=sq.a[:, 0:TT], in_=src_ap, func=AF.Square))
                mm(pb, pb.a, ones, ones.a, sq, sq.a[:, 0:TT])
                rt = t512.get()
                k.A([pb, bias_tl], [rt], lambda e: e.activation(out=rt.a[:, 0:TT], in_=pb.a, func=AF.Sqrt, bias=bias_tl.a[:, 0:1], scale=scale))
                k.V([rt], [rt], lambda e: e.reciprocal(out=rt.a[:, 0:TT], in_=rt.a[:, 0:TT]))
                k.V([src_tl, rt], [dst_tl], lambda e: e.tensor_tensor(out=dst_ap, in0=src_ap, in1=rt.a[:, 0:TT], op=ALU.mult))

            def transpose_f(dst_tl, dst_ap, src_tl, src_ap):
                p = pq.get()
                k.P([src_tl, ident], [p], lambda e: e.transpose(p.a, src_ap, ident.a))
                k.A([p], [dst_tl], lambda e: e.activation(out=dst_ap, in_=p.a, func=AF.Copy))

            for t in range(NT):
                for s in range(4):
                    k.dma("sp", xrow, xrow.a, None, x_in[t * TT + s * 128: t * TT + (s + 1) * 128, :])
                    norm_transpose(xrow, yT, hb1_ap, s * 128, n1c, hnT)
                jobs = []
                for hh in range(4):
                    for r in range(4):
                        jobs.append(("g", hh, r))
                for sbk in range(12):
                    jobs.append(("s", sbk, 0))

                def post(job, pb, ji):
                    kind, a_, r = job
                    if kind == "g":
                        hh = a_
                        if r == 3:
                            k.A([pb], [big], lambda e: e.activation(out=slot(hh * 4 + 3), in_=pb.a, func=AF.Silu))
                        elif r == 2:
                            conv_silu(pb, hh * 3 + r, big, slot(hh * 4 + 2), 4, halo, cw, cbias)
                        else:
                            tmp = t512.get()
                            conv_silu(pb, hh * 3 + r, tmp, tmp.a[:, 0:TT], 4, halo, cw, cbias)
                            if r == 0:
                                l2norm(tmp, tmp.a[:, 0:TT], big, slot(hh * 4 + 0), 128.0, eps128, pqb[ji % 2])
                            else:
                                l2norm(tmp, tmp.a[:, 0:TT], big, slot(hh * 4 + 1), 1.0, epsc, pqb[ji % 2])
                    else:
                        sbk = a_
                        if sbk < 4:
                            k.A([pb], [zst], lambda e: e.activation(out=zst.a[:, sbk, :], in_=pb.a, func=AF.Silu))
                        elif sbk < 8:
                            conv_silu(pb, 12 + (sbk - 4), big, slot(16 + sbk - 4), 4, halo, cw, cbias)
                        elif sbk < 10:
                            conv_silu(pb, 12 + (sbk - 4), big, slot(20 + sbk - 8), 4, halo, cw, cbias)
                        else:
                            conv_silu(pb, 12 + (sbk - 4), sstore, sstore.a[:, sbk - 10, :], 4, halo, cw, cbias)

                prev = None
                for ji, job in enumerate(jobs):
                    cb = (job[1] * 4 + job[2]) if job[0] == "g" else 16 + job[1]
                    pb = proj_block(cb, pbig_l[ji % 2])
                    if prev is not None:
                        post(*prev)
                    prev = (job, pb, ji)
                post(*prev)
                if t >= 1:
                    emit_conv((len(conv_jobs) + NT - 2) // max(NT - 1, 1))
                for c4 in range(4):
                    cs = slice(c4 * 128, (c4 + 1) * 128)
                    psm = pq.get()
                    for kd in range(KD):
                        k.P([hnT, wsm], [psm], lambda e: e.matmul(psm.a[:, 0:16], lhsT=hnT.a[:, kd, cs], rhs=wsm.a[:, kd, :], start=(kd == 0), stop=(kd == KD - 1)))
                    sm = sm12.get()
                    k.A([psm], [sm], lambda e: e.activation(out=sm.a, in_=psm.a[:, 0:16], func=AF.Copy))
                    xs_ = sm12.get(); ax = sm12.get(); ex = sm12.get(); sp = sm12.get(); G = sm12.get(); beta = sm12.get()
                    k.V([sm, dtb], [xs_], lambda e: e.tensor_tensor(out=xs_.a[:, 0:12], in0=sm.a[:, 4:16], in1=dtb.a, op=ALU.add))
                    k.A([xs_], [ax], lambda e: e.activation(out=ax.a[:, 0:12], in_=xs_.a[:, 0:12], func=AF.Abs))
                    k.A([ax], [ex], lambda e: e.activation(out=ex.a[:, 0:12], in_=ax.a[:, 0:12], func=AF.Exp, scale=-1.0))
                    k.A([ex, onec], [ex], lambda e: e.activation(out=ex.a[:, 0:12], in_=ex.a[:, 0:12], func=AF.Ln, bias=onec.a[:, 0:1], scale=1.0))
                    k.V([xs_, ex], [sp], lambda e: e.scalar_tensor_tensor(out=sp.a[:, 0:12], in0=xs_.a[:, 0:12], scalar=0.0, in1=ex.a[:, 0:12], op0=ALU.max, op1=ALU.add))
                    k.V([sp, nA], [G], lambda e: e.tensor_tensor(out=G.a[:, 0:12], in0=sp.a[:, 0:12], in1=nA.a, op=ALU.mult))
                    k.A([sm], [beta], lambda e: e.activation(out=beta.a[:, 0:4], in_=sm.a[:, 0:4], func=AF.Sigmoid))
                    pc = pq.get()
                    mm(pc, pc.a[:, 0:12], triU, triU.a, G, G.a[:, 0:12])
                    mm(pc, pc.a[:, 16:28], ones, ones.a, G, G.a[:, 0:12])
                    gc = sm12.get(); gl = sm12.get(); eg = sm12.get(); kdc = sm12.get(); dbc = sm12.get(); bg = sm12.get(); dif = sm12.get()
                    k.A([pc], [gc], lambda e: e.activation(out=gc.a[:, 0:12], in_=pc.a[:, 0:12], func=AF.Copy))
                    k.A([pc], [gl], lambda e: e.activation(out=gl.a[:, 0:12], in_=pc.a[:, 16:28], func=AF.Copy))
                    k.A([gc], [eg], lambda e: e.activation(out=eg.a[:, 0:12], in_=gc.a[:, 0:12], func=AF.Exp))
                    k.V([gl, gc], [dif], lambda e: e.tensor_tensor(out=dif.a[:, 0:12], in0=gl.a[:, 0:12], in1=gc.a[:, 0:12], op=ALU.subtract))
                    k.A([dif], [kdc], lambda e: e.activation(out=kdc.a[:, 0:12], in_=dif.a[:, 0:12], func=AF.Exp))
                    k.A([gl], [dbc], lambda e: e.activation(out=dbc.a[:, 0:12], in_=gl.a[:, 0:12], func=AF.Exp))
                    k.V([beta, eg], [bg], lambda e: e.tensor_tensor(out=bg.a[:, 0:4], in0=beta.a[:, 0:4], in1=eg.a[:, 0:4], op=ALU.mult))
                    DMP = (t == 0 and c4 < 2)
                    if DMP:
                        tg = "_c%d" % c4
                        dump("sm" + tg, sm, sm.a, [128, 16]); dump("sp" + tg, sp, sp.a[:, 0:12], [128, 12]); dump("G" + tg, G, G.a[:, 0:12], [128, 12])
                        dump("gc" + tg, gc, gc.a[:, 0:12], [128, 12]); dump("gl" + tg, gl, gl.a[:, 0:12], [128, 12]); dump("beta" + tg, beta, beta.a[:, 0:4], [128, 4])
                        dump("kdc" + tg, kdc, kdc.a[:, 0:12], [128, 12]); dump("dbc" + tg, dbc, dbc.a[:, 0:12], [128, 12])
                        if c4 == 0:
                            for i_ in range(22):
                                dump("slot%d" % i_, big, slot(i_), [128, TT])
                            dump("sstore", sstore, sstore.a, [128, 2, TT])

                    def gdn_head(hh):
                        mh, ml = mhp[hh], mlp[hh]
                        qT = slot(hh * 4 + 0)[:, cs]
                        kT = slot(hh * 4 + 1)[:, cs]
                        vT = slot(hh * 4 + 2)[:, cs]
                        p1 = pq.get()
                        k.P([big, ident], [p1], lambda e: e.transpose(p1.a, kT, ident.a))
                        yield
                        k.A([p1], [ktok[hh]], lambda e: e.activation(out=ktok[hh].a, in_=p1.a, func=AF.Copy))
                        p2 = pq.get()
                        k.P([big, ident], [p2], lambda e: e.transpose(p2.a, vT, ident.a))
                        yield
                        k.A([p2], [vtok[hh]], lambda e: e.activation(out=vtok[hh].a, in_=p2.a, func=AF.Copy))
                        gb = mh.get()
                        k.V([ones, G], [gb], lambda e: e.tensor_scalar(out=gb.a, in0=ones.a, scalar1=G.a[:, hh:hh + 1], scalar2=None, op0=ALU.mult))
                        prow = pq.get()
                        mm(prow, prow.a, gb, gb.a, triU, triU.a)
                        yield
                        X = mh.get(); GT = mh.get(); XL = mh.get(); GL = mh.get(); Eg = mh.get()
                        k.V([prow, gc, negU], [X], lambda e: e.scalar_tensor_tensor(out=X.a, in0=prow.a, scalar=gc.a[:, hh:hh + 1], in1=negU.a, op0=ALU.subtract, op1=ALU.min))
                        k.A([X], [GT], lambda e: e.activation(out=GT.a, in_=X.a, func=AF.Exp))
                        k.V([prow, gc, posL], [XL], lambda e: e.scalar_tensor_tensor(out=XL.a, in0=prow.a, scalar=gc.a[:, hh:hh + 1], in1=posL.a, op0=ALU.subtract, op1=ALU.max))
                        k.A([XL], [GL], lambda e: e.activation(out=GL.a, in_=XL.a, func=AF.Exp, scale=-1.0))
                        k.A([prow], [Eg], lambda e: e.activation(out=Eg.a, in_=prow.a, func=AF.Exp))
                        pqk = pq.get()
                        mm(pqk, pqk.a, big, kT, big, qT)
                        yield
                        attnT = ml.get()
                        k.V([pqk, GT], [attnT], lambda e: e.tensor_tensor(out=attnT.a, in0=pqk.a, in1=GT.a, op=ALU.mult))
                        vb = ml.get(); kbg = ml.get(); kdec = ml.get(); qdec = ml.get()
                        k.V([vtok[hh], beta], [vb], lambda e: e.tensor_scalar(out=vb.a, in0=vtok[hh].a, scalar1=beta.a[:, hh:hh + 1], scalar2=None, op0=ALU.mult))
                        k.V([ktok[hh], bg], [kbg], lambda e: e.tensor_scalar(out=kbg.a, in0=ktok[hh].a, scalar1=bg.a[:, hh:hh + 1], scalar2=None, op0=ALU.mult))
                        k.V([ktok[hh], kdc], [kdec], lambda e: e.tensor_scalar(out=kdec.a, in0=ktok[hh].a, scalar1=kdc.a[:, hh:hh + 1], scalar2=None, op0=ALU.mult))
                        k.V([big, Eg], [qdec], lambda e: e.tensor_tensor(out=qdec.a, in0=qT, in1=Eg.a, op=ALU.mult))
                        pkk = pq.get()
                        mm(pkk, pkk.a, big, kT, big, kT)
                        yield
                        Am = mh.get()
                        k.V([pkk, beta, GL], [Am], lambda e: e.scalar_tensor_tensor(out=Am.a, in0=pkk.a, scalar=beta.a[:, hh:hh + 1], in1=GL.a, op0=ALU.mult, op1=ALU.mult))
                        pN = pq.get()
                        k.P([Am, ident], [pN], lambda e: e.transpose(pN.a, Am.a, ident.a))
                        yield
                        Nm = mh.get()
                        k.A([pN], [Nm], lambda e: e.activation(out=Nm.a, in_=pN.a, func=AF.Copy))
                        R = mh.get()
                        k.V([ident, Nm], [R], lambda e: e.tensor_tensor(out=R.a, in0=ident.a, in1=Nm.a, op=ALU.subtract))
                        Pm, PTm = Nm, Am
                        for lvl in range(1, 7):
                            pPT = pq.get()
                            mm(pPT, pPT.a, Pm, Pm.a, PTm, PTm.a)
                            yield
                            nPT = mh.get()
                            k.A([pPT], [nPT], lambda e: e.activation(out=nPT.a, in_=pPT.a, func=AF.Copy))
                            if lvl < 6:
                                pP = pq.get()
                                mm(pP, pP.a, PTm, PTm.a, Pm, Pm.a)
                                yield
                                nP = mh.get()
                                k.V([pP], [nP], lambda e: e.tensor_copy(out=nP.a, in_=pP.a))
                            pR = pq.get()
                            mm(pR, pR.a, nPT, nPT.a, R, R.a)
                            yield
                            nR = mh.get()
                            k.V([pR, R], [nR], lambda e: e.tensor_tensor(out=nR.a, in0=pR.a, in1=R.a, op=ALU.add))
                            R = nR
                            PTm = nPT
                            if lvl < 6:
                                Pm = nP
                        pw = pq.get()
                        mm(pw, pw.a, kbg, kbg.a, R, R.a)
                        yield
                        wTn = mh.get()
                        k.A([pw], [wTn], lambda e: e.activation(out=wTn.a, in_=pw.a, func=AF.Copy, scale=-1.0))
                        pv = pq.get()
                        mm(pv, pv.a, R, R.a, vb, vb.a, start=True, stop=False)
                        mm(pv, pv.a, wTn, wTn.a, S_g[hh], S_g[hh].a, start=False, stop=True)
                        yield
                        vnew = mh.get()
                        k.V([pv], [vnew], lambda e: e.tensor_copy(out=vnew.a, in_=pv.a))
                        po = pq.get()
                        mm(po, po.a, S_g[hh], S_g[hh].a, qdec, qdec.a, start=True, stop=False)
                        mm(po, po.a, vnew, vnew.a, attnT, attnT.a, start=False, stop=True)
                        yield
                        k.A([po], [o_g], lambda e: e.activation(out=o_g.a[:, hh, cs], in_=po.a, func=AF.Copy))
                        pS = pq.get()
                        mm(pS, pS.a, kdec, kdec.a, vnew, vnew.a)
                        yield
                        k.V([S_g[hh], dbc, pS], [S_g[hh]], lambda e: e.scalar_tensor_tensor(out=S_g[hh].a, in0=S_g[hh].a, scalar=dbc.a[:, hh:hh + 1], in1=pS.a, op0=ALU.mult, op1=ALU.add))

                    lockstep([gdn_head(hh) for hh in range(4)])

                    for g2 in range(2):
                        transpose_f(btok[g2], btok[g2].a, big, slot(20 + g2)[:, cs])
                    for blk in range(4):
                        transpose_f(xtok, xtok.a[:, blk * 128:(blk + 1) * 128], big, slot(16 + blk)[:, cs])
                    for g2 in range(2):
                        p = pq.get()
                        mm(p, p.a, big, slot(20 + g2)[:, cs], sstore, sstore.a[:, g2, cs])
                        k.A([p], [mcb[g2]], lambda e: e.activation(out=mcb[g2].a, in_=p.a, func=AF.Copy))

                    def ssd_head(h, pos_):
                        g2 = h // 4
                        col = 4 + h
                        mh = mhp[h % 4]
                        gb = mh.get()
                        k.V([ones, G], [gb], lambda e: e.tensor_scalar(out=gb.a, in0=ones.a, scalar1=G.a[:, col:col + 1], scalar2=None, op0=ALU.mult))
                        prow = pq.get()
                        mm(prow, prow.a, gb, gb.a, triU, triU.a)
                        yield
                        X = mh.get(); GT = mh.get(); Eg = mh.get()
                        k.V([prow, gc, negU], [X], lambda e: e.scalar_tensor_tensor(out=X.a, in0=prow.a, scalar=gc.a[:, col:col + 1], in1=negU.a, op0=ALU.subtract, op1=ALU.min))
                        k.A([X], [GT], lambda e: e.activation(out=GT.a, in_=X.a, func=AF.Exp))
                        k.A([prow], [Eg], lambda e: e.activation(out=Eg.a, in_=prow.a, func=AF.Exp))
                        attnT = mh.get(); qdec = mh.get(); kdec = mh.get(); vh = mh.get()
                        k.V([mcb[g2], GT], [attnT], lambda e: e.tensor_tensor(out=attnT.a, in0=mcb[g2].a, in1=GT.a, op=ALU.mult))
                        k.V([sstore, Eg], [qdec], lambda e: e.tensor_tensor(out=qdec.a, in0=sstore.a[:, g2, cs], in1=Eg.a, op=ALU.mult))
                        k.V([btok[g2], kdc], [kdec], lambda e: e.tensor_scalar(out=kdec.a, in0=btok[g2].a, scalar1=kdc.a[:, col:col + 1], scalar2=None, op0=ALU.mult))
                        k.V([xtok, sp], [vh], lambda e: e.tensor_scalar(out=vh.a[:, 0:64], in0=xtok.a[:, h * 64:(h + 1) * 64], scalar1=sp.a[:, col:col + 1], scalar2=None, op0=ALU.mult))
                        half = slice((h % 2) * 64, (h % 2) * 64 + 64)
                        mm(pos_, pos_.a[half, :], S_s[h], S_s[h].a, qdec, qdec.a, start=True, stop=False)
                        mm(pos_, pos_.a[half, :], vh, vh.a[:, 0:64], attnT, attnT.a, start=False, stop=True)
                        pS = pq.get()
                        mm(pS, pS.a[:, 0:64], kdec, kdec.a, vh, vh.a[:, 0:64])
                        yield
                        k.V([S_s[h], dbc, pS], [S_s[h]], lambda e: e.scalar_tensor_tensor(out=S_s[h].a, in0=S_s[h].a, scalar=dbc.a[:, col:col + 1], in1=pS.a[:, 0:64], op0=ALU.mult, op1=ALU.add))

                    for g2 in range(2):
                        pp = [pq.get(), pq.get()]
                        lockstep([ssd_head(g2 * 4 + i, pp[i // 2]) for i in range(4)])
                        for i2 in range(2):
                            k.A([pp[i2]], [o_s], lambda e: e.activation(out=o_s.a[:, g2 * 2 + i2, cs], in_=pp[i2].a, func=AF.Copy))

                for hh in range(4):
                    sq = t512.get()
                    k.A([o_g], [sq], lambda e: e.activation(out=sq.a[:, 0:TT], in_=o_g.a[:, hh, :], func=AF.Square))
                    pb = pbig.get()
                    mm(pb, pb.a, ones, ones.a, sq, sq.a[:, 0:TT])
                    rt = t512.get()
                    k.A([pb, epsc], [rt], lambda e: e.activation(out=rt.a[:, 0:TT], in_=pb.a, func=AF.Sqrt, bias=epsc.a[:, 0:1], scale=1.0 / 128))
                    k.V([rt], [rt], lambda e: e.reciprocal(out=rt.a[:, 0:TT], in_=rt.a[:, 0:TT]))
                    k.V([o_g, rt], [rt], lambda e: e.tensor_tensor(out=rt.a[:, 0:TT], in0=o_g.a[:, hh, :], in1=rt.a[:, 0:TT], op=ALU.mult))
                    k.V([rt, gdnnw, big], [yT], lambda e: e.scalar_tensor_tensor(out=yT.a[:, hh, :], in0=rt.a[:, 0:TT], scalar=gdnnw.a[:, 0:1], in1=slot(hh * 4 + 3), op0=ALU.mult, op1=ALU.mult))
                for g2 in range(2):
                    ys = []
                    pb = pbig.get()
                    for bi in range(2):
                        blk = g2 * 2 + bi
                        y0 = t512.get()
                        k.V([big, dcol, o_s], [y0], lambda e: e.scalar_tensor_tensor(out=y0.a[:, 0:TT], in0=slot(16 + blk), scalar=dcol.a[:, blk:blk + 1], in1=o_s.a[:, blk, :], op0=ALU.mult, op1=ALU.add))
                        k.V([y0, zst], [y0], lambda e: e.tensor_tensor(out=y0.a[:, 0:TT], in0=y0.a[:, 0:TT], in1=zst.a[:, blk, :], op=ALU.mult))
                        sq = t512.get()
                        k.A([y0], [sq], lambda e: e.activation(out=sq.a[:, 0:TT], in_=y0.a[:, 0:TT], func=AF.Square))
                        mm(pb, pb.a, ones, ones.a, sq, sq.a[:, 0:TT], start=(bi == 0), stop=(bi == 1))
                        ys.append(y0)
                    rt = t512.get()
                    k.A([pb, epsc], [rt], lambda e: e.activation(out=rt.a[:, 0:TT], in_=pb.a, func=AF.Sqrt, bias=epsc.a[:, 0:1], scale=1.0 / 256))
                    k.V([rt], [rt], lambda e: e.reciprocal(out=rt.a[:, 0:TT], in_=rt.a[:, 0:TT]))
                    for bi in range(2):
                        blk = g2 * 2 + bi
                        y0 = ys[bi]
                        k.V([y0, ssmnw, rt], [yT], lambda e: e.scalar_tensor_tensor(out=yT.a[:, 4 + blk, :], in0=y0.a[:, 0:TT], scalar=ssmnw.a[:, blk:blk + 1], in1=rt.a[:, 0:TT], op0=ALU.mult, op1=ALU.mult))
                if t == 0:
                    dump("o_g", o_g, o_g.a, [128, 4, TT]); dump("o_s", o_s, o_s.a, [128, 4, TT])
                k.dma("sp", ybuf, ybuf.a[t].rearrange("(b p) c -> p b c", p=128), yT, yT.a)
                if dbg:
                    k.dma("sp", ydbg, ydbg.a[t].rearrange("(b p) c -> p b c", p=128), yT, yT.a)
                k.collective(ybuf, gath, ybuf.a[t], gath.a[t * 4096:(t + 1) * 4096, :], cfg["groups"], gath)
            emit_conv(len(conv_jobs))
            zt = t512.get()
            k.V([], [zt], lambda e: e.memset(zt.a, 0.0))
            zb16 = zt.a.bitcast(BF16)[:, 0:TT]
            for i in range(32):
                k.dma("sp", gath, gath.a[NT * 4096 + i * 128: NT * 4096 + (i + 1) * 128, :], zt, zb16, semtl=zt)
            k.barrier()
        k.es = es

        es3 = contextlib.ExitStack()
        if cfg.get("stop_after", 3) == 1:
            k.finish([ydbg, gath] + dumped)
            k.barrier()
            return nc
        with es3:
            k.es = es3
            rowb = k.sb("rowb", [128, D], F32)
            hb3 = k.sb("hb3", [128, D], BF16)
            halo2 = k.sb("halo2", [128, FB, 2], F32)
            piece = RR([k.sb("piece%d" % i, [128, 512], F32) for i in range(4)])
            ssq3 = k.sb("ssqf", [128, 4, NG], F32)
            actT = big.a.rearrange("p (f c) -> p f c", c=TT)
            y_sb = big.a[:, 0:KM * 640].rearrange("p (k c) -> p k c", k=KM)
            accs = [pbig_l[0], pbig_l[1], pacc[0], pacc[1]]
            pbB = RR([pbig_l[0], pbig_l[1], pacc[0], pacc[1]])
            WSL = WBE // 512
            for m in range(NT3):
                g0 = 0 if m == 0 else 1
                for kd in range(KM):
                    if m == 0:
                        k.dma("pool", big, y_sb[:, kd, 0:128], gath, gath.a.rearrange("r (q c) -> (r q) c", c=128), kind="ind",
                              in_off=bass.IndirectOffsetOnAxis(ap=idx.a[:, kd:kd + 1], axis=0))
                    k.dma("pool", big, y_sb[:, kd, 128:640], gath, gath.a[:, :], kind="ind",
                          in_off=bass.IndirectOffsetOnAxis(ap=idx.a[:, (m + 1) * KM + kd:(m + 1) * KM + kd + 1], axis=0))
                for g in range(NG):
                    for hf in range(2):
                        wb = wblk.get()
                        wv = wb.a.rearrange("p (k c) -> p k c", k=KM)
                        k.dma("sp", wb, wb.a[:, 0:KM * 256], wos, wos.a[g * 2 + hf])
                        for gi in range(g0, 5):
                            pb = pbig.get()
                            for kd in range(KM):
                                k.P([big, wb], [pb], lambda e: e.matmul(pb.a[:, 0:256], lhsT=y_sb[:, kd, gi * 128:(gi + 1) * 128], rhs=wv[:, kd, :], start=(kd == 0), stop=(kd == KM - 1)))
                            xp = piece.get()
                            r0 = (0 if gi == 0 else 128 + m * TT + (gi - 1) * 128)
                            c0 = g * 512 + hf * 256
                            k.dma("sp", xp, xp.a[:, 0:256], None, xq_in[r0:r0 + 128, c0:c0 + 256])
                            k.V([pb, xp], [xp], lambda e: e.tensor_tensor(out=xp.a[:, 0:256], in0=pb.a[:, 0:256], in1=xp.a[:, 0:256], op=ALU.add))
                            k.dma("sp", hbuf, hbuf.a[r0:r0 + 128, c0:c0 + 256], xp, xp.a[:, 0:256])
                            if dbg:
                                k.dma("sp", hdbg, hdbg.a[r0:r0 + 128, c0:c0 + 256], xp, xp.a[:, 0:256])
                for gi in range(g0, 5):
                    r0 = (0 if gi == 0 else 128 + m * TT + (gi - 1) * 128)
                    k.dma("sp", xrow, xrow.a, hbuf, hbuf.a[r0:r0 + 128, :])
                    if gi == 0:
                        norm_transpose(xrow, hb3, hb3.a, 0, n2c, hnT, ncols=2, srccol=126)
                    else:
                        norm_transpose(xrow, hb3, hb3.a, 2 + (gi - 1) * 128, n2c, hnT)
                k.V([], [ssq3], lambda e: e.memset(ssq3.a, 0.0))
                for hf in range(2):
                    for fl in range(FH):
                        fb = hf * FH + fl
                        wb = wblk.get()
                        wv = wb.a[:, 0:2 * KD * 128].rearrange("p (u k c) -> p u k c", u=2, k=KD)
                        k.dma("sp", wb, wb.a[:, 0:2 * KD * 128], wgs, wgs.a[fb])
                        pg = pbB.get()
                        for kd in range(KD):
                            k.P([wb, hnT], [pg], lambda e: e.matmul(pg.a, lhsT=wv[:, 0, kd, :], rhs=hnT.a[:, kd, 2:2 + TT], start=(kd == 0), stop=(kd == KD - 1)))
                        if m == 0:
                            ph = pq3.get()
                            for kd in range(KD):
                                k.P([wb, hnT], [ph], lambda e: e.matmul(ph.a[:, 0:2], lhsT=wv[:, 0, kd, :], rhs=hnT.a[:, kd, 0:2], start=(kd == 0), stop=(kd == KD - 1)))
                            k.A([ph], [halo2], lambda e: e.activation(out=halo2.a[:, fb, :], in_=ph.a[:, 0:2], func=AF.Copy))
                        pu = pbB.get()
                        for kd in range(KD):
                            k.P([wb, hnT], [pu], lambda e: e.matmul(pu.a, lhsT=wv[:, 1, kd, :], rhs=hnT.a[:, kd, 2:2 + TT], start=(kd == 0), stop=(kd == KD - 1)))
                        sg = t512.get()
                        conv_silu(pg, fb, sg, sg.a[:, 0:TT], 3, halo2, fcw, fcb)
                        k.V([sg, pu], [big], lambda e: e.tensor_tensor(out=actT[:, fl, :], in0=sg.a[:, 0:TT], in1=pu.a, op=ALU.mult))
                    for g in range(NG):
                        nsl = (FH + WSL - 1) // WSL
                        for sl in range(nsl):
                            f0 = sl * WSL
                            nf = min(WSL, FH - f0)
                            wb = wblk.get()
                            wv = wb.a.rearrange("p (f c) -> p f c", c=512)
                            r0 = (hf * FH + f0) * 128
                            si = (hf * NG + g) * NSL + sl
                            k.dma("sp", wb, wb.a[:, 0:nf * 512], wds, wds.a[si][:, 0:nf * 512])
                            for i in range(4):
                                for f in range(nf):
                                    k.P([big, wb], [accs[i]], lambda e: e.matmul(accs[i].a, lhsT=actT[:, f0 + f, i * 128:(i + 1) * 128], rhs=wv[:, f, :], start=(f0 + f == 0), stop=(f0 + f == FH - 1)))
                        for i in range(4):
                            r0 = m * TT + i * 128
                            hp = piece.get()
                            if hf == 0:
                                k.dma("sp", hp, hp.a, hbuf, hbuf.a[128 + r0:128 + r0 + 128, g * 512:(g + 1) * 512])
                            else:
                                k.dma("sp", hp, hp.a, out_d, out_d.a[r0:r0 + 128, g * 512:(g + 1) * 512])
                            k.V([accs[i], hp], [hp], lambda e: e.tensor_tensor(out=hp.a, in0=accs[i].a, in1=hp.a, op=ALU.add))
                            if hf == 1:
                                jk = t512.get()
                                k.A([hp, ssq3], [jk, ssq3], lambda e: e.activation(out=jk.a[:, 0:512], in_=hp.a, func=AF.Square, accum_out=ssq3.a[:, i, g:g + 1]))
                            k.dma("sp", out_d, out_d.a[r0:r0 + 128, g * 512:(g + 1) * 512], hp, hp.a)
                k.dma("sp", xrow, xrow.a, None, nfb_in.partition_broadcast(128))
                for i in range(4):
                    r0 = m * TT + i * 128
                    tot = ssq.get(); rs = rstd.get()
                    k.V([ssq3], [tot], lambda e: e.tensor_reduce(out=tot.a, in_=ssq3.a[:, i, :], axis=mybir.AxisListType.X, op=ALU.add))
                    k.A([tot, epsc], [rs], lambda e: e.activation(out=rs.a, in_=tot.a, func=AF.Sqrt, bias=epsc.a[:, 0:1], scale=1.0 / D))
                    k.V([rs], [rs], lambda e: e.reciprocal(out=rs.a, in_=rs.a))
                    k.dma("sp", rowb, rowb.a, out_d, out_d.a[r0:r0 + 128, :])
                    k.V([rowb, rs, xrow], [rowb], lambda e: e.scalar_tensor_tensor(out=rowb.a, in0=rowb.a, scalar=rs.a[:, 0:1], in1=xrow.a, op0=ALU.mult, op1=ALU.mult))
                    k.dma("sp", out_d, out_d.a[r0:r0 + 128, :], rowb, rowb.a)
            fin = [out_d]
            if dbg:
                fin += [ydbg, hdbg]
            k.finish(fin)
            k.barrier()
        k.es = es
    return nc


Q0, K0, V0, GATE0, BETA0, ALPHA0, Z0, XBC0, DT0 = 0, 2048, 4096, 6144, 8192, 8208, 8224, 10272, 14368
GROUPS = [[0, 1, 2, 3], [4, 5, 6, 7]]


def _masks():
    p = np.arange(128)[:, None]
    f = np.arange(128)[None, :]
    ident = (p == f).astype(np.float32)
    triU = (p <= f).astype(np.float32)
    negU = np.where(f >= p, 0.0, NEG).astype(np.float32)
    posL = np.where(p > f, 0.0, -NEG).astype(np.float32)
    return np.ascontiguousarray(np.stack([ident, triU, negU, posL]))


def prep_inputs(inp, cfg):
    L, D, DFF = cfg["L"], cfg["D"], cfg["DFF"]
    KD, FB, NG = D // 128, DFF // 128, D // 512
    LQ = L // 4
    NT, NT3 = L // TT, LQ // TT
    f32 = np.float32
    x = np.asarray(inp["x"], f32)
    w_in = np.asarray(inp["w_in"], f32)[0]
    w_out = np.asarray(inp["w_out"], f32)[0]
    gcw = np.asarray(inp["gdn_conv_w"], f32)[0]
    scw = np.asarray(inp["ssm_conv_w"], f32)[0]
    scb = np.asarray(inp["ssm_conv_b"], f32)[0]
    ar = np.arange(128)
    masks = _masks()
    wgate = np.asarray(inp["ffn_w_gate"], f32)[0].reshape(KD, 128, FB, 128).transpose(2, 1, 0, 3)
    wup = np.asarray(inp["ffn_w_up"], f32)[0].reshape(KD, 128, FB, 128).transpose(2, 1, 0, 3)
    wg = np.ascontiguousarray(np.stack([wgate, wup], axis=2))
    wd = np.ascontiguousarray(np.asarray(inp["ffn_w_down"], f32)[0])
    fcw = np.ascontiguousarray(np.asarray(inp["ffn_conv_w"], f32)[0].reshape(3, FB, 128).transpose(2, 1, 0))
    fcb = np.ascontiguousarray(np.asarray(inp["ffn_conv_b"], f32)[0].reshape(FB, 128).T)
    n1c = np.ascontiguousarray(np.asarray(inp["norm1_w"], f32)[0].reshape(KD, 128).T)
    n2c = np.ascontiguousarray(np.asarray(inp["norm2_w"], f32)[0].reshape(KD, 128).T)
    nfb = np.ascontiguousarray(np.asarray(inp["norm_f_w"], f32))
    gdnnw = np.ascontiguousarray(np.asarray(inp["gdn_norm_w"], f32)[0].reshape(128, 1))
    perm = []
    for r in range(4):
        for hh in range(4):
            perm.append(128 * (4 * r + hh) + ar)
        perm.append(2048 + 512 * r + np.arange(512))
    perm = np.concatenate(perm)
    wo = np.ascontiguousarray(w_out[perm].reshape(KM, 128, NG, 512).transpose(2, 1, 0, 3))
    maps = []
    for c in range(8):
        b, j = c // 4, c % 4
        cols = []
        for hh in range(4):
            H = 4 * j + hh
            cols += [Q0 + 128 * H + ar, K0 + 128 * H + ar, V0 + 128 * H + ar, GATE0 + 128 * H + ar]
        for zb in range(4):
            cols.append(Z0 + 512 * j + zb * 128 + ar)
        for blk in range(4):
            cols.append(XBC0 + 512 * j + blk * 128 + ar)
        for g in range(2):
            cols.append(XBC0 + 2048 + 128 * (2 * j + g) + ar)
        for g in range(2):
            cols.append(XBC0 + 3072 + 128 * (2 * j + g) + ar)
        cols = np.concatenate(cols)
        w1 = np.ascontiguousarray(w_in[:, cols].reshape(KD, 128, 28, 128).transpose(2, 1, 0, 3))
        small = np.concatenate([BETA0 + 4 * j + np.arange(4), ALPHA0 + 4 * j + np.arange(4), DT0 + 8 * j + np.arange(8)])
        wsm = np.ascontiguousarray(w_in[:, small].reshape(KD, 128, 16).transpose(1, 0, 2))
        cw = np.zeros((128, 20, 4), f32)
        cb = np.zeros((128, 20), f32)
        for hh in range(4):
            H = 4 * j + hh
            for r in range(3):
                cw[:, hh * 3 + r, :] = gcw[:, r * 2048 + 128 * H + ar].T
        for i in range(8):
            if i < 4:
                ch = 512 * j + i * 128 + ar
            elif i < 6:
                ch = 2048 + 128 * (2 * j + (i - 4)) + ar
            else:
                ch = 3072 + 128 * (2 * j + (i - 6)) + ar
            cw[:, 12 + i, :] = scw[:, ch].T
            cb[:, 12 + i] = scb[ch]
        dtb = np.concatenate([np.asarray(inp["gdn_dt_bias"], f32)[0][4 * j:4 * j + 4], np.asarray(inp["ssm_dt_bias"], f32)[0][8 * j:8 * j + 8]])
        alog = np.concatenate([np.asarray(inp["gdn_A_log"], f32)[0][4 * j:4 * j + 4], np.asarray(inp["ssm_A_log"], f32)[0][8 * j:8 * j + 8]])
        ssd_D = np.asarray(inp["ssm_D"], f32)[0]
        dcol = np.zeros((128, 4), f32)
        for blk in range(4):
            dcol[:64, blk] = ssd_D[8 * j + 2 * blk]
            dcol[64:, blk] = ssd_D[8 * j + 2 * blk + 1]
        ssmnw = np.ascontiguousarray(np.asarray(inp["ssm_norm_w"], f32)[0][512 * j:512 * (j + 1)].reshape(4, 128).T)
        xq = np.zeros((128 + LQ, D), f32)
        if j > 0:
            xq[:] = x[b, j * LQ - 128:(j + 1) * LQ]
        else:
            xq[128:] = x[b, 0:LQ]
        idx = np.zeros((128, (NT3 + 1) * KM), np.int32)
        for blk in range(NT3 + 1):
            if blk == 0:
                tile = j * NT3 - 1 if j > 0 else NT
            else:
                tile = j * NT3 + (blk - 1)
            for kd in range(KM):
                idx[:, blk * KM + kd] = tile * 4096 + kd * 128 + ar
                if blk == 0:
                    idx[:, kd] = idx[:, kd] * 4 + 3
        maps.append({
            "x": np.ascontiguousarray(x[b]), "xq": xq, "w1": w1, "wsm": wsm, "n1c": n1c, "n2c": n2c, "nfb": nfb,
            "cw": cw, "cb": cb, "dtb": np.ascontiguousarray(np.tile(dtb[None, :], (128, 1))),
            "alog": np.ascontiguousarray(np.tile(alog[None, :], (128, 1))), "dcol": dcol, "ssmnw": ssmnw, "gdnnw": gdnnw,
            "masks": masks, "wo": wo, "wg": wg, "wd": wd, "fcw": fcw, "fcb": fcb, "idx": idx,
        })
    return maps


def run(inp, cfg):
    nc = build_program(cfg)
    maps = prep_inputs(inp, cfg)
    res = run_bass_kernel_spmd(nc, maps, core_ids=list(range(8)))
    return res


def kernel(**inputs):
    cfg = {"L": 8192, "D": 4096, "DFF": 11008, "groups": GROUPS}
    res = run(inputs, cfg)
    L, D = cfg["L"], cfg["D"]
    LQ = L // 4
    out = np.zeros((2, L, D), np.float32)
    for c in range(8):
        b, j = c // 4, c % 4
        out[b, j * LQ:(j + 1) * LQ] = res.results[c]["out"]
    return out
```

```python
import bisect
import contextlib
import numpy as np
import concourse.bass as bass
import concourse.mybir as mybir
from concourse.bass_utils import run_bass_kernel_spmd

F32 = mybir.dt.float32
BF16 = mybir.dt.bfloat16
I32 = mybir.dt.int32
AF = mybir.ActivationFunctionType
ALU = mybir.AluOpType
EPS = 1e-6
NEG = -30000.0


class Tl:
    __slots__ = ("a", "name", "w", "r", "dsem", "dn", "multi")

    def __init__(self, a, name, multi=False):
        self.a = a
        self.name = name
        self.w = [] if multi else None
        self.r = {}
        self.dsem = {}
        self.dn = {}
        self.multi = multi


class TlView:
    def __init__(self, parent, a):
        object.__setattr__(self, "p", parent)
        object.__setattr__(self, "a", a)

    def __getattr__(self, n):
        return getattr(object.__getattribute__(self, "p"), n)

    def __setattr__(self, n, v):
        setattr(object.__getattribute__(self, "p"), n, v)


class KB:
    ENG = ("pe", "dve", "act", "pool", "sp")

    def __init__(self, nc, es):
        self.nc = nc
        self.es = es
        self.es_sem = es
        self.dma_tiles = []
        self.q = {"pe": nc.tensor, "dve": nc.vector, "act": nc.scalar, "pool": nc.gpsimd, "sp": nc.sync}
        self.sem = {e: es.enter_context(nc.semaphore("s_" + e)) for e in self.ENG}
        self.emitted = {e: 0 for e in self.ENG}
        self.last = {e: None for e in self.ENG}
        self.sigs = {e: [] for e in self.ENG}
        self.seen = {e: {} for e in self.ENG}
        self.nsem = 0
        self.uid = 0

    def sb(self, name, shape, dt):
        t = self.es.enter_context(self.nc.sbuf_tensor("sb_" + name, list(shape), dt))
        return Tl(t[:], name)

    def ps(self, name, shape, dt):
        t = self.es.enter_context(self.nc.psum_tensor("ps_" + name, list(shape), dt))
        return Tl(t[:], name)

    def dram(self, name, shape, dt, kind="Internal", multi=False):
        t = self.nc.dram_tensor(name, list(shape), dt, kind=kind)
        return Tl(t.ap(), name, multi=multi)

    def view(self, ap, name):
        return Tl(ap, name)

    def newsem(self, name):
        self.nsem += 1
        return self.es_sem.enter_context(self.nc.semaphore(name))

    def _setw(self, t, dep):
        if t.multi:
            t.w = [d for d in t.w if d[1] is not dep[1]] + [dep]
        else:
            t.w = dep
        t.r = {}

    def barrier(self):
        engs = ["pe", "dve", "act"]
        for e in engs:
            if self.last[e] is not None and (not self.sigs[e] or self.sigs[e][-1] < self.emitted[e]):
                self.last[e].then_inc(self.sem[e], 1)
                self.sigs[e].append(self.emitted[e])
        for e in self.ENG:
            for o in engs:
                if o != e and self.sigs[o]:
                    self._wait_sem(e, self.sem[o], len(self.sigs[o]))
            for tl in self.dma_tiles:
                for qc in tl.dsem:
                    self._wait_sem(e, tl.dsem[qc], tl.dn[qc])

    def _wait_sem(self, eng, sem, val):
        d = self.seen[eng]
        if d.get(sem.name, 0) >= val:
            return
        d[sem.name] = val
        self.q[eng].wait_ge(sem, val)

    def _need(self, eng, dep):
        if dep is None:
            return
        if isinstance(dep, list):
            for d in dep:
                self._need(eng, d)
            return
        if dep[0] == "e":
            _, e, idx = dep
            sg = self.sigs[e]
            if not sg or sg[-1] < idx:
                self.last[e].then_inc(self.sem[e], 1)
                sg.append(self.emitted[e])
            pos = bisect.bisect_left(sg, idx)
            self._wait_sem(eng, self.sem[e], pos + 1)
        else:
            tl = dep[1]
            for qc in tl.dsem:
                self._wait_sem(eng, tl.dsem[qc], tl.dn[qc])

    def _deps(self, eng, reads, writes, is_dma=False):
        for t in reads:
            w = t.w
            if w is not None:
                if not isinstance(w, list) and w[0] == "e" and w[1] == eng and eng == "pe":
                    continue
                self._need(eng, w)
        for t in writes:
            w = t.w
            if w is not None and (isinstance(w, list) or not (w[0] == "e" and w[1] == eng and eng == "pe")):
                self._need(eng, w)
            for key, dep in t.r.items():
                if dep[0] == "e" and dep[1] == eng and eng == "pe":
                    continue
                self._need(eng, dep)

    def op(self, eng, reads, writes, fn):
        self._deps(eng, reads, writes)
        ins = fn(self.q[eng])
        self.emitted[eng] += 1
        idx = self.emitted[eng]
        self.last[eng] = ins
        for t in reads:
            t.r[eng] = ("e", eng, idx)
        for t in writes:
            self._setw(t, ("e", eng, idx))
        return ins

    def V(self, reads, writes, fn):
        return self.op("dve", reads, writes, fn)

    def A(self, reads, writes, fn):
        return self.op("act", reads, writes, fn)

    def P(self, reads, writes, fn):
        return self.op("pe", reads, writes, fn)

    def dma(self, qe, dst, dst_ap, src, src_ap, semtl=None, kind="dma", in_off=None):
        reads = [src] if src is not None else []
        writes = [dst] if dst is not None else []
        self._deps(qe, reads, writes, is_dma=True)
        if semtl is None:
            semtl = dst if dst is not None else src
        qc = "sw" if qe == "pool" else "hw"
        if qc not in semtl.dsem:
            if not semtl.dsem:
                self.dma_tiles.append(semtl)
            semtl.dsem[qc] = self.newsem("d%s_%s" % (qc, semtl.name))
            semtl.dn[qc] = 0
        e = self.q[qe]
        if kind == "dma":
            ins = e.dma_start(out=dst_ap, in_=src_ap)
            inc = 16
        elif kind == "ind":
            ins = e.indirect_dma_start(out=dst_ap, out_offset=None, in_=src_ap, in_offset=in_off)
            inc = 16
        else:
            raise ValueError(kind)
        semtl.dn[qc] += inc
        ins.then_inc(semtl.dsem[qc], inc)
        self.emitted[qe] += 1
        self.last[qe] = None
        dep = ("d", semtl)
        if src is not None:
            src.r["d_" + semtl.name] = dep
        if dst is not None:
            self._setw(dst, dep)
        return ins

    def collective(self, src, dst, src_ap, dst_ap, groups, cctl):
        self._deps("pool", [src], [dst], is_dma=True)
        if "cc" not in cctl.dsem:
            if not cctl.dsem:
                self.dma_tiles.append(cctl)
            cctl.dsem["cc"] = self.newsem("dcc_" + cctl.name)
            cctl.dn["cc"] = 0
        ins = self.q["pool"].collective_compute("AllGather", ALU.bypass, replica_groups=groups, ins=[src_ap], outs=[dst_ap])
        cctl.dn["cc"] += 1
        ins.then_inc(cctl.dsem["cc"], 1)
        self.emitted["pool"] += 1
        self.last["pool"] = None
        dep = ("d", cctl)
        src.r["d_" + cctl.name] = dep
        self._setw(dst, dep)

    def finish(self, tls):
        for t in tls:
            if t.w is not None:
                self._need("sp", t.w)


class RR:
    def __init__(self, items):
        self.items = items
        self.i = 0

    def get(self):
        x = self.items[self.i % len(self.items)]
        self.i += 1
        return x


KM = 32
TT = 512
WBE = 8192


LOCK = [True]


def lockstep(gens):
    gens = list(gens)
    if not LOCK[0]:
        for g in gens:
            for _ in g:
                pass
        return
    while gens:
        nxt = []
        for g in gens:
            try:
                next(g)
                nxt.append(g)
            except StopIteration:
                pass
        gens = nxt


def build_program(cfg):
    L, D, DFF = cfg["L"], cfg["D"], cfg["DFF"]
    dbg = cfg.get("debug", False)
    LOCK[0] = cfg.get("lock", True)
    KD = D // 128
    NT = L // TT
    LQ = L // 4
    NT3 = LQ // TT
    FB = DFF // 128
    FH = FB // 2
    NG = D // 512
    assert FB % 2 == 0 and LQ % TT == 0 and D % 512 == 0 and FH <= 44 and 2 * KD * 128 <= WBE

    nc = bass.Bass("TRN2", target_bir_lowering=False)
    es = contextlib.ExitStack()
    with es:
        k = KB(nc, es)

        def din(name, shape, dt=F32):
            return nc.dram_tensor(name, list(shape), dt, kind="ExternalInput").ap()

        dump_on = cfg.get("dump", False)
        dumped = []

        def dump(name, tl, ap, shape, dt=F32):
            if not dump_on:
                return
            d = k.dram("D_" + name, list(shape), dt, kind="ExternalOutput")
            k.dma("sp", d, d.a, tl, ap)
            dumped.append(d)

        x_in = din("x", [L, D])
        xq_in = din("xq", [128 + LQ, D])
        w1_in = din("w1", [28, 128, KD, 128])
        wsm_in = din("wsm", [128, KD, 16])
        n1c_in = din("n1c", [128, KD])
        n2c_in = din("n2c", [128, KD])
        nfb_in = din("nfb", [D])
        cw_in = din("cw", [128, 20, 4])
        cb_in = din("cb", [128, 20])
        dtb_in = din("dtb", [128, 12])
        alog_in = din("alog", [128, 12])
        dcol_in = din("dcol", [128, 4])
        ssmnw_in = din("ssmnw", [128, 4])
        gdnnw_in = din("gdnnw", [128, 1])
        masks_in = din("masks", [4, 128, 128])
        wo_in = din("wo", [NG, 128, KM, 512])
        wg_in = din("wg", [FB, 128, 2, KD, 128])
        wd_in = din("wd", [DFF, D])
        fcw_in = din("fcw", [128, FB, 3])
        fcb_in = din("fcb", [128, FB])
        idx_in = din("idx", [128, (NT3 + 1) * KM], I32)
        out_d = k.dram("out", [LQ, D], F32, kind="ExternalOutput")
        ybuf = k.dram("ybuf", [NT, 1024, TT], BF16)
        gath = k.dram("gath", [(NT + 1) * 4096, TT], BF16, multi=True)
        hbuf = k.dram("hbuf", [128 + LQ, D], F32)
        WSLc = WBE // 512
        NSL = (FH + WSLc - 1) // WSLc
        w1s = k.dram("w1s", [28, 128, KD * 128], BF16)
        wos = k.dram("wos", [NG * 2, 128, KM * 256], BF16)
        wgs = k.dram("wgs", [FB, 128, 2 * KD * 128], BF16)
        wds = k.dram("wds", [2 * NG * NSL, 128, WBE], BF16)
        if dbg:
            ydbg = k.dram("ydbg", [NT, 1024, TT], BF16, kind="ExternalOutput")
            hdbg = k.dram("hdbg", [128 + LQ, D], F32, kind="ExternalOutput")

        cst = {"sp": Tl(None, "cst_sp"), "pool": Tl(None, "cst_pool")}

        HB = WBE // 2
        conv_jobs = []
        for g in range(NG):
            for hf in range(2):
                for q2 in range(2):
                    ks = slice(q2 * (KM // 2), (q2 + 1) * (KM // 2))
                    conv_jobs.append((wos, wos.a[g * 2 + hf].rearrange("p (k c) -> p k c", k=KM)[:, ks, :],
                                      wo_in[g][:, ks, hf * 256:(hf + 1) * 256], (KM // 2, 256)))
        for fb in range(FB):
            for u in range(2):
                conv_jobs.append((wgs, wgs.a[fb].rearrange("p (u k c) -> p u k c", u=2, k=KD)[:, u], wg_in[fb][:, u], (KD, 128)))
        for hf in range(2):
            for g in range(NG):
                for sl in range(NSL):
                    f0 = sl * WSLc
                    nf = min(WSLc, FH - f0)
                    si = (hf * NG + g) * NSL + sl
                    for q2 in range(2):
                        fa = q2 * (WSLc // 2)
                        fn_ = min(WSLc // 2, nf - fa)
                        if fn_ <= 0:
                            continue
                        r0 = (hf * FH + f0 + fa) * 128
                        conv_jobs.append((wds, wds.a[si][:, fa * 512:(fa + fn_) * 512].rearrange("p (f c) -> p f c", c=512),
                                          wd_in[r0:r0 + fn_ * 128, g * 512:(g + 1) * 512].rearrange("(f p) c -> p f c", p=128), (fn_, 512)))
        conv_pos = [0]
        bnc = []

        def emit_conv(n):
            for _ in range(n):
                if conv_pos[0] < len(conv_jobs):
                    d, dap, sap, (a_, b_) = conv_jobs[conv_pos[0]]
                    bt = bnc[conv_pos[0] % 2]
                    bv = bt.a[:, 0:a_ * b_].rearrange("p (a b) -> p a b", a=a_)
                    k.dma("pool", bt, bv, None, sap)
                    k.dma("pool", d, dap, bt, bv)
                    conv_pos[0] += 1

        def cload(name, shape, src_ap, dt=F32, q="sp"):
            t = k.sb(name, shape, dt)
            k.dma(q, t, t.a, None, src_ap, semtl=cst[q])
            return t

        ident = cload("ident", [128, 128], masks_in[0])
        triU = cload("triU", [128, 128], masks_in[1])
        negU = cload("negU", [128, 128], masks_in[2])
        posL = cload("posL", [128, 128], masks_in[3])
        n1c = cload("n1c", [128, KD], n1c_in)
        n2c = cload("n2c", [128, KD], n2c_in)
        cw = cload("cw", [128, 20, 4], cw_in)
        cbias = cload("cbias", [128, 20], cb_in)
        dtb = cload("dtb", [128, 12], dtb_in)
        alog = cload("alog", [128, 12], alog_in)
        dcol = cload("dcol", [128, 4], dcol_in)
        ssmnw = cload("ssmnw", [128, 4], ssmnw_in)
        gdnnw = cload("gdnnw", [128, 1], gdnnw_in)
        fcw = cload("fcw", [128, FB, 3], fcw_in)
        fcb = cload("fcb", [128, FB], fcb_in)
        idx = cload("idx", [128, (NT3 + 1) * KM], idx_in, dt=I32, q="pool")
        wsm = cload("wsm", [128, KD, 16], wsm_in, dt=BF16, q="pool")
        identb = k.sb("identb", [128, 128], BF16)
        ones = k.sb("ones", [128, 128], F32)
        epsc = k.sb("epsc", [128, 1], F32)
        eps128 = k.sb("eps128", [128, 1], F32)
        onec = k.sb("onec", [128, 1], F32)
        nA = k.sb("nA", [128, 12], F32)
        k.V([ident], [identb], lambda e: e.tensor_copy(out=identb.a, in_=ident.a))
        k.V([], [ones], lambda e: e.memset(ones.a, 1.0))
        k.V([], [epsc], lambda e: e.memset(epsc.a, EPS))
        k.V([], [eps128], lambda e: e.memset(eps128.a, EPS * 128.0))
        k.V([], [onec], lambda e: e.memset(onec.a, 1.0))
        k.A([alog], [nA], lambda e: e.activation(out=nA.a, in_=alog.a, func=AF.Exp))
        k.V([nA], [nA], lambda e: e.tensor_scalar(out=nA.a, in0=nA.a, scalar1=-1.0, scalar2=None, op0=ALU.mult))

        pbig_l = [k.ps("pb%d" % i, [128, 512], F32) for i in range(2)]
        pbig = RR(pbig_l)
        ptr = RR([k.ps("ptr%d" % i, [128, 1024], BF16) for i in range(2)])
        pqb = [k.ps("pqb%d" % i, [128, 512], F32) for i in range(4)]
        pq = RR([TlView(pqb[i], pqb[i].a[:, qd * 128:(qd + 1) * 128]) for qd in range(4) for i in range(4)])
        pq3 = RR([TlView(pqb[i], pqb[i].a[:, qd * 128:(qd + 1) * 128]) for qd in range(4) for i in range(2)])
        pacc = [pqb[2], pqb[3]]

        xrow = k.sb("xrow", [128, D], F32)
        hnT = k.sb("hnT", [128, KD, TT + 2], BF16)
        wblk_l = [k.sb("wblk%d" % i, [128, WBE], BF16) for i in range(2)]
        wblk = RR(wblk_l)
        wlo = RR([Tl(wblk_l[i].a[:, 0:WBE // 2], "wlo%d" % i) for i in range(2)])
        bnc.extend([Tl(wblk_l[i].a[:, WBE // 2:WBE], "bnc%d" % i) for i in range(2)])
        ssq = RR([k.sb("ssq%d" % i, [128, 1], F32) for i in range(4)])
        rstd = RR([k.sb("rstd%d" % i, [128, 1], F32) for i in range(4)])
        big = k.sb("big", [128, 22528], BF16)
        t512 = RR([k.sb("t512_%d" % i, [128, 516], F32) for i in range(5)])

        def norm_transpose(xr, hb, hba, col0, nwc, tgt, ncols=128, srccol=0):
            sq = ssq.get()
            rs = rstd.get()
            k.V([], [sq], lambda e: e.memset(sq.a, 0.0))
            k.A([xr, sq], [hb, sq], lambda e: e.activation(out=hba, in_=xr.a, func=AF.Square, accum_out=sq.a))
            k.A([sq, epsc], [rs], lambda e: e.activation(out=rs.a, in_=sq.a, func=AF.Sqrt, bias=epsc.a[:, 0:1], scale=1.0 / D))
            k.V([rs], [rs], lambda e: e.reciprocal(out=rs.a, in_=rs.a))
            k.A([xr, rs], [hb], lambda e: e.activation(out=hba, in_=xr.a, func=AF.Copy, scale=rs.a[:, 0:1]))
            for kg in range(KD // 4):
                pt = ptr.get()
                for qd in range(4):
                    kd = kg * 4 + qd
                    k.P([hb, identb], [pt], lambda e: e.transpose(pt.a[:, qd * 128:(qd + 1) * 128], hba[:, kd * 128:(kd + 1) * 128], identb.a))
                src = pt.a[:, 0:512].rearrange("p (q c) -> p q c", q=4)[:, :, srccol:srccol + ncols]
                dstv = tgt.a[:, kg * 4:kg * 4 + 4, col0:col0 + ncols]
                sc = nwc.a[:, kg * 4:kg * 4 + 4].unsqueeze(2).to_broadcast([128, 4, ncols])
                k.V([pt, nwc], [tgt], lambda e: e.tensor_tensor(out=dstv, in0=src, in1=sc, op=ALU.mult))

        def conv_silu(pb, ci, dst_tl, dst_ap, K, hal, cwt, cbt):
            pre = t512.get()
            acc = t512.get()
            H = K - 1
            W = TT
            k.A([hal], [pre], lambda e: e.activation(out=pre.a[:, 0:H], in_=hal.a[:, ci, 0:H], func=AF.Copy))
            k.A([pb], [pre], lambda e: e.activation(out=pre.a[:, H:H + W], in_=pb.a[:, 0:W], func=AF.Copy))
            k.V([pre, cwt, cbt], [acc], lambda e: e.tensor_scalar(out=acc.a[:, 0:W], in0=pre.a[:, H:H + W], scalar1=cwt.a[:, ci, K - 1:K], scalar2=cbt.a[:, ci:ci + 1], op0=ALU.mult, op1=ALU.add))
            for j in range(K - 1):
                k.V([pre, cwt, acc], [acc], lambda e: e.scalar_tensor_tensor(out=acc.a[:, 0:W], in0=pre.a[:, j:j + W], scalar=cwt.a[:, ci, j:j + 1], in1=acc.a[:, 0:W], op0=ALU.mult, op1=ALU.add))
            k.A([pre], [hal], lambda e: e.activation(out=hal.a[:, ci, 0:H], in_=pre.a[:, W:W + H], func=AF.Copy))
            k.A([acc], [dst_tl], lambda e: e.activation(out=dst_ap, in_=acc.a[:, 0:W], func=AF.Silu))

        def mm(out_tl, out_ap, l_tl, l_ap, r_tl, r_ap, start=True, stop=True):
            k.P([l_tl, r_tl], [out_tl], lambda e: e.matmul(out_ap, lhsT=l_ap, rhs=r_ap, start=start, stop=stop))

        es1 = contextlib.ExitStack()
        with es1:
            k.es = es1
            sv = big.a.bitcast(F32)

            def slot(i):
                return sv[:, i * TT:(i + 1) * TT]
            sstore = k.sb("sstore", [128, 2, TT], F32)
            zst = k.sb("zst", [128, 4, TT], BF16)
            halo = k.sb("halo", [128, 20, 3], F32)
            k.V([], [halo], lambda e: e.memset(halo.a, 0.0))
            o_g = k.sb("o_g", [128, 4, TT], BF16)
            o_s = k.sb("o_s", [128, 4, TT], BF16)
            yT = k.sb("yT", [128, 8, TT], BF16)
            hb1_ap = yT.a.rearrange("p b c -> p (b c)")[:, 0:D]
            S_g = [k.sb("S_g%d" % h, [128, 128], F32) for h in range(4)]
            S_s = [k.sb("S_s%d" % h, [128, 64], F32) for h in range(8)]
            for s_ in S_g + S_s:
                k.V([], [s_], lambda e: e.memset(s_.a, 0.0))
            mhp = [RR([k.sb("mh%d_%d" % (h, i), [128, 128], F32) for i in range(8)]) for h in range(4)]
            mlp = [RR([k.sb("ml%d_%d" % (h, i), [128, 128], F32) for i in range(5)]) for h in range(4)]
            mcb = [k.sb("mcb%d" % i, [128, 128], F32) for i in range(2)]
            sm12 = RR([k.sb("sm%d" % i, [128, 16], F32) for i in range(24)])
            ktok = [k.sb("ktok%d" % h, [128, 128], F32) for h in range(4)]
            vtok = [k.sb("vtok%d" % h, [128, 128], F32) for h in range(4)]
            btok = [k.sb("btok%d" % g, [128, 128], F32) for g in range(2)]
            xtok = k.sb("xtok", [128, 512], F32)

            def proj_block(cb, pb):
                wb = wlo.get()
                wv = wb.a[:, 0:KD * 128].rearrange("p (k c) -> p k c", k=KD)
                if t == 0:
                    k.dma("pool", wb, wv, None, w1_in[cb])
                    k.dma("sp", w1s, w1s.a[cb], wb, wb.a[:, 0:KD * 128])
                else:
                    k.dma("sp", wb, wb.a[:, 0:KD * 128], w1s, w1s.a[cb])
                for kd in range(KD):
                    k.P([wb, hnT], [pb], lambda e: e.matmul(pb.a, lhsT=wv[:, kd, :], rhs=hnT.a[:, kd, 0:TT], start=(kd == 0), stop=(kd == KD - 1)))
                return pb

            def l2norm(src_tl, src_ap, dst_tl, dst_ap, scale, bias_tl, pb):
                sq = t512.get()
                k.A([src_tl], [sq], lambda e: e.activation(out=sq.a[:, 0:TT], in_=src_ap, func=AF.Square))
                mm(pb, pb.a, ones, ones.a, sq, sq.a[:, 0:TT])
                rt = t512.get()
                k.A([pb, bias_tl], [rt], lambda e: e.activation(out=rt.a[:, 0:TT], in_=pb.a, func=AF.Sqrt, bias=bias_tl.a[:, 0:1], scale=scale))
                k.V([rt], [rt], lambda e: e.reciprocal(out=rt.a[:, 0:TT], in_=rt.a[:, 0:TT]))
                k.V([src_tl, rt], [dst_tl], lambda e: e.tensor_tensor(out=dst_ap, in0=src_ap, in1=rt.a[:, 0:TT], op=ALU.mult))

            def transpose_f(dst_tl, dst_ap, src_tl, src_ap):
                p = pq.get()
                k.P([src_tl, ident], [p], lambda e: e.transpose(p.a, src_ap, ident.a))
                k.A([p], [dst_tl], lambda e: e.activation(out=dst_ap, in_=p.a, func=AF.Copy))

            for t in range(NT):
                for s in range(4):
                    k.dma("sp", xrow, xrow.a, None, x_in[t * TT + s * 128: t * TT + (s + 1) * 128, :])
                    norm_transpose(xrow, yT, hb1_ap, s * 128, n1c, hnT)
                jobs = []
                for hh in range(4):
                    for r in range(4):
                        jobs.append(("g", hh, r))
                for sbk in range(12):
                    jobs.append(("s", sbk, 0))

                def post(job, pb, ji):
                    kind, a_, r = job
                    if kind == "g":
                        hh = a_
                        if r == 3:
                            k.A([pb], [big], lambda e: e.activation(out=slot(hh * 4 + 3), in_=pb.a, func=AF.Silu))
                        elif r == 2:
                            conv_silu(pb, hh * 3 + r, big, slot(hh * 4 + 2), 4, halo, cw, cbias)
                        else:
                            tmp = t512.get()
                            conv_silu(pb, hh * 3 + r, tmp, tmp.a[:, 0:TT], 4, halo, cw, cbias)
                            if r == 0:
                                l2norm(tmp, tmp.a[:, 0:TT], big, slot(hh * 4 + 0), 128.0, eps128, pqb[ji % 2])
                            else:
                                l2norm(tmp, tmp.a[:, 0:TT], big, slot(hh * 4 + 1), 1.0, epsc, pqb[ji % 2])
                    else:
                        sbk = a_
                        if sbk < 4:
                            k.A([pb], [zst], lambda e: e.activation(out=zst.a[:, sbk, :], in_=pb.a, func=AF.Silu))
                        elif sbk < 8:
                            conv_silu(pb, 12 + (sbk - 4), big, slot(16 + sbk - 4), 4, halo, cw, cbias)
                        elif sbk < 10:
                            conv_silu(pb, 12 + (sbk - 4), big, slot(20 + sbk - 8), 4, halo, cw, cbias)
                        else:
                            conv_silu(pb, 12 + (sbk - 4), sstore, sstore.a[:, sbk - 10, :], 4, halo, cw, cbias)

                prev = None
                for ji, job in enumerate(jobs):
                    cb = (job[1] * 4 + job[2]) if job[0] == "g" else 16 + job[1]
                    pb = proj_block(cb, pbig_l[ji % 2])
                    if prev is not None:
                        post(*prev)
                    prev = (job, pb, ji)
                post(*prev)
                if t >= 1:
                    emit_conv((len(conv_jobs) + NT - 2) // max(NT - 1, 1))
                for c4 in range(4):
                    cs = slice(c4 * 128, (c4 + 1) * 128)
                    psm = pq.get()
                    for kd in range(KD):
                        k.P([hnT, wsm], [psm], lambda e: e.matmul(psm.a[:, 0:16], lhsT=hnT.a[:, kd, cs], rhs=wsm.a[:, kd, :], start=(kd == 0), stop=(kd == KD - 1)))
                    sm = sm12.get()
                    k.A([psm], [sm], lambda e: e.activation(out=sm.a, in_=psm.a[:, 0:16], func=AF.Copy))
                    xs_ = sm12.get(); ax = sm12.get(); ex = sm12.get(); sp = sm12.get(); G = sm12.get(); beta = sm12.get()
                    k.V([sm, dtb], [xs_], lambda e: e.tensor_tensor(out=xs_.a[:, 0:12], in0=sm.a[:, 4:16], in1=dtb.a, op=ALU.add))
                    k.A([xs_], [ax], lambda e: e.activation(out=ax.a[:, 0:12], in_=xs_.a[:, 0:12], func=AF.Abs))
                    k.A([ax], [ex], lambda e: e.activation(out=ex.a[:, 0:12], in_=ax.a[:, 0:12], func=AF.Exp, scale=-1.0))
                    k.A([ex, onec], [ex], lambda e: e.activation(out=ex.a[:, 0:12], in_=ex.a[:, 0:12], func=AF.Ln, bias=onec.a[:, 0:1], scale=1.0))
                    k.V([xs_, ex], [sp], lambda e: e.scalar_tensor_tensor(out=sp.a[:, 0:12], in0=xs_.a[:, 0:12], scalar=0.0, in1=ex.a[:, 0:12], op0=ALU.max, op1=ALU.add))
                    k.V([sp, nA], [G], lambda e: e.tensor_tensor(out=G.a[:, 0:12], in0=sp.a[:, 0:12], in1=nA.a, op=ALU.mult))
                    k.A([sm], [beta], lambda e: e.activation(out=beta.a[:, 0:4], in_=sm.a[:, 0:4], func=AF.Sigmoid))
                    pc = pq.get()
                    mm(pc, pc.a[:, 0:12], triU, triU.a, G, G.a[:, 0:12])
                    mm(pc, pc.a[:, 16:28], ones, ones.a, G, G.a[:, 0:12])
                    gc = sm12.get(); gl = sm12.get(); eg = sm12.get(); kdc = sm12.get(); dbc = sm12.get(); bg = sm12.get(); dif = sm12.get()
                    k.A([pc], [gc], lambda e: e.activation(out=gc.a[:, 0:12], in_=pc.a[:, 0:12], func=AF.Copy))
                    k.A([pc], [gl], lambda e: e.activation(out=gl.a[:, 0:12], in_=pc.a[:, 16:28], func=AF.Copy))
                    k.A([gc], [eg], lambda e: e.activation(out=eg.a[:, 0:12], in_=gc.a[:, 0:12], func=AF.Exp))
                    k.V([gl, gc], [dif], lambda e: e.tensor_tensor(out=dif.a[:, 0:12], in0=gl.a[:, 0:12], in1=gc.a[:, 0:12], op=ALU.subtract))
                    k.A([dif], [kdc], lambda e: e.activation(out=kdc.a[:, 0:12], in_=dif.a[:, 0:12], func=AF.Exp))
                    k.A([gl], [dbc], lambda e: e.activation(out=dbc.a[:, 0:12], in_=gl.a[:, 0:12], func=AF.Exp))
                    k.V([beta, eg], [bg], lambda e: e.tensor_tensor(out=bg.a[:, 0:4], in0=beta.a[:, 0:4], in1=eg.a[:, 0:4], op=ALU.mult))
                    DMP = (t == 0 and c4 < 2)
                    if DMP:
                        tg = "_c%d" % c4
                        dump("sm" + tg, sm, sm.a, [128, 16]); dump("sp" + tg, sp, sp.a[:, 0:12], [128, 12]); dump("G" + tg, G, G.a[:, 0:12], [128, 12])
                        dump("gc" + tg, gc, gc.a[:, 0:12], [128, 12]); dump("gl" + tg, gl, gl.a[:, 0:12], [128, 12]); dump("beta" + tg, beta, beta.a[:, 0:4], [128, 4])
                        dump("kdc" + tg, kdc, kdc.a[:, 0:12], [128, 12]); dump("dbc" + tg, dbc, dbc.a[:, 0:12], [128, 12])
                        if c4 == 0:
                            for i_ in range(22):
                                dump("slot%d" % i_, big, slot(i_), [128, TT])
                            dump("sstore", sstore, sstore.a, [128, 2, TT])

                    def gdn_head(hh):
                        mh, ml = mhp[hh], mlp[hh]
                        qT = slot(hh * 4 + 0)[:, cs]
                        kT = slot(hh * 4 + 1)[:, cs]
                        vT = slot(hh * 4 + 2)[:, cs]
                        p1 = pq.get()
                        k.P([big, ident], [p1], lambda e: e.transpose(p1.a, kT, ident.a))
                        yield
                        k.A([p1], [ktok[hh]], lambda e: e.activation(out=ktok[hh].a, in_=p1.a, func=AF.Copy))
                        p2 = pq.get()
                        k.P([big, ident], [p2], lambda e: e.transpose(p2.a, vT, ident.a))
                        yield
                        k.A([p2], [vtok[hh]], lambda e: e.activation(out=vtok[hh].a, in_=p2.a, func=AF.Copy))
                        gb = mh.get()
                        k.V([ones, G], [gb], lambda e: e.tensor_scalar(out=gb.a, in0=ones.a, scalar1=G.a[:, hh:hh + 1], scalar2=None, op0=ALU.mult))
                        prow = pq.get()
                        mm(prow, prow.a, gb, gb.a, triU, triU.a)
                        yield
                        X = mh.get(); GT = mh.get(); XL = mh.get(); GL = mh.get(); Eg = mh.get()
                        k.V([prow, gc, negU], [X], lambda e: e.scalar_tensor_tensor(out=X.a, in0=prow.a, scalar=gc.a[:, hh:hh + 1], in1=negU.a, op0=ALU.subtract, op1=ALU.min))
                        k.A([X], [GT], lambda e: e.activation(out=GT.a, in_=X.a, func=AF.Exp))
                        k.V([prow, gc, posL], [XL], lambda e: e.scalar_tensor_tensor(out=XL.a, in0=prow.a, scalar=gc.a[:, hh:hh + 1], in1=posL.a, op0=ALU.subtract, op1=ALU.max))
                        k.A([XL], [GL], lambda e: e.activation(out=GL.a, in_=XL.a, func=AF.Exp, scale=-1.0))
                        k.A([prow], [Eg], lambda e: e.activation(out=Eg.a, in_=prow.a, func=AF.Exp))
                        pqk = pq.get()
                        mm(pqk, pqk.a, big, kT, big, qT)
                        yield
                        attnT = ml.get()
                        k.V([pqk, GT], [attnT], lambda e: e.tensor_tensor(out=attnT.a, in0=pqk.a, in1=GT.a, op=ALU.mult))
                        vb = ml.get(); kbg = ml.get(); kdec = ml.get(); qdec = ml.get()
                        k.V([vtok[hh], beta], [vb], lambda e: e.tensor_scalar(out=vb.a, in0=vtok[hh].a, scalar1=beta.a[:, hh:hh + 1], scalar2=None, op0=ALU.mult))
                        k.V([ktok[hh], bg], [kbg], lambda e: e.tensor_scalar(out=kbg.a, in0=ktok[hh].a, scalar1=bg.a[:, hh:hh + 1], scalar2=None, op0=ALU.mult))
                        k.V([ktok[hh], kdc], [kdec], lambda e: e.tensor_scalar(out=kdec.a, in0=ktok[hh].a, scalar1=kdc.a[:, hh:hh + 1], scalar2=None, op0=ALU.mult))
                        k.V([big, Eg], [qdec], lambda e: e.tensor_tensor(out=qdec.a, in0=qT, in1=Eg.a, op=ALU.mult))
                        pkk = pq.get()
                        mm(pkk, pkk.a, big, kT, big, kT)
                        yield
                        Am = mh.get()
                        k.V([pkk, beta, GL], [Am], lambda e: e.scalar_tensor_tensor(out=Am.a, in0=pkk.a, scalar=beta.a[:, hh:hh + 1], in1=GL.a, op0=ALU.mult, op1=ALU.mult))
                        pN = pq.get()
                        k.P([Am, ident], [pN], lambda e: e.transpose(pN.a, Am.a, ident.a))
                        yield
                        Nm = mh.get()
                        k.A([pN], [Nm], lambda e: e.activation(out=Nm.a, in_=pN.a, func=AF.Copy))
                        R = mh.get()
                        k.V([ident, Nm], [R], lambda e: e.tensor_tensor(out=R.a, in0=ident.a, in1=Nm.a, op=ALU.subtract))
                        Pm, PTm = Nm, Am
                        for lvl in range(1, 7):
                            pPT = pq.get()
                            mm(pPT, pPT.a, Pm, Pm.a, PTm, PTm.a)
                            yield
                            nPT = mh.get()
                            k.A([pPT], [nPT], lambda e: e.activation(out=nPT.a, in_=pPT.a, func=AF.Copy))
                            if lvl < 6:
                                pP = pq.get()
                                mm(pP, pP.a, PTm, PTm.a, Pm, Pm.a)
                                yield
                                nP = mh.get()
                                k.V([pP], [nP], lambda e: e.tensor_copy(out=nP.a, in_=pP.a))
                            pR = pq.get()
                            mm(pR, pR.a, nPT, nPT.a, R, R.a)
                            yield
                            nR = mh.get()
                            k.V([pR, R], [nR], lambda e: e.tensor_tensor(out=nR.a, in0=pR.a, in1=R.a, op=ALU.add))
                            R = nR
                            PTm = nPT
                            if lvl < 6:
                                Pm = nP
                        pw = pq.get()
                        mm(pw, pw.a, kbg, kbg.a, R, R.a)
                        yield
                        wTn = mh.get()
                        k.A([pw], [wTn], lambda e: e.activation(out=wTn.a, in_=pw.a, func=AF.Copy, scale=-1.0))
                        pv = pq.get()
                        mm(pv, pv.a, R, R.a, vb, vb.a, start=True, stop=False)
                        mm(pv, pv.a, wTn, wTn.a, S_g[hh], S_g[hh].a, start=False, stop=True)
                        yield
                        vnew = mh.get()
                        k.V([pv], [vnew], lambda e: e.tensor_copy(out=vnew.a, in_=pv.a))
                        po = pq.get()
                        mm(po, po.a, S_g[hh], S_g[hh].a, qdec, qdec.a, start=True, stop=False)
                        mm(po, po.a, vnew, vnew.a, attnT, attnT.a, start=False, stop=True)
                        yield
                        k.A([po], [o_g], lambda e: e.activation(out=o_g.a[:, hh, cs], in_=po.a, func=AF.Copy))
                        pS = pq.get()
                        mm(pS, pS.a, kdec, kdec.a, vnew, vnew.a)
                        yield
                        k.V([S_g[hh], dbc, pS], [S_g[hh]], lambda e: e.scalar_tensor_tensor(out=S_g[hh].a, in0=S_g[hh].a, scalar=dbc.a[:, hh:hh + 1], in1=pS.a, op0=ALU.mult, op1=ALU.add))

                    lockstep([gdn_head(hh) for hh in range(4)])

                    for g2 in range(2):
                        transpose_f(btok[g2], btok[g2].a, big, slot(20 + g2)[:, cs])
                    for blk in range(4):
                        transpose_f(xtok, xtok.a[:, blk * 128:(blk + 1) * 128], big, slot(16 + blk)[:, cs])
                    for g2 in range(2):
                        p = pq.get()
                        mm(p, p.a, big, slot(20 + g2)[:, cs], sstore, sstore.a[:, g2, cs])
                        k.A([p], [mcb[g2]], lambda e: e.activation(out=mcb[g2].a, in_=p.a, func=AF.Copy))

                    def ssd_head(h, pos_):
                        g2 = h // 4
                        col = 4 + h
                        mh = mhp[h % 4]
                        gb = mh.get()
                        k.V([ones, G], [gb], lambda e: e.tensor_scalar(out=gb.a, in0=ones.a, scalar1=G.a[:, col:col + 1], scalar2=None, op0=ALU.mult))
                        prow = pq.get()
                        mm(prow, prow.a, gb, gb.a, triU, triU.a)
                        yield
                        X = mh.get(); GT = mh.get(); Eg = mh.get()
                        k.V([prow, gc, negU], [X], lambda e: e.scalar_tensor_tensor(out=X.a, in0=prow.a, scalar=gc.a[:, col:col + 1], in1=negU.a, op0=ALU.subtract, op1=ALU.min))
                        k.A([X], [GT], lambda e: e.activation(out=GT.a, in_=X.a, func=AF.Exp))
                        k.A([prow], [Eg], lambda e: e.activation(out=Eg.a, in_=prow.a, func=AF.Exp))
                        attnT = mh.get(); qdec = mh.get(); kdec = mh.get(); vh = mh.get()
                        k.V([mcb[g2], GT], [attnT], lambda e: e.tensor_tensor(out=attnT.a, in0=mcb[g2].a, in1=GT.a, op=ALU.mult))
                        k.V([sstore, Eg], [qdec], lambda e: e.tensor_tensor(out=qdec.a, in0=sstore.a[:, g2, cs], in1=Eg.a, op=ALU.mult))
                        k.V([btok[g2], kdc], [kdec], lambda e: e.tensor_scalar(out=kdec.a, in0=btok[g2].a, scalar1=kdc.a[:, col:col + 1], scalar2=None, op0=ALU.mult))
                        k.V([xtok, sp], [vh], lambda e: e.tensor_scalar(out=vh.a[:, 0:64], in0=xtok.a[:, h * 64:(h + 1) * 64], scalar1=sp.a[:, col:col + 1], scalar2=None, op0=ALU.mult))
                        half = slice((h % 2) * 64, (h % 2) * 64 + 64)
                        mm(pos_, pos_.a[half, :], S_s[h], S_s[h].a, qdec, qdec.a, start=True, stop=False)
                        mm(pos_, pos_.a[half, :], vh, vh.a[:, 0:64], attnT, attnT.a, start=False, stop=True)
                        pS = pq.get()
                        mm(pS, pS.a[:, 0:64], kdec, kdec.a, vh, vh.a[:, 0:64])
                        yield
                        k.V([S_s[h], dbc, pS], [S_s[h]], lambda e: e.scalar_tensor_tensor(out=S_s[h].a, in0=S_s[h].a, scalar=dbc.a[:, col:col + 1], in1=pS.a[:, 0:64], op0=ALU.mult, op1=ALU.add))

                    for g2 in range(2):
                        pp = [pq.get(), pq.get()]
                        lockstep([ssd_head(g2 * 4 + i, pp[i // 2]) for i in range(4)])
                        for i2 in range(2):
                            k.A([pp[i2]], [o_s], lambda e: e.activation(out=o_s.a[:, g2 * 2 + i2, cs], in_=pp[i2].a, func=AF.Copy))

                for hh in range(4):
                    sq = t512.get()
                    k.A([o_g], [sq], lambda e: e.activation(out=sq.a[:, 0:TT], in_=o_g.a[:, hh, :], func=AF.Square))
                    pb = pbig.get()
                    mm(pb, pb.a, ones, ones.a, sq, sq.a[:, 0:TT])
                    rt = t512.get()
                    k.A([pb, epsc], [rt], lambda e: e.activation(out=rt.a[:, 0:TT], in_=pb.a, func=AF.Sqrt, bias=epsc.a[:, 0:1], scale=1.0 / 128))
                    k.V([rt], [rt], lambda e: e.reciprocal(out=rt.a[:, 0:TT], in_=rt.a[:, 0:TT]))
                    k.V([o_g, rt], [rt], lambda e: e.tensor_tensor(out=rt.a[:, 0:TT], in0=o_g.a[:, hh, :], in1=rt.a[:, 0:TT], op=ALU.mult))
                    k.V([rt, gdnnw, big], [yT], lambda e: e.scalar_tensor_tensor(out=yT.a[:, hh, :], in0=rt.a[:, 0:TT], scalar=gdnnw.a[:, 0:1], in1=slot(hh * 4 + 3), op0=ALU.mult, op1=ALU.mult))
                for g2 in range(2):
                    ys = []
                    pb = pbig.get()
                    for bi in range(2):
                        blk = g2 * 2 + bi
                        y0 = t512.get()
                        k.V([big, dcol, o_s], [y0], lambda e: e.scalar_tensor_tensor(out=y0.a[:, 0:TT], in0=slot(16 + blk), scalar=dcol.a[:, blk:blk + 1], in1=o_s.a[:, blk, :], op0=ALU.mult, op1=ALU.add))
                        k.V([y0, zst], [y0], lambda e: e.tensor_tensor(out=y0.a[:, 0:TT], in0=y0.a[:, 0:TT], in1=zst.a[:, blk, :], op=ALU.mult))
                        sq = t512.get()
                        k.A([y0], [sq], lambda e: e.activation(out=sq.a[:, 0:TT], in_=y0.a[:, 0:TT], func=AF.Square))
                        mm(pb, pb.a, ones, ones.a, sq, sq.a[:, 0:TT], start=(bi == 0), stop=(bi == 1))
                        ys.append(y0)
                    rt = t512.get()
                    k.A([pb, epsc], [rt], lambda e: e.activation(out=rt.a[:, 0:TT], in_=pb.a, func=AF.Sqrt, bias=epsc.a[:, 0:1], scale=1.0 / 256))
                    k.V([rt], [rt], lambda e: e.reciprocal(out=rt.a[:, 0:TT], in_=rt.a[:, 0:TT]))
                    for bi in range(2):
                        blk = g2 * 2 + bi
                        y0 = ys[bi]
                        k.V([y0, ssmnw, rt], [yT], lambda e: e.scalar_tensor_tensor(out=yT.a[:, 4 + blk, :], in0=y0.a[:, 0:TT], scalar=ssmnw.a[:, blk:blk + 1], in1=rt.a[:, 0:TT], op0=ALU.mult, op1=ALU.mult))
                if t == 0:
                    dump("o_g", o_g, o_g.a, [128, 4, TT]); dump("o_s", o_s, o_s.a, [128, 4, TT])
                k.dma("sp", ybuf, ybuf.a[t].rearrange("(b p) c -> p b c", p=128), yT, yT.a)
                if dbg:
                    k.dma("sp", ydbg, ydbg.a[t].rearrange("(b p) c -> p b c", p=128), yT, yT.a)
                k.collective(ybuf, gath, ybuf.a[t], gath.a[t * 4096:(t + 1) * 4096, :], cfg["groups"], gath)
            emit_conv(len(conv_jobs))
            zt = t512.get()
            k.V([], [zt], lambda e: e.memset(zt.a, 0.0))
            zb16 = zt.a.bitcast(BF16)[:, 0:TT]
            for i in range(32):
                k.dma("sp", gath, gath.a[NT * 4096 + i * 128: NT * 4096 + (i + 1) * 128, :], zt, zb16, semtl=zt)
            k.barrier()
        k.es = es

        es3 = contextlib.ExitStack()
        if cfg.get("stop_after", 3) == 1:
            k.finish([ydbg, gath] + dumped)
            k.barrier()
            return nc
        with es3:
            k.es = es3
            rowb = k.sb("rowb", [128, D], F32)
            hb3 = k.sb("hb3", [128, D], BF16)
            halo2 = k.sb("halo2", [128, FB, 2], F32)
            piece = RR([k.sb("piece%d" % i, [128, 512], F32) for i in range(4)])
            ssq3 = k.sb("ssqf", [128, 4, NG], F32)
            actT = big.a.rearrange("p (f c) -> p f c", c=TT)
            y_sb = big.a[:, 0:KM * 640].rearrange("p (k c) -> p k c", k=KM)
            accs = [pbig_l[0], pbig_l[1], pacc[0], pacc[1]]
            pbB = RR([pbig_l[0], pbig_l[1], pacc[0], pacc[1]])
            WSL = WBE // 512
            for m in range(NT3):
                g0 = 0 if m == 0 else 1
                for kd in range(KM):
                    if m == 0:
                        k.dma("pool", big, y_sb[:, kd, 0:128], gath, gath.a.rearrange("r (q c) -> (r q) c", c=128), kind="ind",
                              in_off=bass.IndirectOffsetOnAxis(ap=idx.a[:, kd:kd + 1], axis=0))
                    k.dma("pool", big, y_sb[:, kd, 128:640], gath, gath.a[:, :], kind="ind",
                          in_off=bass.IndirectOffsetOnAxis(ap=idx.a[:, (m + 1) * KM + kd:(m + 1) * KM + kd + 1], axis=0))
                for g in range(NG):
                    for hf in range(2):
                        wb = wblk.get()
                        wv = wb.a.rearrange("p (k c) -> p k c", k=KM)
                        k.dma("sp", wb, wb.a[:, 0:KM * 256], wos, wos.a[g * 2 + hf])
                        for gi in range(g0, 5):
                            pb = pbig.get()
                            for kd in range(KM):
                                k.P([big, wb], [pb], lambda e: e.matmul(pb.a[:, 0:256], lhsT=y_sb[:, kd, gi * 128:(gi + 1) * 128], rhs=wv[:, kd, :], start=(kd == 0), stop=(kd == KM - 1)))
                            xp = piece.get()
                            r0 = (0 if gi == 0 else 128 + m * TT + (gi - 1) * 128)
                            c0 = g * 512 + hf * 256
                            k.dma("pool", xp, xp.a[:, 0:256], None, xq_in[r0:r0 + 128, c0:c0 + 256])
                            k.V([pb, xp], [xp], lambda e: e.tensor_tensor(out=xp.a[:, 0:256], in0=pb.a[:, 0:256], in1=xp.a[:, 0:256], op=ALU.add))
                            k.dma("pool", hbuf, hbuf.a[r0:r0 + 128, c0:c0 + 256], xp, xp.a[:, 0:256])
                            if dbg:
                                k.dma("pool", hdbg, hdbg.a[r0:r0 + 128, c0:c0 + 256], xp, xp.a[:, 0:256])
                for gi in range(g0, 5):
                    r0 = (0 if gi == 0 else 128 + m * TT + (gi - 1) * 128)
                    k.dma("sp", xrow, xrow.a, hbuf, hbuf.a[r0:r0 + 128, :])
                    if gi == 0:
                        norm_transpose(xrow, hb3, hb3.a, 0, n2c, hnT, ncols=2, srccol=126)
                    else:
                        norm_transpose(xrow, hb3, hb3.a, 2 + (gi - 1) * 128, n2c, hnT)
                k.V([], [ssq3], lambda e: e.memset(ssq3.a, 0.0))
                for hf in range(2):
                    for fl in range(FH):
                        fb = hf * FH + fl
                        wb = wblk.get()
                        wv = wb.a[:, 0:2 * KD * 128].rearrange("p (u k c) -> p u k c", u=2, k=KD)
                        k.dma("sp", wb, wb.a[:, 0:2 * KD * 128], wgs, wgs.a[fb])
                        pg = pbB.get()
                        for kd in range(KD):
                            k.P([wb, hnT], [pg], lambda e: e.matmul(pg.a, lhsT=wv[:, 0, kd, :], rhs=hnT.a[:, kd, 2:2 + TT], start=(kd == 0), stop=(kd == KD - 1)))
                        if m == 0:
                            ph = pq3.get()
                            for kd in range(KD):
                                k.P([wb, hnT], [ph], lambda e: e.matmul(ph.a[:, 0:2], lhsT=wv[:, 0, kd, :], rhs=hnT.a[:, kd, 0:2], start=(kd == 0), stop=(kd == KD - 1)))
                            k.A([ph], [halo2], lambda e: e.activation(out=halo2.a[:, fb, :], in_=ph.a[:, 0:2], func=AF.Copy))
                        pu = pbB.get()
                        for kd in range(KD):
                            k.P([wb, hnT], [pu], lambda e: e.matmul(pu.a, lhsT=wv[:, 1, kd, :], rhs=hnT.a[:, kd, 2:2 + TT], start=(kd == 0), stop=(kd == KD - 1)))
                        sg = t512.get()
                        conv_silu(pg, fb, sg, sg.a[:, 0:TT], 3, halo2, fcw, fcb)
                        k.V([sg, pu], [big], lambda e: e.tensor_tensor(out=actT[:, fl, :], in0=sg.a[:, 0:TT], in1=pu.a, op=ALU.mult))
                    for g in range(NG):
                        nsl = (FH + WSL - 1) // WSL
                        for sl in range(nsl):
                            f0 = sl * WSL
                            nf = min(WSL, FH - f0)
                            wb = wblk.get()
                            wv = wb.a.rearrange("p (f c) -> p f c", c=512)
                            r0 = (hf * FH + f0) * 128
                            si = (hf * NG + g) * NSL + sl
                            k.dma("sp", wb, wb.a[:, 0:nf * 512], wds, wds.a[si][:, 0:nf * 512])
                            for i in range(4):
                                for f in range(nf):
                                    k.P([big, wb], [accs[i]], lambda e: e.matmul(accs[i].a, lhsT=actT[:, f0 + f, i * 128:(i + 1) * 128], rhs=wv[:, f, :], start=(f0 + f == 0), stop=(f0 + f == FH - 1)))
                        for i in range(4):
                            r0 = m * TT + i * 128
                            hp = piece.get()
                            if hf == 0:
                                k.dma("pool", hp, hp.a, hbuf, hbuf.a[128 + r0:128 + r0 + 128, g * 512:(g + 1) * 512])
                            else:
                                k.dma("pool", hp, hp.a, out_d, out_d.a[r0:r0 + 128, g * 512:(g + 1) * 512])
                            k.V([accs[i], hp], [hp], lambda e: e.tensor_tensor(out=hp.a, in0=accs[i].a, in1=hp.a, op=ALU.add))
                            if hf == 1:
                                jk = t512.get()
                                k.A([hp, ssq3], [jk, ssq3], lambda e: e.activation(out=jk.a[:, 0:512], in_=hp.a, func=AF.Square, accum_out=ssq3.a[:, i, g:g + 1]))
                            k.dma("pool", out_d, out_d.a[r0:r0 + 128, g * 512:(g + 1) * 512], hp, hp.a)
                k.dma("sp", xrow, xrow.a, None, nfb_in.partition_broadcast(128))
                for i in range(4):
                    r0 = m * TT + i * 128
                    tot = ssq.get(); rs = rstd.get()
                    k.V([ssq3], [tot], lambda e: e.tensor_reduce(out=tot.a, in_=ssq3.a[:, i, :], axis=mybir.AxisListType.X, op=ALU.add))
                    k.A([tot, epsc], [rs], lambda e: e.activation(out=rs.a, in_=tot.a, func=AF.Sqrt, bias=epsc.a[:, 0:1], scale=1.0 / D))
                    k.V([rs], [rs], lambda e: e.reciprocal(out=rs.a, in_=rs.a))
                    k.dma("pool", rowb, rowb.a, out_d, out_d.a[r0:r0 + 128, :])
                    k.V([rowb, rs, xrow], [rowb], lambda e: e.scalar_tensor_tensor(out=rowb.a, in0=rowb.a, scalar=rs.a[:, 0:1], in1=xrow.a, op0=ALU.mult, op1=ALU.mult))
                    k.dma("pool", out_d, out_d.a[r0:r0 + 128, :], rowb, rowb.a)
            fin = [out_d]
            if dbg:
                fin += [ydbg, hdbg]
            k.finish(fin)
            k.barrier()
        k.es = es
    return nc


Q0, K0, V0, GATE0, BETA0, ALPHA0, Z0, XBC0, DT0 = 0, 2048, 4096, 6144, 8192, 8208, 8224, 10272, 14368
GROUPS = [[0, 1, 2, 3], [4, 5, 6, 7]]


def _masks():
    p = np.arange(128)[:, None]
    f = np.arange(128)[None, :]
    ident = (p == f).astype(np.float32)
    triU = (p <= f).astype(np.float32)
    negU = np.where(f >= p, 0.0, NEG).astype(np.float32)
    posL = np.where(p > f, 0.0, -NEG).astype(np.float32)
    return np.ascontiguousarray(np.stack([ident, triU, negU, posL]))


def prep_inputs(inp, cfg):
    L, D, DFF = cfg["L"], cfg["D"], cfg["DFF"]
    KD, FB, NG = D // 128, DFF // 128, D // 512
    LQ = L // 4
    NT, NT3 = L // TT, LQ // TT
    f32 = np.float32
    x = np.asarray(inp["x"], f32)
    w_in = np.asarray(inp["w_in"], f32)[0]
    w_out = np.asarray(inp["w_out"], f32)[0]
    gcw = np.asarray(inp["gdn_conv_w"], f32)[0]
    scw = np.asarray(inp["ssm_conv_w"], f32)[0]
    scb = np.asarray(inp["ssm_conv_b"], f32)[0]
    ar = np.arange(128)
    masks = _masks()
    wgate = np.asarray(inp["ffn_w_gate"], f32)[0].reshape(KD, 128, FB, 128).transpose(2, 1, 0, 3)
    wup = np.asarray(inp["ffn_w_up"], f32)[0].reshape(KD, 128, FB, 128).transpose(2, 1, 0, 3)
    wg = np.ascontiguousarray(np.stack([wgate, wup], axis=2))
    wd = np.ascontiguousarray(np.asarray(inp["ffn_w_down"], f32)[0])
    fcw = np.ascontiguousarray(np.asarray(inp["ffn_conv_w"], f32)[0].reshape(3, FB, 128).transpose(2, 1, 0))
    fcb = np.ascontiguousarray(np.asarray(inp["ffn_conv_b"], f32)[0].reshape(FB, 128).T)
    n1c = np.ascontiguousarray(np.asarray(inp["norm1_w"], f32)[0].reshape(KD, 128).T)
    n2c = np.ascontiguousarray(np.asarray(inp["norm2_w"], f32)[0].reshape(KD, 128).T)
    nfb = np.ascontiguousarray(np.asarray(inp["norm_f_w"], f32))
    gdnnw = np.ascontiguousarray(np.asarray(inp["gdn_norm_w"], f32)[0].reshape(128, 1))
    perm = []
    for r in range(4):
        for hh in range(4):
            perm.append(128 * (4 * r + hh) + ar)
        perm.append(2048 + 512 * r + np.arange(512))
    perm = np.concatenate(perm)
    wo = np.ascontiguousarray(w_out[perm].reshape(KM, 128, NG, 512).transpose(2, 1, 0, 3))
    maps = []
    for c in range(8):
        b, j = c // 4, c % 4
        cols = []
        for hh in range(4):
            H = 4 * j + hh
            cols += [Q0 + 128 * H + ar, K0 + 128 * H + ar, V0 + 128 * H + ar, GATE0 + 128 * H + ar]
        for zb in range(4):
            cols.append(Z0 + 512 * j + zb * 128 + ar)
        for blk in range(4):
            cols.append(XBC0 + 512 * j + blk * 128 + ar)
        for g in range(2):
            cols.append(XBC0 + 2048 + 128 * (2 * j + g) + ar)
        for g in range(2):
            cols.append(XBC0 + 3072 + 128 * (2 * j + g) + ar)
        cols = np.concatenate(cols)
        w1 = np.ascontiguousarray(w_in[:, cols].reshape(KD, 128, 28, 128).transpose(2, 1, 0, 3))
        small = np.concatenate([BETA0 + 4 * j + np.arange(4), ALPHA0 + 4 * j + np.arange(4), DT0 + 8 * j + np.arange(8)])
        wsm = np.ascontiguousarray(w_in[:, small].reshape(KD, 128, 16).transpose(1, 0, 2))
        cw = np.zeros((128, 20, 4), f32)
        cb = np.zeros((128, 20), f32)
        for hh in range(4):
            H = 4 * j + hh
            for r in range(3):
                cw[:, hh * 3 + r, :] = gcw[:, r * 2048 + 128 * H + ar].T
        for i in range(8):
            if i < 4:
                ch = 512 * j + i * 128 + ar
            elif i < 6:
                ch = 2048 + 128 * (2 * j + (i - 4)) + ar
            else:
                ch = 3072 + 128 * (2 * j + (i - 6)) + ar
            cw[:, 12 + i, :] = scw[:, ch].T
            cb[:, 12 + i] = scb[ch]
        dtb = np.concatenate([np.asarray(inp["gdn_dt_bias"], f32)[0][4 * j:4 * j + 4], np.asarray(inp["ssm_dt_bias"], f32)[0][8 * j:8 * j + 8]])
        alog = np.concatenate([np.asarray(inp["gdn_A_log"], f32)[0][4 * j:4 * j + 4], np.asarray(inp["ssm_A_log"], f32)[0][8 * j:8 * j + 8]])
        ssd_D = np.asarray(inp["ssm_D"], f32)[0]
        dcol = np.zeros((128, 4), f32)
        for blk in range(4):
            dcol[:64, blk] = ssd_D[8 * j + 2 * blk]
            dcol[64:, blk] = ssd_D[8 * j + 2 * blk + 1]
        ssmnw = np.ascontiguousarray(np.asarray(inp["ssm_norm_w"], f32)[0][512 * j:512 * (j + 1)].reshape(4, 128).T)
        xq = np.zeros((128 + LQ, D), f32)
        if j > 0:
            xq[:] = x[b, j * LQ - 128:(j + 1) * LQ]
        else:
            xq[128:] = x[b, 0:LQ]
        idx = np.zeros((128, (NT3 + 1) * KM), np.int32)
        for blk in range(NT3 + 1):
            if blk == 0:
                tile = j * NT3 - 1 if j > 0 else NT
            else:
                tile = j * NT3 + (blk - 1)
            for kd in range(KM):
                idx[:, blk * KM + kd] = tile * 4096 + kd * 128 + ar
                if blk == 0:
                    idx[:, kd] = idx[:, kd] * 4 + 3
        maps.append({
            "x": np.ascontiguousarray(x[b]), "xq": xq, "w1": w1, "wsm": wsm, "n1c": n1c, "n2c": n2c, "nfb": nfb,
            "cw": cw, "cb": cb, "dtb": np.ascontiguousarray(np.tile(dtb[None, :], (128, 1))),
            "alog": np.ascontiguousarray(np.tile(alog[None, :], (128, 1))), "dcol": dcol, "ssmnw": ssmnw, "gdnnw": gdnnw,
            "masks": masks, "wo": wo, "wg": wg, "wd": wd, "fcw": fcw, "fcb": fcb, "idx": idx,
        })
    return maps


def run(inp, cfg):
    nc = build_program(cfg)
    maps = prep_inputs(inp, cfg)
    res = run_bass_kernel_spmd(nc, maps, core_ids=list(range(8)))
    return res


def kernel(**inputs):
    cfg = {"L": 8192, "D": 4096, "DFF": 11008, "groups": GROUPS}
    res = run(inputs, cfg)
    L, D = cfg["L"], cfg["D"]
    LQ = L // 4
    out = np.zeros((2, L, D), np.float32)
    for c in range(8):
        b, j = c // 4, c % 4
        out[b, j * LQ:(j + 1) * LQ] = res.results[c]["out"]
    return out
```

```python
import bisect
import contextlib
import numpy as np
import concourse.bass as bass
import concourse.mybir as mybir
from concourse.bass_utils import run_bass_kernel_spmd

F32 = mybir.dt.float32
BF16 = mybir.dt.bfloat16
I32 = mybir.dt.int32
AF = mybir.ActivationFunctionType
ALU = mybir.AluOpType
EPS = 1e-6
NEG = -30000.0


class Tl:
    __slots__ = ("a", "name", "w", "r", "dsem", "dn", "multi")

    def __init__(self, a, name, multi=False):
        self.a = a
        self.name = name
        self.w = [] if multi else None
        self.r = {}
        self.dsem = {}
        self.dn = {}
        self.multi = multi


class TlView:
    def __init__(self, parent, a):
        object.__setattr__(self, "p", parent)
        object.__setattr__(self, "a", a)

    def __getattr__(self, n):
        return getattr(object.__getattribute__(self, "p"), n)

    def __setattr__(self, n, v):
        setattr(object.__getattribute__(self, "p"), n, v)


class KB:
    ENG = ("pe", "dve", "act", "pool", "sp")

    def __init__(self, nc, es):
        self.nc = nc
        self.es = es
        self.es_sem = es
        self.dma_tiles = []
        self.q = {"pe": nc.tensor, "dve": nc.vector, "act": nc.scalar, "pool": nc.gpsimd, "sp": nc.sync}
        self.sem = {e: es.enter_context(nc.semaphore("s_" + e)) for e in self.ENG}
        self.emitted = {e: 0 for e in self.ENG}
        self.last = {e: None for e in self.ENG}
        self.sigs = {e: [] for e in self.ENG}
        self.seen = {e: {} for e in self.ENG}
        self.nsem = 0
        self.uid = 0

    def sb(self, name, shape, dt):
        t = self.es.enter_context(self.nc.sbuf_tensor("sb_" + name, list(shape), dt))
        return Tl(t[:], name)

    def ps(self, name, shape, dt):
        t = self.es.enter_context(self.nc.psum_tensor("ps_" + name, list(shape), dt))
        return Tl(t[:], name)

    def dram(self, name, shape, dt, kind="Internal", multi=False):
        t = self.nc.dram_tensor(name, list(shape), dt, kind=kind)
        return Tl(t.ap(), name, multi=multi)

    def view(self, ap, name):
        return Tl(ap, name)

    def newsem(self, name):
        self.nsem += 1
        return self.es_sem.enter_context(self.nc.semaphore(name))

    def _setw(self, t, dep):
        if t.multi:
            t.w = [d for d in t.w if d[1] is not dep[1]] + [dep]
        else:
            t.w = dep
        t.r = {}

    def barrier(self):
        engs = ["pe", "dve", "act"]
        for e in engs:
            if self.last[e] is not None and (not self.sigs[e] or self.sigs[e][-1] < self.emitted[e]):
                self.last[e].then_inc(self.sem[e], 1)
                self.sigs[e].append(self.emitted[e])
        for e in self.ENG:
            for o in engs:
                if o != e and self.sigs[o]:
                    self._wait_sem(e, self.sem[o], len(self.sigs[o]))
            for tl in self.dma_tiles:
                for qc in tl.dsem:
                    self._wait_sem(e, tl.dsem[qc], tl.dn[qc])

    def _wait_sem(self, eng, sem, val):
        d = self.seen[eng]
        if d.get(sem.name, 0) >= val:
            return
        d[sem.name] = val
        self.q[eng].wait_ge(sem, val)

    def _need(self, eng, dep):
        if dep is None:
            return
        if isinstance(dep, list):
            for d in dep:
                self._need(eng, d)
            return
        if dep[0] == "e":
            _, e, idx = dep
            sg = self.sigs[e]
            if not sg or sg[-1] < idx:
                self.last[e].then_inc(self.sem[e], 1)
                sg.append(self.emitted[e])
            pos = bisect.bisect_left(sg, idx)
            self._wait_sem(eng, self.sem[e], pos + 1)
        else:
            tl = dep[1]
            for qc in tl.dsem:
                self._wait_sem(eng, tl.dsem[qc], tl.dn[qc])

    def _deps(self, eng, reads, writes, is_dma=False):
        for t in reads:
            w = t.w
            if w is not None:
                if not isinstance(w, list) and w[0] == "e" and w[1] == eng and eng == "pe":
                    continue
                self._need(eng, w)
        for t in writes:
            w = t.w
            if w is not None and (isinstance(w, list) or not (w[0] == "e" and w[1] == eng and eng == "pe")):
                self._need(eng, w)
            for key, dep in t.r.items():
                if dep[0] == "e" and dep[1] == eng and eng == "pe":
                    continue
                self._need(eng, dep)

    def op(self, eng, reads, writes, fn):
        self._deps(eng, reads, writes)
        ins = fn(self.q[eng])
        self.emitted[eng] += 1
        idx = self.emitted[eng]
        self.last[eng] = ins
        for t in reads:
            t.r[eng] = ("e", eng, idx)
        for t in writes:
            self._setw(t, ("e", eng, idx))
        return ins

    def V(self, reads, writes, fn):
        return self.op("dve", reads, writes, fn)

    def A(self, reads, writes, fn):
        return self.op("act", reads, writes, fn)

    def P(self, reads, writes, fn):
        return self.op("pe", reads, writes, fn)

    def dma(self, qe, dst, dst_ap, src, src_ap, semtl=None, kind="dma", in_off=None):
        reads = [src] if src is not None else []
        writes = [dst] if dst is not None else []
        self._deps(qe, reads, writes, is_dma=True)
        if semtl is None:
            semtl = dst if dst is not None else src
        qc = "sw" if qe == "pool" else "hw"
        if qc not in semtl.dsem:
            if not semtl.dsem:
                self.dma_tiles.append(semtl)
            semtl.dsem[qc] = self.newsem("d%s_%s" % (qc, semtl.name))
            semtl.dn[qc] = 0
        e = self.q[qe]
        if kind == "dma":
            ins = e.dma_start(out=dst_ap, in_=src_ap)
            inc = 16
        elif kind == "ind":
            ins = e.indirect_dma_start(out=dst_ap, out_offset=None, in_=src_ap, in_offset=in_off)
            inc = 16
        else:
            raise ValueError(kind)
        semtl.dn[qc] += inc
        ins.then_inc(semtl.dsem[qc], inc)
        self.emitted[qe] += 1
        self.last[qe] = None
        dep = ("d", semtl)
        if src is not None:
            src.r["d_" + semtl.name] = dep
        if dst is not None:
            self._setw(dst, dep)
        return ins

    def collective(self, src, dst, src_ap, dst_ap, groups, cctl):
        self._deps("pool", [src], [dst], is_dma=True)
        if "cc" not in cctl.dsem:
            if not cctl.dsem:
                self.dma_tiles.append(cctl)
            cctl.dsem["cc"] = self.newsem("dcc_" + cctl.name)
            cctl.dn["cc"] = 0
        ins = self.q["pool"].collective_compute("AllGather", ALU.bypass, replica_groups=groups, ins=[src_ap], outs=[dst_ap])
        cctl.dn["cc"] += 1
        ins.then_inc(cctl.dsem["cc"], 1)
        self.emitted["pool"] += 1
        self.last["pool"] = None
        dep = ("d", cctl)
        src.r["d_" + cctl.name] = dep
        self._setw(dst, dep)

    def finish(self, tls):
        for t in tls:
            if t.w is not None:
                self._need("sp", t.w)


class RR:
    def __init__(self, items):
        self.items = items
        self.i = 0

    def get(self):
        x = self.items[self.i % len(self.items)]
        self.i += 1
        return x


KM = 32
TT = 512
WBE = 8192


LOCK = [True]


def lockstep(gens):
    gens = list(gens)
    if not LOCK[0]:
        for g in gens:
            for _ in g:
                pass
        return
    while gens:
        nxt = []
        for g in gens:
            try:
                next(g)
                nxt.append(g)
            except StopIteration:
                pass
        gens = nxt


def build_program(cfg):
    L, D, DFF = cfg["L"], cfg["D"], cfg["DFF"]
    dbg = cfg.get("debug", False)
    LOCK[0] = cfg.get("lock", True)
    KD = D // 128
    NT = L // TT
    LQ = L // 4
    NT3 = LQ // TT
    FB = DFF // 128
    FH = FB // 2
    NG = D // 512
    assert FB % 2 == 0 and LQ % TT == 0 and D % 512 == 0 and FH <= 44 and 2 * KD * 128 <= WBE

    nc = bass.Bass("TRN2", target_bir_lowering=False)
    es = contextlib.ExitStack()
    with es:
        k = KB(nc, es)

        def din(name, shape, dt=F32):
            return nc.dram_tensor(name, list(shape), dt, kind="ExternalInput").ap()

        dump_on = cfg.get("dump", False)
        dumped = []

        def dump(name, tl, ap, shape, dt=F32):
            if not dump_on:
                return
            d = k.dram("D_" + name, list(shape), dt, kind="ExternalOutput")
            k.dma("sp", d, d.a, tl, ap)
            dumped.append(d)

        x_in = din("x", [L, D])
        xq_in = din("xq", [128 + LQ, D])
        w1_in = din("w1", [28, 128, KD, 128])
        wsm_in = din("wsm", [128, KD, 16])
        n1c_in = din("n1c", [128, KD])
        n2c_in = din("n2c", [128, KD])
        nfb_in = din("nfb", [D])
        cw_in = din("cw", [128, 20, 4])
        cb_in = din("cb", [128, 20])
        dtb_in = din("dtb", [128, 12])
        alog_in = din("alog", [128, 12])
        dcol_in = din("dcol", [128, 4])
        ssmnw_in = din("ssmnw", [128, 4])
        gdnnw_in = din("gdnnw", [128, 1])
        masks_in = din("masks", [4, 128, 128])
        wo_in = din("wo", [NG, 128, KM, 512])
        wg_in = din("wg", [FB, 128, 2, KD, 128])
        wd_in = din("wd", [DFF, D])
        fcw_in = din("fcw", [128, FB, 3])
        fcb_in = din("fcb", [128, FB])
        idx_in = din("idx", [128, (NT3 + 1) * KM], I32)
        out_d = k.dram("out", [LQ, D], F32, kind="ExternalOutput")
        ybuf = k.dram("ybuf", [NT, 1024, TT], BF16)
        gath = k.dram("gath", [(NT + 1) * 4096, TT], BF16, multi=True)
        hbuf = k.dram("hbuf", [128 + LQ, D], F32)
        WSLc = WBE // 512
        NSL = (FH + WSLc - 1) // WSLc
        w1s = k.dram("w1s", [28, 128, KD * 128], BF16)
        wos = k.dram("wos", [NG * 2, 128, KM * 256], BF16)
        wgs = k.dram("wgs", [FB, 128, 2 * KD * 128], BF16)
        wds = k.dram("wds", [2 * NG * NSL, 128, WBE], BF16)
        if dbg:
            ydbg = k.dram("ydbg", [NT, 1024, TT], BF16, kind="ExternalOutput")
            hdbg = k.dram("hdbg", [128 + LQ, D], F32, kind="ExternalOutput")

        cst = {"sp": Tl(None, "cst_sp"), "pool": Tl(None, "cst_pool")}

        HB = WBE // 2
        conv_jobs = []
        for g in range(NG):
            for hf in range(2):
                for q2 in range(2):
                    ks = slice(q2 * (KM // 2), (q2 + 1) * (KM // 2))
                    conv_jobs.append((wos, wos.a[g * 2 + hf].rearrange("p (k c) -> p k c", k=KM)[:, ks, :],
                                      wo_in[g][:, ks, hf * 256:(hf + 1) * 256], (KM // 2, 256)))
        for fb in range(FB):
            for u in range(2):
                conv_jobs.append((wgs, wgs.a[fb].rearrange("p (u k c) -> p u k c", u=2, k=KD)[:, u], wg_in[fb][:, u], (KD, 128)))
        for hf in range(2):
            for g in range(NG):
                for sl in range(NSL):
                    f0 = sl * WSLc
                    nf = min(WSLc, FH - f0)
                    si = (hf * NG + g) * NSL + sl
                    for q2 in range(2):
                        fa = q2 * (WSLc // 2)
                        fn_ = min(WSLc // 2, nf - fa)
                        if fn_ <= 0:
                            continue
                        r0 = (hf * FH + f0 + fa) * 128
                        conv_jobs.append((wds, wds.a[si][:, fa * 512:(fa + fn_) * 512].rearrange("p (f c) -> p f c", c=512),
                                          wd_in[r0:r0 + fn_ * 128, g * 512:(g + 1) * 512].rearrange("(f p) c -> p f c", p=128), (fn_, 512)))
        conv_pos = [0]
        bnc = []

        def emit_conv(n):
            for _ in range(n):
                if conv_pos[0] < len(conv_jobs):
                    d, dap, sap, (a_, b_) = conv_jobs[conv_pos[0]]
                    bt = bnc[conv_pos[0] % 2]
                    bv = bt.a[:, 0:a_ * b_].rearrange("p (a b) -> p a b", a=a_)
                    k.dma("pool", bt, bv, None, sap)
                    k.dma("pool", d, dap, bt, bv)
                    conv_pos[0] += 1

        def cload(name, shape, src_ap, dt=F32, q="sp"):
            t = k.sb(name, shape, dt)
            k.dma(q, t, t.a, None, src_ap, semtl=cst[q])
            return t

        ident = cload("ident", [128, 128], masks_in[0])
        triU = cload("triU", [128, 128], masks_in[1])
        negU = cload("negU", [128, 128], masks_in[2])
        posL = cload("posL", [128, 128], masks_in[3])
        n1c = cload("n1c", [128, KD], n1c_in)
        n2c = cload("n2c", [128, KD], n2c_in)
        cw = cload("cw", [128, 20, 4], cw_in)
        cbias = cload("cbias", [128, 20], cb_in)
        dtb = cload("dtb", [128, 12], dtb_in)
        alog = cload("alog", [128, 12], alog_in)
        dcol = cload("dcol", [128, 4], dcol_in)
        ssmnw = cload("ssmnw", [128, 4], ssmnw_in)
        gdnnw = cload("gdnnw", [128, 1], gdnnw_in)
        fcw = cload("fcw", [128, FB, 3], fcw_in)
        fcb = cload("fcb", [128, FB], fcb_in)
        idx = cload("idx", [128, (NT3 + 1) * KM], idx_in, dt=I32, q="pool")
        wsm = cload("wsm", [128, KD, 16], wsm_in, dt=BF16, q="pool")
        identb = k.sb("identb", [128, 128], BF16)
        ones = k.sb("ones", [128, 128], F32)
        epsc = k.sb("epsc", [128, 1], F32)
        eps128 = k.sb("eps128", [128, 1], F32)
        onec = k.sb("onec", [128, 1], F32)
        nA = k.sb("nA", [128, 12], F32)
        k.V([ident], [identb], lambda e: e.tensor_copy(out=identb.a, in_=ident.a))
        k.V([], [ones], lambda e: e.memset(ones.a, 1.0))
        k.V([], [epsc], lambda e: e.memset(epsc.a, EPS))
        k.V([], [eps128], lambda e: e.memset(eps128.a, EPS * 128.0))
        k.V([], [onec], lambda e: e.memset(onec.a, 1.0))
        k.A([alog], [nA], lambda e: e.activation(out=nA.a, in_=alog.a, func=AF.Exp))
        k.V([nA], [nA], lambda e: e.tensor_scalar(out=nA.a, in0=nA.a, scalar1=-1.0, scalar2=None, op0=ALU.mult))

        pbig_l = [k.ps("pb%d" % i, [128, 512], F32) for i in range(2)]
        pbig = RR(pbig_l)
        ptr = RR([k.ps("ptr%d" % i, [128, 1024], BF16) for i in range(2)])
        pqb = [k.ps("pqb%d" % i, [128, 512], F32) for i in range(4)]
        pq = RR([TlView(pqb[i], pqb[i].a[:, qd * 128:(qd + 1) * 128]) for qd in range(4) for i in range(4)])
        pq3 = RR([TlView(pqb[i], pqb[i].a[:, qd * 128:(qd + 1) * 128]) for qd in range(4) for i in range(2)])
        pacc = [pqb[2], pqb[3]]

        xrow = k.sb("xrow", [128, D], F32)
        hnT = k.sb("hnT", [128, KD, TT + 2], BF16)
        wblk_l = [k.sb("wblk%d" % i, [128, WBE], BF16) for i in range(2)]
        wblk = RR(wblk_l)
        wlo = RR([Tl(wblk_l[i].a[:, 0:WBE // 2], "wlo%d" % i) for i in range(2)])
        bnc.extend([Tl(wblk_l[i].a[:, WBE // 2:WBE], "bnc%d" % i) for i in range(2)])
        ssq = RR([k.sb("ssq%d" % i, [128, 1], F32) for i in range(4)])
        rstd = RR([k.sb("rstd%d" % i, [128, 1], F32) for i in range(4)])
        big = k.sb("big", [128, 22528], BF16)
        t512 = RR([k.sb("t512_%d" % i, [128, 516], F32) for i in range(5)])

        def norm_transpose(xr, hb, hba, col0, nwc, tgt, ncols=128, srccol=0):
            sq = ssq.get()
            rs = rstd.get()
            k.V([], [sq], lambda e: e.memset(sq.a, 0.0))
            k.A([xr, sq], [hb, sq], lambda e: e.activation(out=hba, in_=xr.a, func=AF.Square, accum_out=sq.a))
            k.A([sq, epsc], [rs], lambda e: e.activation(out=rs.a, in_=sq.a, func=AF.Sqrt, bias=epsc.a[:, 0:1], scale=1.0 / D))
            k.V([rs], [rs], lambda e: e.reciprocal(out=rs.a, in_=rs.a))
            k.A([xr, rs], [hb], lambda e: e.activation(out=hba, in_=xr.a, func=AF.Copy, scale=rs.a[:, 0:1]))
            for kg in range(KD // 4):
                pt = ptr.get()
                for qd in range(4):
                    kd = kg * 4 + qd
                    k.P([hb, identb], [pt], lambda e: e.transpose(pt.a[:, qd * 128:(qd + 1) * 128], hba[:, kd * 128:(kd + 1) * 128], identb.a))
                src = pt.a[:, 0:512].rearrange("p (q c) -> p q c", q=4)[:, :, srccol:srccol + ncols]
                dstv = tgt.a[:, kg * 4:kg * 4 + 4, col0:col0 + ncols]
                sc = nwc.a[:, kg * 4:kg * 4 + 4].unsqueeze(2).to_broadcast([128, 4, ncols])
                k.V([pt, nwc], [tgt], lambda e: e.tensor_tensor(out=dstv, in0=src, in1=sc, op=ALU.mult))

        def conv_silu(pb, ci, dst_tl, dst_ap, K, hal, cwt, cbt):
            pre = t512.get()
            acc = t512.get()
            H = K - 1
            W = TT
            k.A([hal], [pre], lambda e: e.activation(out=pre.a[:, 0:H], in_=hal.a[:, ci, 0:H], func=AF.Copy))
            k.A([pb], [pre], lambda e: e.activation(out=pre.a[:, H:H + W], in_=pb.a[:, 0:W], func=AF.Copy))
            k.V([pre, cwt, cbt], [acc], lambda e: e.tensor_scalar(out=acc.a[:, 0:W], in0=pre.a[:, H:H + W], scalar1=cwt.a[:, ci, K - 1:K], scalar2=cbt.a[:, ci:ci + 1], op0=ALU.mult, op1=ALU.add))
            for j in range(K - 1):
                k.V([pre, cwt, acc], [acc], lambda e: e.scalar_tensor_tensor(out=acc.a[:, 0:W], in0=pre.a[:, j:j + W], scalar=cwt.a[:, ci, j:j + 1], in1=acc.a[:, 0:W], op0=ALU.mult, op1=ALU.add))
            k.A([pre], [hal], lambda e: e.activation(out=hal.a[:, ci, 0:H], in_=pre.a[:, W:W + H], func=AF.Copy))
            k.A([acc], [dst_tl], lambda e: e.activation(out=dst_ap, in_=acc.a[:, 0:W], func=AF.Silu))

        def mm(out_tl, out_ap, l_tl, l_ap, r_tl, r_ap, start=True, stop=True):
            k.P([l_tl, r_tl], [out_tl], lambda e: e.matmul(out_ap, lhsT=l_ap, rhs=r_ap, start=start, stop=stop))

        es1 = contextlib.ExitStack()
        with es1:
            k.es = es1
            sv = big.a.bitcast(F32)

            def slot(i):
                return sv[:, i * TT:(i + 1) * TT]
            sstore = k.sb("sstore", [128, 2, TT], F32)
            zst = k.sb("zst", [128, 4, TT], BF16)
            halo = k.sb("halo", [128, 20, 3], F32)
            k.V([], [halo], lambda e: e.memset(halo.a, 0.0))
            o_g = k.sb("o_g", [128, 4, TT], BF16)
            o_s = k.sb("o_s", [128, 4, TT], BF16)
            yT = k.sb("yT", [128, 8, TT], BF16)
            hb1_ap = yT.a.rearrange("p b c -> p (b c)")[:, 0:D]
            S_g = [k.sb("S_g%d" % h, [128, 128], F32) for h in range(4)]
            S_s = [k.sb("S_s%d" % h, [128, 64], F32) for h in range(8)]
            for s_ in S_g + S_s:
                k.V([], [s_], lambda e: e.memset(s_.a, 0.0))
            mhp = [RR([k.sb("mh%d_%d" % (h, i), [128, 128], F32) for i in range(8)]) for h in range(4)]
            mlp = [RR([k.sb("ml%d_%d" % (h, i), [128, 128], F32) for i in range(5)]) for h in range(4)]
            mcb = [k.sb("mcb%d" % i, [128, 128], F32) for i in range(2)]
            sm12 = RR([k.sb("sm%d" % i, [128, 16], F32) for i in range(24)])
            ktok = [k.sb("ktok%d" % h, [128, 128], F32) for h in range(4)]
            vtok = [k.sb("vtok%d" % h, [128, 128], F32) for h in range(4)]
            btok = [k.sb("btok%d" % g, [128, 128], F32) for g in range(2)]
            xtok = k.sb("xtok", [128, 512], F32)

            def proj_block(cb, pb):
                wb = wlo.get()
                wv = wb.a[:, 0:KD * 128].rearrange("p (k c) -> p k c", k=KD)
                if t == 0:
                    k.dma("pool", wb, wv, None, w1_in[cb])
                    k.dma("sp", w1s, w1s.a[cb], wb, wb.a[:, 0:KD * 128])
                else:
                    k.dma("sp", wb, wb.a[:, 0:KD * 128], w1s, w1s.a[cb])
                for kd in range(KD):
                    k.P([wb, hnT], [pb], lambda e: e.matmul(pb.a, lhsT=wv[:, kd, :], rhs=hnT.a[:, kd, 0:TT], start=(kd == 0), stop=(kd == KD - 1)))
                return pb

            def l2norm(src_tl, src_ap, dst_tl, dst_ap, scale, bias_tl, pb):
                sq = t512.get()
                k.A([src_tl], [sq], lambda e: e.activation(out=sq.a[:, 0:TT], in_=src_ap, func=AF.Square))
                mm(pb, pb.a, ones, ones.a, sq, sq.a[:, 0:TT])
                rt = t512.get()
                k.A([pb, bias_tl], [rt], lambda e: e.activation(out=rt.a[:, 0:TT], in_=pb.a, func=AF.Sqrt, bias=bias_tl.a[:, 0:1], scale=scale))
                k.V([rt], [rt], lambda e: e.reciprocal(out=rt.a[:, 0:TT], in_=rt.a[:, 0:TT]))
                k.V([src_tl, rt], [dst_tl], lambda e: e.tensor_tensor(out=dst_ap, in0=src_ap, in1=rt.a[:, 0:TT], op=ALU.mult))

            def transpose_f(dst_tl, dst_ap, src_tl, src_ap):
                p = pq.get()
                k.P([src_tl, ident], [p], lambda e: e.transpose(p.a, src_ap, ident.a))
                k.A([p], [dst_tl], lambda e: e.activation(out=dst_ap, in_=p.a, func=AF.Copy))

            for t in range(NT):
                for s in range(4):
                    k.dma("sp", xrow, xrow.a, None, x_in[t * TT + s * 128: t * TT + (s + 1) * 128, :])
                    norm_transpose(xrow, yT, hb1_ap, s * 128, n1c, hnT)
                jobs = []
                for hh in range(4):
                    for r in range(4):
                        jobs.append(("g", hh, r))
                for sbk in range(12):
                    jobs.append(("s", sbk, 0))

                def post(job, pb, ji):
                    kind, a_, r = job
                    if kind == "g":
                        hh = a_
                        if r == 3:
                            k.A([pb], [big], lambda e: e.activation(out=slot(hh * 4 + 3), in_=pb.a, func=AF.Silu))
                        elif r == 2:
                            conv_silu(pb, hh * 3 + r, big, slot(hh * 4 + 2), 4, halo, cw, cbias)
                        else:
                            tmp = t512.get()
                            conv_silu(pb, hh * 3 + r, tmp, tmp.a[:, 0:TT], 4, halo, cw, cbias)
                            if r == 0:
                                l2norm(tmp, tmp.a[:, 0:TT], big, slot(hh * 4 + 0), 128.0, eps128, pqb[ji % 2])
                            else:
                                l2norm(tmp, tmp.a[:, 0:TT], big, slot(hh * 4 + 1), 1.0, epsc, pqb[ji % 2])
                    else:
                        sbk = a_
                        if sbk < 4:
                            k.A([pb], [zst], lambda e: e.activation(out=zst.a[:, sbk, :], in_=pb.a, func=AF.Silu))
                        elif sbk < 8:
                            conv_silu(pb, 12 + (sbk - 4), big, slot(16 + sbk - 4), 4, halo, cw, cbias)
                        elif sbk < 10:
                            conv_silu(pb, 12 + (sbk - 4), big, slot(20 + sbk - 8), 4, halo, cw, cbias)
                        else:
                            conv_silu(pb, 12 + (sbk - 4), sstore, sstore.a[:, sbk - 10, :], 4, halo, cw, cbias)

                prev = None
                for ji, job in enumerate(jobs):
                    cb = (job[1] * 4 + job[2]) if job[0] == "g" else 16 + job[1]
                    pb = proj_block(cb, pbig_l[ji % 2])
                    if prev is not None:
                        post(*prev)
                    prev = (job, pb, ji)
                post(*prev)
                if t >= 1:
                    emit_conv((len(conv_jobs) + NT - 2) // max(NT - 1, 1))
                for c4 in range(4):
                    cs = slice(c4 * 128, (c4 + 1) * 128)
                    psm = pq.get()
                    for kd in range(KD):
                        k.P([hnT, wsm], [psm], lambda e: e.matmul(psm.a[:, 0:16], lhsT=hnT.a[:, kd, cs], rhs=wsm.a[:, kd, :], start=(kd == 0), stop=(kd == KD - 1)))
                    sm = sm12.get()
                    k.A([psm], [sm], lambda e: e.activation(out=sm.a, in_=psm.a[:, 0:16], func=AF.Copy))
                    xs_ = sm12.get(); ax = sm12.get(); ex = sm12.get(); sp = sm12.get(); G = sm12.get(); beta = sm12.get()
                    k.V([sm, dtb], [xs_], lambda e: e.tensor_tensor(out=xs_.a[:, 0:12], in0=sm.a[:, 4:16], in1=dtb.a, op=ALU.add))
                    k.A([xs_], [ax], lambda e: e.activation(out=ax.a[:, 0:12], in_=xs_.a[:, 0:12], func=AF.Abs))
                    k.A([ax], [ex], lambda e: e.activation(out=ex.a[:, 0:12], in_=ax.a[:, 0:12], func=AF.Exp, scale=-1.0))
                    k.A([ex, onec], [ex], lambda e: e.activation(out=ex.a[:, 0:12], in_=ex.a[:, 0:12], func=AF.Ln, bias=onec.a[:, 0:1], scale=1.0))
                    k.V([xs_, ex], [sp], lambda e: e.scalar_tensor_tensor(out=sp.a[:, 0:12], in0=xs_.a[:, 0:12], scalar=0.0, in1=ex.a[:, 0:12], op0=ALU.max, op1=ALU.add))
                    k.V([sp, nA], [G], lambda e: e.tensor_tensor(out=G.a[:, 0:12], in0=sp.a[:, 0:12], in1=nA.a, op=ALU.mult))
                    k.A([sm], [beta], lambda e: e.activation(out=beta.a[:, 0:4], in_=sm.a[:, 0:4], func=AF.Sigmoid))
                    pc = pq.get()
                    mm(pc, pc.a[:, 0:12], triU, triU.a, G, G.a[:, 0:12])
                    mm(pc, pc.a[:, 16:28], ones, ones.a, G, G.a[:, 0:12])
                    gc = sm12.get(); gl = sm12.get(); eg = sm12.get(); kdc = sm12.get(); dbc = sm12.get(); bg = sm12.get(); dif = sm12.get()
                    k.A([pc], [gc], lambda e: e.activation(out=gc.a[:, 0:12], in_=pc.a[:, 0:12], func=AF.Copy))
                    k.A([pc], [gl], lambda e: e.activation(out=gl.a[:, 0:12], in_=pc.a[:, 16:28], func=AF.Copy))
                    k.A([gc], [eg], lambda e: e.activation(out=eg.a[:, 0:12], in_=gc.a[:, 0:12], func=AF.Exp))
                    k.V([gl, gc], [dif], lambda e: e.tensor_tensor(out=dif.a[:, 0:12], in0=gl.a[:, 0:12], in1=gc.a[:, 0:12], op=ALU.subtract))
                    k.A([dif], [kdc], lambda e: e.activation(out=kdc.a[:, 0:12], in_=dif.a[:, 0:12], func=AF.Exp))
                    k.A([gl], [dbc], lambda e: e.activation(out=dbc.a[:, 0:12], in_=gl.a[:, 0:12], func=AF.Exp))
                    k.V([beta, eg], [bg], lambda e: e.tensor_tensor(out=bg.a[:, 0:4], in0=beta.a[:, 0:4], in1=eg.a[:, 0:4], op=ALU.mult))
                    DMP = (t == 0 and c4 < 2)
                    if DMP:
                        tg = "_c%d" % c4
                        dump("sm" + tg, sm, sm.a, [128, 16]); dump("sp" + tg, sp, sp.a[:, 0:12], [128, 12]); dump("G" + tg, G, G.a[:, 0:12], [128, 12])
                        dump("gc" + tg, gc, gc.a[:, 0:12], [128, 12]); dump("gl" + tg, gl, gl.a[:, 0:12], [128, 12]); dump("beta" + tg, beta, beta.a[:, 0:4], [128, 4])
                        dump("kdc" + tg, kdc, kdc.a[:, 0:12], [128, 12]); dump("dbc" + tg, dbc, dbc.a[:, 0:12], [128, 12])
                        if c4 == 0:
                            for i_ in range(22):
                                dump("slot%d" % i_, big, slot(i_), [128, TT])
                            dump("sstore", sstore, sstore.a, [128, 2, TT])

                    def gdn_head(hh):
                        mh, ml = mhp[hh], mlp[hh]
                        qT = slot(hh * 4 + 0)[:, cs]
                        kT = slot(hh * 4 + 1)[:, cs]
                        vT = slot(hh * 4 + 2)[:, cs]
                        p1 = pq.get()
                        k.P([big, ident], [p1], lambda e: e.transpose(p1.a, kT, ident.a))
                        yield
                        k.A([p1], [ktok[hh]], lambda e: e.activation(out=ktok[hh].a, in_=p1.a, func=AF.Copy))
                        p2 = pq.get()
                        k.P([big, ident], [p2], lambda e: e.transpose(p2.a, vT, ident.a))
                        yield
                        k.A([p2], [vtok[hh]], lambda e: e.activation(out=vtok[hh].a, in_=p2.a, func=AF.Copy))
                        gb = mh.get()
                        k.V([ones, G], [gb], lambda e: e.tensor_scalar(out=gb.a, in0=ones.a, scalar1=G.a[:, hh:hh + 1], scalar2=None, op0=ALU.mult))
                        prow = pq.get()
                        mm(prow, prow.a, gb, gb.a, triU, triU.a)
                        yield
                        X = mh.get(); GT = mh.get(); XL = mh.get(); GL = mh.get(); Eg = mh.get()
                        k.V([prow, gc, negU], [X], lambda e: e.scalar_tensor_tensor(out=X.a, in0=prow.a, scalar=gc.a[:, hh:hh + 1], in1=negU.a, op0=ALU.subtract, op1=ALU.min))
                        k.A([X], [GT], lambda e: e.activation(out=GT.a, in_=X.a, func=AF.Exp))
                        k.V([prow, gc, posL], [XL], lambda e: e.scalar_tensor_tensor(out=XL.a, in0=prow.a, scalar=gc.a[:, hh:hh + 1], in1=posL.a, op0=ALU.subtract, op1=ALU.max))
                        k.A([XL], [GL], lambda e: e.activation(out=GL.a, in_=XL.a, func=AF.Exp, scale=-1.0))
                        k.A([prow], [Eg], lambda e: e.activation(out=Eg.a, in_=prow.a, func=AF.Exp))
                        pqk = pq.get()
                        mm(pqk, pqk.a, big, kT, big, qT)
                        yield
                        attnT = ml.get()
                        k.V([pqk, GT], [attnT], lambda e: e.tensor_tensor(out=attnT.a, in0=pqk.a, in1=GT.a, op=ALU.mult))
                        vb = ml.get(); kbg = ml.get(); kdec = ml.get(); qdec = ml.get()
                        k.V([vtok[hh], beta], [vb], lambda e: e.tensor_scalar(out=vb.a, in0=vtok[hh].a, scalar1=beta.a[:, hh:hh + 1], scalar2=None, op0=ALU.mult))
                        k.V([ktok[hh], bg], [kbg], lambda e: e.tensor_scalar(out=kbg.a, in0=ktok[hh].a, scalar1=bg.a[:, hh:hh + 1], scalar2=None, op0=ALU.mult))
                        k.V([ktok[hh], kdc], [kdec], lambda e: e.tensor_scalar(out=kdec.a, in0=ktok[hh].a, scalar1=kdc.a[:, hh:hh + 1], scalar2=None, op0=ALU.mult))
                        k.V([big, Eg], [qdec], lambda e: e.tensor_tensor(out=qdec.a, in0=qT, in1=Eg.a, op=ALU.mult))
                        pkk = pq.get()
                        mm(pkk, pkk.a, big, kT, big, kT)
                        yield
                        Am = mh.get()
                        k.V([pkk, beta, GL], [Am], lambda e: e.scalar_tensor_tensor(out=Am.a, in0=pkk.a, scalar=beta.a[:, hh:hh + 1], in1=GL.a, op0=ALU.mult, op1=ALU.mult))
                        pN = pq.get()
                        k.P([Am, ident], [pN], lambda e: e.transpose(pN.a, Am.a, ident.a))
                        yield
                        Nm = mh.get()
                        k.A([pN], [Nm], lambda e: e.activation(out=Nm.a, in_=pN.a, func=AF.Copy))
                        R = mh.get()
                        k.V([ident, Nm], [R], lambda e: e.tensor_tensor(out=R.a, in0=ident.a, in1=Nm.a, op=ALU.subtract))
                        Pm, PTm = Nm, Am
                        for lvl in range(1, 7):
                            pPT = pq.get()
                            mm(pPT, pPT.a, Pm, Pm.a, PTm, PTm.a)
                            yield
                            nPT = mh.get()
                            k.A([pPT], [nPT], lambda e: e.activation(out=nPT.a, in_=pPT.a, func=AF.Copy))
                            if lvl < 6:
                                pP = pq.get()
                                mm(pP, pP.a, PTm, PTm.a, Pm, Pm.a)
                                yield
                                nP = mh.get()
                                k.V([pP], [nP], lambda e: e.tensor_copy(out=nP.a, in_=pP.a))
                            pR = pq.get()
                            mm(pR, pR.a, nPT, nPT.a, R, R.a)
                            yield
                            nR = mh.get()
                            k.V([pR, R], [nR], lambda e: e.tensor_tensor(out=nR.a, in0=pR.a, in1=R.a, op=ALU.add))
                            R = nR
                            PTm = nPT
                            if lvl < 6:
                                Pm = nP
                        pw = pq.get()
                        mm(pw, pw.a, kbg, kbg.a, R, R.a)
                        yield
                        wTn = mh.get()
                        k.A([pw], [wTn], lambda e: e.activation(out=wTn.a, in_=pw.a, func=AF.Copy, scale=-1.0))
                        pv = pq.get()
                        mm(pv, pv.a, R, R.a, vb, vb.a, start=True, stop=False)
                        mm(pv, pv.a, wTn, wTn.a, S_g[hh], S_g[hh].a, start=False, stop=True)
                        yield
                        vnew = mh.get()
                        k.V([pv], [vnew], lambda e: e.tensor_copy(out=vnew.a, in_=pv.a))
                        po = pq.get()
                        mm(po, po.a, S_g[hh], S_g[hh].a, qdec, qdec.a, start=True, stop=False)
                        mm(po, po.a, vnew, vnew.a, attnT, attnT.a, start=False, stop=True)
                        yield
                        k.A([po], [o_g], lambda e: e.activation(out=o_g.a[:, hh, cs], in_=po.a, func=AF.Copy))
                        pS = pq.get()
                        mm(pS, pS.a, kdec, kdec.a, vnew, vnew.a)
                        yield
                        k.V([S_g[hh], dbc, pS], [S_g[hh]], lambda e: e.scalar_tensor_tensor(out=S_g[hh].a, in0=S_g[hh].a, scalar=dbc.a[:, hh:hh + 1], in1=pS.a, op0=ALU.mult, op1=ALU.add))

                    lockstep([gdn_head(hh) for hh in range(4)])

                    for g2 in range(2):
                        transpose_f(btok[g2], btok[g2].a, big, slot(20 + g2)[:, cs])
                    for blk in range(4):
                        transpose_f(xtok, xtok.a[:, blk * 128:(blk + 1) * 128], big, slot(16 + blk)[:, cs])
                    for g2 in range(2):
                        p = pq.get()
                        mm(p, p.a, big, slot(20 + g2)[:, cs], sstore, sstore.a[:, g2, cs])
                        k.A([p], [mcb[g2]], lambda e: e.activation(out=mcb[g2].a, in_=p.a, func=AF.Copy))

                    def ssd_head(h, pos_):
                        g2 = h // 4
                        col = 4 + h
                        mh = mhp[h % 4]
                        gb = mh.get()
                        k.V([ones, G], [gb], lambda e: e.tensor_scalar(out=gb.a, in0=ones.a, scalar1=G.a[:, col:col + 1], scalar2=None, op0=ALU.mult))
                        prow = pq.get()
                        mm(prow, prow.a, gb, gb.a, triU, triU.a)
                        yield
                        X = mh.get(); GT = mh.get(); Eg = mh.get()
                        k.V([prow, gc, negU], [X], lambda e: e.scalar_tensor_tensor(out=X.a, in0=prow.a, scalar=gc.a[:, col:col + 1], in1=negU.a, op0=ALU.subtract, op1=ALU.min))
                        k.A([X], [GT], lambda e: e.activation(out=GT.a, in_=X.a, func=AF.Exp))
                        k.A([prow], [Eg], lambda e: e.activation(out=Eg.a, in_=prow.a, func=AF.Exp))
                        attnT = mh.get(); qdec = mh.get(); kdec = mh.get(); vh = mh.get()
                        k.V([mcb[g2], GT], [attnT], lambda e: e.tensor_tensor(out=attnT.a, in0=mcb[g2].a, in1=GT.a, op=ALU.mult))
                        k.V([sstore, Eg], [qdec], lambda e: e.tensor_tensor(out=qdec.a, in0=sstore.a[:, g2, cs], in1=Eg.a, op=ALU.mult))
                        k.V([btok[g2], kdc], [kdec], lambda e: e.tensor_scalar(out=kdec.a, in0=btok[g2].a, scalar1=kdc.a[:, col:col + 1], scalar2=None, op0=ALU.mult))
                        k.V([xtok, sp], [vh], lambda e: e.tensor_scalar(out=vh.a[:, 0:64], in0=xtok.a[:, h * 64:(h + 1) * 64], scalar1=sp.a[:, col:col + 1], scalar2=None, op0=ALU.mult))
                        half = slice((h % 2) * 64, (h % 2) * 64 + 64)
                        mm(pos_, pos_.a[half, :], S_s[h], S_s[h].a, qdec, qdec.a, start=True, stop=False)
                        mm(pos_, pos_.a[half, :], vh, vh.a[:, 0:64], attnT, attnT.a, start=False, stop=True)
                        pS = pq.get()
                        mm(pS, pS.a[:, 0:64], kdec, kdec.a, vh, vh.a[:, 0:64])
                        yield
                        k.V([S_s[h], dbc, pS], [S_s[h]], lambda e: e.scalar_tensor_tensor(out=S_s[h].a, in0=S_s[h].a, scalar=dbc.a[:, col:col + 1], in1=pS.a[:, 0:64], op0=ALU.mult, op1=ALU.add))

                    for g2 in range(2):
                        pp = [pq.get(), pq.get()]
                        lockstep([ssd_head(g2 * 4 + i, pp[i // 2]) for i in range(4)])
                        for i2 in range(2):
                            k.A([pp[i2]], [o_s], lambda e: e.activation(out=o_s.a[:, g2 * 2 + i2, cs], in_=pp[i2].a, func=AF.Copy))

                for hh in range(4):
                    sq = t512.get()
                    k.A([o_g], [sq], lambda e: e.activation(out=sq.a[:, 0:TT], in_=o_g.a[:, hh, :], func=AF.Square))
                    pb = pbig.get()
                    mm(pb, pb.a, ones, ones.a, sq, sq.a[:, 0:TT])
                    rt = t512.get()
                    k.A([pb, epsc], [rt], lambda e: e.activation(out=rt.a[:, 0:TT], in_=pb.a, func=AF.Sqrt, bias=epsc.a[:, 0:1], scale=1.0 / 128))
                    k.V([rt], [rt], lambda e: e.reciprocal(out=rt.a[:, 0:TT], in_=rt.a[:, 0:TT]))
                    k.V([o_g, rt], [rt], lambda e: e.tensor_tensor(out=rt.a[:, 0:TT], in0=o_g.a[:, hh, :], in1=rt.a[:, 0:TT], op=ALU.mult))
                    k.V([rt, gdnnw, big], [yT], lambda e: e.scalar_tensor_tensor(out=yT.a[:, hh, :], in0=rt.a[:, 0:TT], scalar=gdnnw.a[:, 0:1], in1=slot(hh * 4 + 3), op0=ALU.mult, op1=ALU.mult))
                for g2 in range(2):
                    ys = []
                    pb = pbig.get()
                    for bi in range(2):
                        blk = g2 * 2 + bi
                        y0 = t512.get()
                        k.V([big, dcol, o_s], [y0], lambda e: e.scalar_tensor_tensor(out=y0.a[:, 0:TT], in0=slot(16 + blk), scalar=dcol.a[:, blk:blk + 1], in1=o_s.a[:, blk, :], op0=ALU.mult, op1=ALU.add))
                        k.V([y0, zst], [y0], lambda e: e.tensor_tensor(out=y0.a[:, 0:TT], in0=y0.a[:, 0:TT], in1=zst.a[:, blk, :], op=ALU.mult))
                        sq = t512.get()
                        k.A([y0], [sq], lambda e: e.activation(out=sq.a[:, 0:TT], in_=y0.a[:, 0:TT], func=AF.Square))
                        mm(pb, pb.a, ones, ones.a, sq, sq.a[:, 0:TT], start=(bi == 0), stop=(bi == 1))
                        ys.append(y0)
                    rt = t512.get()
                    k.A([pb, epsc], [rt], lambda e: e.activation(out=rt.a[:, 0:TT], in_=pb.a, func=AF.Sqrt, bias=epsc.a[:, 0:1], scale=1.0 / 256))
                    k.V([rt], [rt], lambda e: e.reciprocal(out=rt.a[:, 0:TT], in_=rt.a[:, 0:TT]))
                    for bi in range(2):
                        blk = g2 * 2 + bi
                        y0 = ys[bi]
                        k.V([y0, ssmnw, rt], [yT], lambda e: e.scalar_tensor_tensor(out=yT.a[:, 4 + blk, :], in0=y0.a[:, 0:TT], scalar=ssmnw.a[:, blk:blk + 1], in1=rt.a[:, 0:TT], op0=ALU.mult, op1=ALU.mult))
                if t == 0:
                    dump("o_g", o_g, o_g.a, [128, 4, TT]); dump("o_s", o_s, o_s.a, [128, 4, TT])
                k.dma("sp", ybuf, ybuf.a[t].rearrange("(b p) c -> p b c", p=128), yT, yT.a)
                if dbg:
                    k.dma("sp", ydbg, ydbg.a[t].rearrange("(b p) c -> p b c", p=128), yT, yT.a)
                k.collective(ybuf, gath, ybuf.a[t], gath.a[t * 4096:(t + 1) * 4096, :], cfg["groups"], gath)
            emit_conv(len(conv_jobs))
            zt = t512.get()
            k.V([], [zt], lambda e: e.memset(zt.a, 0.0))
            zb16 = zt.a.bitcast(BF16)[:, 0:TT]
            for i in range(32):
                k.dma("sp", gath, gath.a[NT * 4096 + i * 128: NT * 4096 + (i + 1) * 128, :], zt, zb16, semtl=zt)
            k.barrier()
        k.es = es

        es3 = contextlib.ExitStack()
        if cfg.get("stop_after", 3) == 1:
            k.finish([ydbg, gath] + dumped)
            k.barrier()
            return nc
        with es3:
            k.es = es3
            rowb = k.sb("rowb", [128, D], F32)
            hb3 = k.sb("hb3", [128, D], BF16)
            halo2 = k.sb("halo2", [128, FB, 2], F32)
            piece = RR([k.sb("piece%d" % i, [128, 512], F32) for i in range(8)])
            ssq3 = k.sb("ssqf", [128, 4, NG], F32)
            actT = big.a.rearrange("p (f c) -> p f c", c=TT)
            y_sb = big.a[:, 0:KM * 640].rearrange("p (k c) -> p k c", k=KM)
            accs = [pbig_l[0], pbig_l[1], pacc[0], pacc[1]]
            pbB = RR([pbig_l[0], pbig_l[1], pacc[0], pacc[1]])
            WSL = WBE // 512
            for m in range(NT3):
                g0 = 0 if m == 0 else 1
                for kd in range(KM):
                    if m == 0:
                        k.dma("pool", big, y_sb[:, kd, 0:128], gath, gath.a.rearrange("r (q c) -> (r q) c", c=128), kind="ind",
                              in_off=bass.IndirectOffsetOnAxis(ap=idx.a[:, kd:kd + 1], axis=0))
                    k.dma("pool", big, y_sb[:, kd, 128:640], gath, gath.a[:, :], kind="ind",
                          in_off=bass.IndirectOffsetOnAxis(ap=idx.a[:, (m + 1) * KM + kd:(m + 1) * KM + kd + 1], axis=0))
                for g in range(NG):
                    for hf in range(2):
                        wb = wblk.get()
                        wv = wb.a.rearrange("p (k c) -> p k c", k=KM)
                        k.dma("sp", wb, wb.a[:, 0:KM * 256], wos, wos.a[g * 2 + hf])
                        for gi in range(g0, 5):
                            xp = piece.get()
                            r0 = (0 if gi == 0 else 128 + m * TT + (gi - 1) * 128)
                            c0 = g * 512 + hf * 256
                            k.dma("pool", xp, xp.a[:, 0:256], None, xq_in[r0:r0 + 128, c0:c0 + 256])
                            pb = pbig.get()
                            for kd in range(KM):
                                k.P([big, wb], [pb], lambda e: e.matmul(pb.a[:, 0:256], lhsT=y_sb[:, kd, gi * 128:(gi + 1) * 128], rhs=wv[:, kd, :], start=(kd == 0), stop=(kd == KM - 1)))
                            k.V([pb, xp], [xp], lambda e: e.tensor_tensor(out=xp.a[:, 0:256], in0=pb.a[:, 0:256], in1=xp.a[:, 0:256], op=ALU.add))
                            k.dma("pool", hbuf, hbuf.a[r0:r0 + 128, c0:c0 + 256], xp, xp.a[:, 0:256])
                            if dbg:
                                k.dma("pool", hdbg, hdbg.a[r0:r0 + 128, c0:c0 + 256], xp, xp.a[:, 0:256])
                for gi in range(g0, 5):
                    r0 = (0 if gi == 0 else 128 + m * TT + (gi - 1) * 128)
                    k.dma("sp", xrow, xrow.a, hbuf, hbuf.a[r0:r0 + 128, :])
                    if gi == 0:
                        norm_transpose(xrow, hb3, hb3.a, 0, n2c, hnT, ncols=2, srccol=126)
                    else:
                        norm_transpose(xrow, hb3, hb3.a, 2 + (gi - 1) * 128, n2c, hnT)
                k.V([], [ssq3], lambda e: e.memset(ssq3.a, 0.0))
                for hf in range(2):
                    for fl in range(FH):
                        fb = hf * FH + fl
                        wb = wblk.get()
                        wv = wb.a[:, 0:2 * KD * 128].rearrange("p (u k c) -> p u k c", u=2, k=KD)
                        k.dma("sp", wb, wb.a[:, 0:2 * KD * 128], wgs, wgs.a[fb])
                        pg = pbB.get()
                        for kd in range(KD):
                            k.P([wb, hnT], [pg], lambda e: e.matmul(pg.a, lhsT=wv[:, 0, kd, :], rhs=hnT.a[:, kd, 2:2 + TT], start=(kd == 0), stop=(kd == KD - 1)))
                        if m == 0:
                            ph = pq3.get()
                            for kd in range(KD):
                                k.P([wb, hnT], [ph], lambda e: e.matmul(ph.a[:, 0:2], lhsT=wv[:, 0, kd, :], rhs=hnT.a[:, kd, 0:2], start=(kd == 0), stop=(kd == KD - 1)))
                            k.A([ph], [halo2], lambda e: e.activation(out=halo2.a[:, fb, :], in_=ph.a[:, 0:2], func=AF.Copy))
                        pu = pbB.get()
                        for kd in range(KD):
                            k.P([wb, hnT], [pu], lambda e: e.matmul(pu.a, lhsT=wv[:, 1, kd, :], rhs=hnT.a[:, kd, 2:2 + TT], start=(kd == 0), stop=(kd == KD - 1)))
                        sg = t512.get()
                        conv_silu(pg, fb, sg, sg.a[:, 0:TT], 3, halo2, fcw, fcb)
                        k.V([sg, pu], [big], lambda e: e.tensor_tensor(out=actT[:, fl, :], in0=sg.a[:, 0:TT], in1=pu.a, op=ALU.mult))
                    for g in range(NG):
                        nsl = (FH + WSL - 1) // WSL
                        hps = []
                        for i in range(4):
                            r0 = m * TT + i * 128
                            hp = piece.get()
                            if hf == 0:
                                k.dma("pool", hp, hp.a, hbuf, hbuf.a[128 + r0:128 + r0 + 128, g * 512:(g + 1) * 512])
                            else:
                                k.dma("pool", hp, hp.a, out_d, out_d.a[r0:r0 + 128, g * 512:(g + 1) * 512])
                            hps.append(hp)
                        for sl in range(nsl):
                            f0 = sl * WSL
                            nf = min(WSL, FH - f0)
                            wb = wblk.get()
                            wv = wb.a.rearrange("p (f c) -> p f c", c=512)
                            r0 = (hf * FH + f0) * 128
                            si = (hf * NG + g) * NSL + sl
                            k.dma("sp", wb, wb.a[:, 0:nf * 512], wds, wds.a[si][:, 0:nf * 512])
                            for i in range(4):
                                for f in range(nf):
                                    k.P([big, wb], [accs[i]], lambda e: e.matmul(accs[i].a, lhsT=actT[:, f0 + f, i * 128:(i + 1) * 128], rhs=wv[:, f, :], start=(f0 + f == 0), stop=(f0 + f == FH - 1)))
                        for i in range(4):
                            r0 = m * TT + i * 128
                            hp = hps[i]
                            k.V([accs[i], hp], [hp], lambda e: e.tensor_tensor(out=hp.a, in0=accs[i].a, in1=hp.a, op=ALU.add))
                            if hf == 1:
                                jk = t512.get()
                                k.A([hp, ssq3], [jk, ssq3], lambda e: e.activation(out=jk.a[:, 0:512], in_=hp.a, func=AF.Square, accum_out=ssq3.a[:, i, g:g + 1]))
                            k.dma("pool", out_d, out_d.a[r0:r0 + 128, g * 512:(g + 1) * 512], hp, hp.a)
                k.dma("sp", xrow, xrow.a, None, nfb_in.partition_broadcast(128))
                for i in range(4):
                    r0 = m * TT + i * 128
                    tot = ssq.get(); rs = rstd.get()
                    k.V([ssq3], [tot], lambda e: e.tensor_reduce(out=tot.a, in_=ssq3.a[:, i, :], axis=mybir.AxisListType.X, op=ALU.add))
                    k.A([tot, epsc], [rs], lambda e: e.activation(out=rs.a, in_=tot.a, func=AF.Sqrt, bias=epsc.a[:, 0:1], scale=1.0 / D))
                    k.V([rs], [rs], lambda e: e.reciprocal(out=rs.a, in_=rs.a))
                    k.dma("pool", rowb, rowb.a, out_d, out_d.a[r0:r0 + 128, :])
                    k.V([rowb, rs, xrow], [rowb], lambda e: e.scalar_tensor_tensor(out=rowb.a, in0=rowb.a, scalar=rs.a[:, 0:1], in1=xrow.a, op0=ALU.mult, op1=ALU.mult))
                    k.dma("pool", out_d, out_d.a[r0:r0 + 128, :], rowb, rowb.a)
            fin = [out_d]
            if dbg:
                fin += [ydbg, hdbg]
            k.finish(fin)
            k.barrier()
        k.es = es
    return nc


Q0, K0, V0, GATE0, BETA0, ALPHA0, Z0, XBC0, DT0 = 0, 2048, 4096, 6144, 8192, 8208, 8224, 10272, 14368
GROUPS = [[0, 1, 2, 3], [4, 5, 6, 7]]


def _masks():
    p = np.arange(128)[:, None]
    f = np.arange(128)[None, :]
    ident = (p == f).astype(np.float32)
    triU = (p <= f).astype(np.float32)
    negU = np.where(f >= p, 0.0, NEG).astype(np.float32)
    posL = np.where(p > f, 0.0, -NEG).astype(np.float32)
    return np.ascontiguousarray(np.stack([ident, triU, negU, posL]))


def prep_inputs(inp, cfg):
    L, D, DFF = cfg["L"], cfg["D"], cfg["DFF"]
    KD, FB, NG = D // 128, DFF // 128, D // 512
    LQ = L // 4
    NT, NT3 = L // TT, LQ // TT
    f32 = np.float32
    x = np.asarray(inp["x"], f32)
    w_in = np.asarray(inp["w_in"], f32)[0]
    w_out = np.asarray(inp["w_out"], f32)[0]
    gcw = np.asarray(inp["gdn_conv_w"], f32)[0]
    scw = np.asarray(inp["ssm_conv_w"], f32)[0]
    scb = np.asarray(inp["ssm_conv_b"], f32)[0]
    ar = np.arange(128)
    masks = _masks()
    wgate = np.asarray(inp["ffn_w_gate"], f32)[0].reshape(KD, 128, FB, 128).transpose(2, 1, 0, 3)
    wup = np.asarray(inp["ffn_w_up"], f32)[0].reshape(KD, 128, FB, 128).transpose(2, 1, 0, 3)
    wg = np.ascontiguousarray(np.stack([wgate, wup], axis=2))
    wd = np.ascontiguousarray(np.asarray(inp["ffn_w_down"], f32)[0])
    fcw = np.ascontiguousarray(np.asarray(inp["ffn_conv_w"], f32)[0].reshape(3, FB, 128).transpose(2, 1, 0))
    fcb = np.ascontiguousarray(np.asarray(inp["ffn_conv_b"], f32)[0].reshape(FB, 128).T)
    n1c = np.ascontiguousarray(np.asarray(inp["norm1_w"], f32)[0].reshape(KD, 128).T)
    n2c = np.ascontiguousarray(np.asarray(inp["norm2_w"], f32)[0].reshape(KD, 128).T)
    nfb = np.ascontiguousarray(np.asarray(inp["norm_f_w"], f32))
    gdnnw = np.ascontiguousarray(np.asarray(inp["gdn_norm_w"], f32)[0].reshape(128, 1))
    perm = []
    for r in range(4):
        for hh in range(4):
            perm.append(128 * (4 * r + hh) + ar)
        perm.append(2048 + 512 * r + np.arange(512))
    perm = np.concatenate(perm)
    wo = np.ascontiguousarray(w_out[perm].reshape(KM, 128, NG, 512).transpose(2, 1, 0, 3))
    maps = []
    for c in range(8):
        b, j = c // 4, c % 4
        cols = []
        for hh in range(4):
            H = 4 * j + hh
            cols += [Q0 + 128 * H + ar, K0 + 128 * H + ar, V0 + 128 * H + ar, GATE0 + 128 * H + ar]
        for zb in range(4):
            cols.append(Z0 + 512 * j + zb * 128 + ar)
        for blk in range(4):
            cols.append(XBC0 + 512 * j + blk * 128 + ar)
        for g in range(2):
            cols.append(XBC0 + 2048 + 128 * (2 * j + g) + ar)
        for g in range(2):
            cols.append(XBC0 + 3072 + 128 * (2 * j + g) + ar)
        cols = np.concatenate(cols)
        w1 = np.ascontiguousarray(w_in[:, cols].reshape(KD, 128, 28, 128).transpose(2, 1, 0, 3))
        small = np.concatenate([BETA0 + 4 * j + np.arange(4), ALPHA0 + 4 * j + np.arange(4), DT0 + 8 * j + np.arange(8)])
        wsm = np.ascontiguousarray(w_in[:, small].reshape(KD, 128, 16).transpose(1, 0, 2))
        cw = np.zeros((128, 20, 4), f32)
        cb = np.zeros((128, 20), f32)
        for hh in range(4):
            H = 4 * j + hh
            for r in range(3):
                cw[:, hh * 3 + r, :] = gcw[:, r * 2048 + 128 * H + ar].T
        for i in range(8):
            if i < 4:
                ch = 512 * j + i * 128 + ar
            elif i < 6:
                ch = 2048 + 128 * (2 * j + (i - 4)) + ar
            else:
                ch = 3072 + 128 * (2 * j + (i - 6)) + ar
            cw[:, 12 + i, :] = scw[:, ch].T
            cb[:, 12 + i] = scb[ch]
        dtb = np.concatenate([np.asarray(inp["gdn_dt_bias"], f32)[0][4 * j:4 * j + 4], np.asarray(inp["ssm_dt_bias"], f32)[0][8 * j:8 * j + 8]])
        alog = np.concatenate([np.asarray(inp["gdn_A_log"], f32)[0][4 * j:4 * j + 4], np.asarray(inp["ssm_A_log"], f32)[0][8 * j:8 * j + 8]])
        ssd_D = np.asarray(inp["ssm_D"], f32)[0]
        dcol = np.zeros((128, 4), f32)
        for blk in range(4):
            dcol[:64, blk] = ssd_D[8 * j + 2 * blk]
            dcol[64:, blk] = ssd_D[8 * j + 2 * blk + 1]
        ssmnw = np.ascontiguousarray(np.asarray(inp["ssm_norm_w"], f32)[0][512 * j:512 * (j + 1)].reshape(4, 128).T)
        xq = np.zeros((128 + LQ, D), f32)
        if j > 0:
            xq[:] = x[b, j * LQ - 128:(j + 1) * LQ]
        else:
            xq[128:] = x[b, 0:LQ]
        idx = np.zeros((128, (NT3 + 1) * KM), np.int32)
        for blk in range(NT3 + 1):
            if blk == 0:
                tile = j * NT3 - 1 if j > 0 else NT
            else:
                tile = j * NT3 + (blk - 1)
            for kd in range(KM):
                idx[:, blk * KM + kd] = tile * 4096 + kd * 128 + ar
                if blk == 0:
                    idx[:, kd] = idx[:, kd] * 4 + 3
        maps.append({
            "x": np.ascontiguousarray(x[b]), "xq": xq, "w1": w1, "wsm": wsm, "n1c": n1c, "n2c": n2c, "nfb": nfb,
            "cw": cw, "cb": cb, "dtb": np.ascontiguousarray(np.tile(dtb[None, :], (128, 1))),
            "alog": np.ascontiguousarray(np.tile(alog[None, :], (128, 1))), "dcol": dcol, "ssmnw": ssmnw, "gdnnw": gdnnw,
            "masks": masks, "wo": wo, "wg": wg, "wd": wd, "fcw": fcw, "fcb": fcb, "idx": idx,
        })
    return maps


def run(inp, cfg):
    nc = build_program(cfg)
    maps = prep_inputs(inp, cfg)
    res = run_bass_kernel_spmd(nc, maps, core_ids=list(range(8)))
    return res


def kernel(**inputs):
    cfg = {"L": 8192, "D": 4096, "DFF": 11008, "groups": GROUPS}
    res = run(inputs, cfg)
    L, D = cfg["L"], cfg["D"]
    LQ = L // 4
    out = np.zeros((2, L, D), np.float32)
    for c in range(8):
        b, j = c // 4, c % 4
        out[b, j * LQ:(j + 1) * LQ] = res.results[c]["out"]
    return out
```

```python
import bisect
import contextlib
import numpy as np
import concourse.bass as bass
import concourse.mybir as mybir
from concourse.bass_utils import run_bass_kernel_spmd

F32 = mybir.dt.float32
BF16 = mybir.dt.bfloat16
I32 = mybir.dt.int32
AF = mybir.ActivationFunctionType
ALU = mybir.AluOpType
EPS = 1e-6
NEG = -30000.0


class Tl:
    __slots__ = ("a", "name", "w", "r", "dsem", "dn", "multi")

    def __init__(self, a, name, multi=False):
        self.a = a
        self.name = name
        self.w = [] if multi else None
        self.r = {}
        self.dsem = {}
        self.dn = {}
        self.multi = multi


class TlView:
    def __init__(self, parent, a):
        object.__setattr__(self, "p", parent)
        object.__setattr__(self, "a", a)

    def __getattr__(self, n):
        return getattr(object.__getattribute__(self, "p"), n)

    def __setattr__(self, n, v):
        setattr(object.__getattribute__(self, "p"), n, v)


class KB:
    ENG = ("pe", "dve", "act", "pool", "sp")

    def __init__(self, nc, es):
        self.nc = nc
        self.es = es
        self.es_sem = es
        self.dma_tiles = []
        self.q = {"pe": nc.tensor, "dve": nc.vector, "act": nc.scalar, "pool": nc.gpsimd, "sp": nc.sync}
        self.sem = {e: es.enter_context(nc.semaphore("s_" + e)) for e in self.ENG}
        self.emitted = {e: 0 for e in self.ENG}
        self.last = {e: None for e in self.ENG}
        self.sigs = {e: [] for e in self.ENG}
        self.seen = {e: {} for e in self.ENG}
        self.nsem = 0
        self.uid = 0

    def sb(self, name, shape, dt):
        t = self.es.enter_context(self.nc.sbuf_tensor("sb_" + name, list(shape), dt))
        return Tl(t[:], name)

    def ps(self, name, shape, dt):
        t = self.es.enter_context(self.nc.psum_tensor("ps_" + name, list(shape), dt))
        return Tl(t[:], name)

    def dram(self, name, shape, dt, kind="Internal", multi=False):
        t = self.nc.dram_tensor(name, list(shape), dt, kind=kind)
        return Tl(t.ap(), name, multi=multi)

    def view(self, ap, name):
        return Tl(ap, name)

    def newsem(self, name):
        self.nsem += 1
        return self.es_sem.enter_context(self.nc.semaphore(name))

    def _setw(self, t, dep):
        if t.multi:
            t.w = [d for d in t.w if d[1] is not dep[1]] + [dep]
        else:
            t.w = dep
        t.r = {}

    def barrier(self):
        engs = ["pe", "dve", "act"]
        for e in engs:
            if self.last[e] is not None and (not self.sigs[e] or self.sigs[e][-1] < self.emitted[e]):
                self.last[e].then_inc(self.sem[e], 1)
                self.sigs[e].append(self.emitted[e])
        for e in self.ENG:
            for o in engs:
                if o != e and self.sigs[o]:
                    self._wait_sem(e, self.sem[o], len(self.sigs[o]))
            for tl in self.dma_tiles:
                for qc in tl.dsem:
                    self._wait_sem(e, tl.dsem[qc], tl.dn[qc])

    def _wait_sem(self, eng, sem, val):
        d = self.seen[eng]
        if d.get(sem.name, 0) >= val:
            return
        d[sem.name] = val
        self.q[eng].wait_ge(sem, val)

    def _need(self, eng, dep):
        if dep is None:
            return
        if isinstance(dep, list):
            for d in dep:
                self._need(eng, d)
            return
        if dep[0] == "e":
            _, e, idx = dep
            sg = self.sigs[e]
            if not sg or sg[-1] < idx:
                self.last[e].then_inc(self.sem[e], 1)
                sg.append(self.emitted[e])
            pos = bisect.bisect_left(sg, idx)
            self._wait_sem(eng, self.sem[e], pos + 1)
        else:
            tl = dep[1]
            for qc in tl.dsem:
                self._wait_sem(eng, tl.dsem[qc], tl.dn[qc])

    def _deps(self, eng, reads, writes, is_dma=False):
        for t in reads:
            w = t.w
            if w is not None:
                if not isinstance(w, list) and w[0] == "e" and w[1] == eng and eng == "pe":
                    continue
                self._need(eng, w)
        for t in writes:
            w = t.w
            if w is not None and (isinstance(w, list) or not (w[0] == "e" and w[1] == eng and eng == "pe")):
                self._need(eng, w)
            for key, dep in t.r.items():
                if dep[0] == "e" and dep[1] == eng and eng == "pe":
                    continue
                self._need(eng, dep)

    def op(self, eng, reads, writes, fn):
        self._deps(eng, reads, writes)
        ins = fn(self.q[eng])
        self.emitted[eng] += 1
        idx = self.emitted[eng]
        self.last[eng] = ins
        for t in reads:
            t.r[eng] = ("e", eng, idx)
        for t in writes:
            self._setw(t, ("e", eng, idx))
        return ins

    def V(self, reads, writes, fn):
        return self.op("dve", reads, writes, fn)

    def A(self, reads, writes, fn):
        return self.op("act", reads, writes, fn)

    def P(self, reads, writes, fn):
        return self.op("pe", reads, writes, fn)

    def dma(self, qe, dst, dst_ap, src, src_ap, semtl=None, kind="dma", in_off=None):
        reads = [src] if src is not None else []
        writes = [dst] if dst is not None else []
        self._deps(qe, reads, writes, is_dma=True)
        if semtl is None:
            semtl = dst if dst is not None else src
        qc = "sw" if qe == "pool" else "hw"
        if qc not in semtl.dsem:
            if not semtl.dsem:
                self.dma_tiles.append(semtl)
            semtl.dsem[qc] = self.newsem("d%s_%s" % (qc, semtl.name))
            semtl.dn[qc] = 0
        e = self.q[qe]
        if kind == "dma":
            ins = e.dma_start(out=dst_ap, in_=src_ap)
            inc = 16
        elif kind == "ind":
            ins = e.indirect_dma_start(out=dst_ap, out_offset=None, in_=src_ap, in_offset=in_off)
            inc = 16
        else:
            raise ValueError(kind)
        semtl.dn[qc] += inc
        ins.then_inc(semtl.dsem[qc], inc)
        self.emitted[qe] += 1
        self.last[qe] = None
        dep = ("d", semtl)
        if src is not None:
            src.r["d_" + semtl.name] = dep
        if dst is not None:
            self._setw(dst, dep)
        return ins

    def collective(self, src, dst, src_ap, dst_ap, groups, cctl):
        self._deps("pool", [src], [dst], is_dma=True)
        if "cc" not in cctl.dsem:
            if not cctl.dsem:
                self.dma_tiles.append(cctl)
            cctl.dsem["cc"] = self.newsem("dcc_" + cctl.name)
            cctl.dn["cc"] = 0
        ins = self.q["pool"].collective_compute("AllGather", ALU.bypass, replica_groups=groups, ins=[src_ap], outs=[dst_ap])
        cctl.dn["cc"] += 1
        ins.then_inc(cctl.dsem["cc"], 1)
        self.emitted["pool"] += 1
        self.last["pool"] = None
        dep = ("d", cctl)
        src.r["d_" + cctl.name] = dep
        self._setw(dst, dep)

    def finish(self, tls):
        for t in tls:
            if t.w is not None:
                self._need("sp", t.w)


class RR:
    def __init__(self, items):
        self.items = items
        self.i = 0

    def get(self):
        x = self.items[self.i % len(self.items)]
        self.i += 1
        return x


KM = 32
TT = 512
WBE = 8192


LOCK = [True]


def lockstep(gens):
    gens = list(gens)
    if not LOCK[0]:
        for g in gens:
            for _ in g:
                pass
        return
    while gens:
        nxt = []
        for g in gens:
            try:
                next(g)
                nxt.append(g)
            except StopIteration:
                pass
        gens = nxt


def build_program(cfg):
    L, D, DFF = cfg["L"], cfg["D"], cfg["DFF"]
    dbg = cfg.get("debug", False)
    LOCK[0] = cfg.get("lock", True)
    KD = D // 128
    NT = L // TT
    LQ = L // 4
    NT3 = LQ // TT
    FB = DFF // 128
    FH = FB // 2
    NG = D // 512
    assert FB % 2 == 0 and LQ % TT == 0 and D % 512 == 0 and FH <= 44 and 2 * KD * 128 <= WBE

    nc = bass.Bass("TRN2", target_bir_lowering=False)
    es = contextlib.ExitStack()
    with es:
        k = KB(nc, es)

        def din(name, shape, dt=F32):
            return nc.dram_tensor(name, list(shape), dt, kind="ExternalInput").ap()

        dump_on = cfg.get("dump", False)
        dumped = []

        def dump(name, tl, ap, shape, dt=F32):
            if not dump_on:
                return
            d = k.dram("D_" + name, list(shape), dt, kind="ExternalOutput")
            k.dma("sp", d, d.a, tl, ap)
            dumped.append(d)

        x_in = din("x", [L, D])
        xq_in = din("xq", [128 + LQ, D])
        w1_in = din("w1", [28, 128, KD, 128])
        wsm_in = din("wsm", [128, KD, 16])
        n1c_in = din("n1c", [128, KD])
        n2c_in = din("n2c", [128, KD])
        nfb_in = din("nfb", [D])
        cw_in = din("cw", [128, 20, 4])
        cb_in = din("cb", [128, 20])
        dtb_in = din("dtb", [128, 12])
        alog_in = din("alog", [128, 12])
        dcol_in = din("dcol", [128, 4])
        ssmnw_in = din("ssmnw", [128, 4])
        gdnnw_in = din("gdnnw", [128, 1])
        masks_in = din("masks", [4, 128, 128])
        wo_in = din("wo", [NG, 128, KM, 512])
        wg_in = din("wg", [FB, 128, 2, KD, 128])
        wd_in = din("wd", [DFF, D])
        fcw_in = din("fcw", [128, FB, 3])
        fcb_in = din("fcb", [128, FB])
        idx_in = din("idx", [128, (NT3 + 1) * KM], I32)
        out_d = k.dram("out", [LQ, D], F32, kind="ExternalOutput")
        ybuf = k.dram("ybuf", [NT, 1024, TT], BF16)
        gath = k.dram("gath", [(NT + 1) * 4096, TT], BF16, multi=True)
        hbuf = k.dram("hbuf", [128 + LQ, D], F32)
        WSLc = WBE // 512
        NSL = (FH + WSLc - 1) // WSLc
        w1s = k.dram("w1s", [28, 128, KD * 128], BF16)
        wos = k.dram("wos", [NG * 2, 128, KM * 256], BF16)
        wgs = k.dram("wgs", [FB, 128, 2 * KD * 128], BF16)
        wds = k.dram("wds", [2 * NG * NSL, 128, WBE], BF16)
        if dbg:
            ydbg = k.dram("ydbg", [NT, 1024, TT], BF16, kind="ExternalOutput")
            hdbg = k.dram("hdbg", [128 + LQ, D], F32, kind="ExternalOutput")

        cst = {"sp": Tl(None, "cst_sp"), "pool": Tl(None, "cst_pool")}

        HB = WBE // 2
        conv_jobs = []
        for g in range(NG):
            for hf in range(2):
                for q2 in range(2):
                    ks = slice(q2 * (KM // 2), (q2 + 1) * (KM // 2))
                    conv_jobs.append((wos, wos.a[g * 2 + hf].rearrange("p (k c) -> p k c", k=KM)[:, ks, :],
                                      wo_in[g][:, ks, hf * 256:(hf + 1) * 256], (KM // 2, 256)))
        for fb in range(FB):
            for u in range(2):
                conv_jobs.append((wgs, wgs.a[fb].rearrange("p (u k c) -> p u k c", u=2, k=KD)[:, u], wg_in[fb][:, u], (KD, 128)))
        for hf in range(2):
            for g in range(NG):
                for sl in range(NSL):
                    f0 = sl * WSLc
                    nf = min(WSLc, FH - f0)
                    si = (hf * NG + g) * NSL + sl
                    for q2 in range(2):
                        fa = q2 * (WSLc // 2)
                        fn_ = min(WSLc // 2, nf - fa)
                        if fn_ <= 0:
                            continue
                        r0 = (hf * FH + f0 + fa) * 128
                        conv_jobs.append((wds, wds.a[si][:, fa * 512:(fa + fn_) * 512].rearrange("p (f c) -> p f c", c=512),
                                          wd_in[r0:r0 + fn_ * 128, g * 512:(g + 1) * 512].rearrange("(f p) c -> p f c", p=128), (fn_, 512)))
        conv_pos = [0]
        bnc = []

        def emit_conv(n):
            for _ in range(n):
                if conv_pos[0] < len(conv_jobs):
                    d, dap, sap, (a_, b_) = conv_jobs[conv_pos[0]]
                    bt = bnc[conv_pos[0] % 2]
                    bv = bt.a[:, 0:a_ * b_].rearrange("p (a b) -> p a b", a=a_)
                    k.dma("pool", bt, bv, None, sap)
                    k.dma("pool", d, dap, bt, bv)
                    conv_pos[0] += 1

        def cload(name, shape, src_ap, dt=F32, q="sp"):
            t = k.sb(name, shape, dt)
            k.dma(q, t, t.a, None, src_ap, semtl=cst[q])
            return t

        ident = cload("ident", [128, 128], masks_in[0])
        triU = cload("triU", [128, 128], masks_in[1])
        negU = cload("negU", [128, 128], masks_in[2])
        posL = cload("posL", [128, 128], masks_in[3])
        n1c = cload("n1c", [128, KD], n1c_in)
        n2c = cload("n2c", [128, KD], n2c_in)
        cw = cload("cw", [128, 20, 4], cw_in)
        cbias = cload("cbias", [128, 20], cb_in)
        dtb = cload("dtb", [128, 12], dtb_in)
        alog = cload("alog", [128, 12], alog_in)
        dcol = cload("dcol", [128, 4], dcol_in)
        ssmnw = cload("ssmnw", [128, 4], ssmnw_in)
        gdnnw = cload("gdnnw", [128, 1], gdnnw_in)
        fcw = cload("fcw", [128, FB, 3], fcw_in)
        fcb = cload("fcb", [128, FB], fcb_in)
        idx = cload("idx", [128, (NT3 + 1) * KM], idx_in, dt=I32, q="pool")
        wsm = cload("wsm", [128, KD, 16], wsm_in, dt=BF16, q="pool")
        identb = k.sb("identb", [128, 128], BF16)
        ones = k.sb("ones", [128, 128], F32)
        epsc = k.sb("epsc", [128, 1], F32)
        eps128 = k.sb("eps128", [128, 1], F32)
        onec = k.sb("onec", [128, 1], F32)
        nA = k.sb("nA", [128, 12], F32)
        k.V([ident], [identb], lambda e: e.tensor_copy(out=identb.a, in_=ident.a))
        k.V([], [ones], lambda e: e.memset(ones.a, 1.0))
        k.V([], [epsc], lambda e: e.memset(epsc.a, EPS))
        k.V([], [eps128], lambda e: e.memset(eps128.a, EPS * 128.0))
        k.V([], [onec], lambda e: e.memset(onec.a, 1.0))
        k.A([alog], [nA], lambda e: e.activation(out=nA.a, in_=alog.a, func=AF.Exp))
        k.V([nA], [nA], lambda e: e.tensor_scalar(out=nA.a, in0=nA.a, scalar1=-1.0, scalar2=None, op0=ALU.mult))

        pbig_l = [k.ps("pb%d" % i, [128, 512], F32) for i in range(2)]
        pbig = RR(pbig_l)
        ptr = RR([k.ps("ptr%d" % i, [128, 1024], BF16) for i in range(2)])
        pqb = [k.ps("pqb%d" % i, [128, 512], F32) for i in range(4)]
        pq = RR([TlView(pqb[i], pqb[i].a[:, qd * 128:(qd + 1) * 128]) for qd in range(4) for i in range(4)])
        pq3 = RR([TlView(pqb[i], pqb[i].a[:, qd * 128:(qd + 1) * 128]) for qd in range(4) for i in range(2)])
        pacc = [pqb[2], pqb[3]]

        xrow = k.sb("xrow", [128, D], F32)
        hnT = k.sb("hnT", [128, KD, TT + 2], BF16)
        wblk_l = [k.sb("wblk%d" % i, [128, WBE], BF16) for i in range(2)]
        wblk = RR(wblk_l)
        wlo = RR([Tl(wblk_l[i].a[:, 0:WBE // 2], "wlo%d" % i) for i in range(2)])
        bnc.extend([Tl(wblk_l[i].a[:, WBE // 2:WBE], "bnc%d" % i) for i in range(2)])
        ssq = RR([k.sb("ssq%d" % i, [128, 1], F32) for i in range(4)])
        rstd = RR([k.sb("rstd%d" % i, [128, 1], F32) for i in range(4)])
        big = k.sb("big", [128, 22528], BF16)
        t512 = RR([k.sb("t512_%d" % i, [128, 516], F32) for i in range(5)])

        def norm_transpose(xr, hb, hba, col0, nwc, tgt, ncols=128, srccol=0):
            sq = ssq.get()
            rs = rstd.get()
            k.V([], [sq], lambda e: e.memset(sq.a, 0.0))
            k.A([xr, sq], [hb, sq], lambda e: e.activation(out=hba, in_=xr.a, func=AF.Square, accum_out=sq.a))
            k.A([sq, epsc], [rs], lambda e: e.activation(out=rs.a, in_=sq.a, func=AF.Sqrt, bias=epsc.a[:, 0:1], scale=1.0 / D))
            k.V([rs], [rs], lambda e: e.reciprocal(out=rs.a, in_=rs.a))
            k.A([xr, rs], [hb], lambda e: e.activation(out=hba, in_=xr.a, func=AF.Copy, scale=rs.a[:, 0:1]))
            for kg in range(KD // 4):
                pt = ptr.get()
                for qd in range(4):
                    kd = kg * 4 + qd
                    k.P([hb, identb], [pt], lambda e: e.transpose(pt.a[:, qd * 128:(qd + 1) * 128], hba[:, kd * 128:(kd + 1) * 128], identb.a))
                src = pt.a[:, 0:512].rearrange("p (q c) -> p q c", q=4)[:, :, srccol:srccol + ncols]
                dstv = tgt.a[:, kg * 4:kg * 4 + 4, col0:col0 + ncols]
                sc = nwc.a[:, kg * 4:kg * 4 + 4].unsqueeze(2).to_broadcast([128, 4, ncols])
                k.V([pt, nwc], [tgt], lambda e: e.tensor_tensor(out=dstv, in0=src, in1=sc, op=ALU.mult))

        def conv_silu(pb, ci, dst_tl, dst_ap, K, hal, cwt, cbt):
            pre = t512.get()
            acc = t512.get()
            H = K - 1
            W = TT
            k.A([hal], [pre], lambda e: e.activation(out=pre.a[:, 0:H], in_=hal.a[:, ci, 0:H], func=AF.Copy))
            k.A([pb], [pre], lambda e: e.activation(out=pre.a[:, H:H + W], in_=pb.a[:, 0:W], func=AF.Copy))
            k.V([pre, cwt, cbt], [acc], lambda e: e.tensor_scalar(out=acc.a[:, 0:W], in0=pre.a[:, H:H + W], scalar1=cwt.a[:, ci, K - 1:K], scalar2=cbt.a[:, ci:ci + 1], op0=ALU.mult, op1=ALU.add))
            for j in range(K - 1):
                k.V([pre, cwt, acc], [acc], lambda e: e.scalar_tensor_tensor(out=acc.a[:, 0:W], in0=pre.a[:, j:j + W], scalar=cwt.a[:, ci, j:j + 1], in1=acc.a[:, 0:W], op0=ALU.mult, op1=ALU.add))
            k.A([pre], [hal], lambda e: e.activation(out=hal.a[:, ci, 0:H], in_=pre.a[:, W:W + H], func=AF.Copy))
            k.A([acc], [dst_tl], lambda e: e.activation(out=dst_ap, in_=acc.a[:, 0:W], func=AF.Silu))

        def mm(out_tl, out_ap, l_tl, l_ap, r_tl, r_ap, start=True, stop=True):
            k.P([l_tl, r_tl], [out_tl], lambda e: e.matmul(out_ap, lhsT=l_ap, rhs=r_ap, start=start, stop=stop))

        es1 = contextlib.ExitStack()
        with es1:
            k.es = es1
            sv = big.a.bitcast(F32)

            def slot(i):
                return sv[:, i * TT:(i + 1) * TT]
            sstore = k.sb("sstore", [128, 2, TT], F32)
            zst = k.sb("zst", [128, 4, TT], BF16)
            halo = k.sb("halo", [128, 20, 3], F32)
            k.V([], [halo], lambda e: e.memset(halo.a, 0.0))
            o_g = k.sb("o_g", [128, 4, TT], BF16)
            o_s = k.sb("o_s", [128, 4, TT], BF16)
            yT = k.sb("yT", [128, 8, TT], BF16)
            hb1_ap = yT.a.rearrange("p b c -> p (b c)")[:, 0:D]
            S_g = [k.sb("S_g%d" % h, [128, 128], F32) for h in range(4)]
            S_s = [k.sb("S_s%d" % h, [128, 64], F32) for h in range(8)]
            for s_ in S_g + S_s:
                k.V([], [s_], lambda e: e.memset(s_.a, 0.0))
            mhp = [RR([k.sb("mh%d_%d" % (h, i), [128, 128], F32) for i in range(8)]) for h in range(4)]
            mlp = [RR([k.sb("ml%d_%d" % (h, i), [128, 128], F32) for i in range(5)]) for h in range(4)]
            mcb = [k.sb("mcb%d" % i, [128, 128], F32) for i in range(2)]
            sm12 = RR([k.sb("sm%d" % i, [128, 16], F32) for i in range(24)])
            ktok = [k.sb("ktok%d" % h, [128, 128], F32) for h in range(4)]
            vtok = [k.sb("vtok%d" % h, [128, 128], F32) for h in range(4)]
            btok = [k.sb("btok%d" % g, [128, 128], F32) for g in range(2)]
            xtok = k.sb("xtok", [128, 512], F32)

            def proj_block(cb, pb):
                wb = wlo.get()
                wv = wb.a[:, 0:KD * 128].rearrange("p (k c) -> p k c", k=KD)
                if t == 0:
                    k.dma("pool", wb, wv, None, w1_in[cb])
                    k.dma("sp", w1s, w1s.a[cb], wb, wb.a[:, 0:KD * 128])
                else:
                    k.dma("sp", wb, wb.a[:, 0:KD * 128], w1s, w1s.a[cb])
                for kd in range(KD):
                    k.P([wb, hnT], [pb], lambda e: e.matmul(pb.a, lhsT=wv[:, kd, :], rhs=hnT.a[:, kd, 0:TT], start=(kd == 0), stop=(kd == KD - 1)))
                return pb

            def l2norm(src_tl, src_ap, dst_tl, dst_ap, scale, bias_tl, pb):
                sq = t512.get()
                k.A([src_tl], [sq], lambda e: e.activation(out=sq.a[:, 0:TT], in_=src_ap, func=AF.Square))
                mm(pb, pb.a, ones, ones.a, sq, sq.a[:, 0:TT])
                rt = t512.get()
                k.A([pb, bias_tl], [rt], lambda e: e.activation(out=rt.a[:, 0:TT], in_=pb.a, func=AF.Sqrt, bias=bias_tl.a[:, 0:1], scale=scale))
                k.V([rt], [rt], lambda e: e.reciprocal(out=rt.a[:, 0:TT], in_=rt.a[:, 0:TT]))
                k.V([src_tl, rt], [dst_tl], lambda e: e.tensor_tensor(out=dst_ap, in0=src_ap, in1=rt.a[:, 0:TT], op=ALU.mult))

            def transpose_f(dst_tl, dst_ap, src_tl, src_ap):
                p = pq.get()
                k.P([src_tl, ident], [p], lambda e: e.transpose(p.a, src_ap, ident.a))
                k.A([p], [dst_tl], lambda e: e.activation(out=dst_ap, in_=p.a, func=AF.Copy))

            for t in range(NT):
                xr2 = TlView(big, big.a.bitcast(F32)[:, 0:D])
                xbufs = [xrow, xr2]

                def xload(s_):
                    xb = xbufs[s_ % 2]
                    k.dma("sp", xb, xb.a, None, x_in[t * TT + s_ * 128: t * TT + (s_ + 1) * 128, :])
                xload(0)
                xload(1)
                for s in range(4):
                    norm_transpose(xbufs[s % 2], yT, hb1_ap, s * 128, n1c, hnT)
                    if s + 2 < 4:
                        xload(s + 2)
                jobs = []
                for hh in range(4):
                    for r in range(4):
                        jobs.append(("g", hh, r))
                for sbk in range(12):
                    jobs.append(("s", sbk, 0))

                def post(job, pb, ji):
                    kind, a_, r = job
                    if kind == "g":
                        hh = a_
                        if r == 3:
                            k.A([pb], [big], lambda e: e.activation(out=slot(hh * 4 + 3), in_=pb.a, func=AF.Silu))
                        elif r == 2:
                            conv_silu(pb, hh * 3 + r, big, slot(hh * 4 + 2), 4, halo, cw, cbias)
                        else:
                            tmp = t512.get()
                            conv_silu(pb, hh * 3 + r, tmp, tmp.a[:, 0:TT], 4, halo, cw, cbias)
                            if r == 0:
                                l2norm(tmp, tmp.a[:, 0:TT], big, slot(hh * 4 + 0), 128.0, eps128, pqb[ji % 2])
                            else:
                                l2norm(tmp, tmp.a[:, 0:TT], big, slot(hh * 4 + 1), 1.0, epsc, pqb[ji % 2])
                    else:
                        sbk = a_
                        if sbk < 4:
                            k.A([pb], [zst], lambda e: e.activation(out=zst.a[:, sbk, :], in_=pb.a, func=AF.Silu))
                        elif sbk < 8:
                            conv_silu(pb, 12 + (sbk - 4), big, slot(16 + sbk - 4), 4, halo, cw, cbias)
                        elif sbk < 10:
                            conv_silu(pb, 12 + (sbk - 4), big, slot(20 + sbk - 8), 4, halo, cw, cbias)
                        else:
                            conv_silu(pb, 12 + (sbk - 4), sstore, sstore.a[:, sbk - 10, :], 4, halo, cw, cbias)

                prev = None
                for ji, job in enumerate(jobs):
                    cb = (job[1] * 4 + job[2]) if job[0] == "g" else 16 + job[1]
                    pb = proj_block(cb, pbig_l[ji % 2])
                    if prev is not None:
                        post(*prev)
                    prev = (job, pb, ji)
                post(*prev)
                if t >= 1:
                    emit_conv((len(conv_jobs) + NT - 2) // max(NT - 1, 1))
                for c4 in range(4):
                    cs = slice(c4 * 128, (c4 + 1) * 128)
                    psm = pq.get()
                    for kd in range(KD):
                        k.P([hnT, wsm], [psm], lambda e: e.matmul(psm.a[:, 0:16], lhsT=hnT.a[:, kd, cs], rhs=wsm.a[:, kd, :], start=(kd == 0), stop=(kd == KD - 1)))
                    sm = sm12.get()
                    k.A([psm], [sm], lambda e: e.activation(out=sm.a, in_=psm.a[:, 0:16], func=AF.Copy))
                    xs_ = sm12.get(); ax = sm12.get(); ex = sm12.get(); sp = sm12.get(); G = sm12.get(); beta = sm12.get()
                    k.V([sm, dtb], [xs_], lambda e: e.tensor_tensor(out=xs_.a[:, 0:12], in0=sm.a[:, 4:16], in1=dtb.a, op=ALU.add))
                    k.A([xs_], [ax], lambda e: e.activation(out=ax.a[:, 0:12], in_=xs_.a[:, 0:12], func=AF.Abs))
                    k.A([ax], [ex], lambda e: e.activation(out=ex.a[:, 0:12], in_=ax.a[:, 0:12], func=AF.Exp, scale=-1.0))
                    k.A([ex, onec], [ex], lambda e: e.activation(out=ex.a[:, 0:12], in_=ex.a[:, 0:12], func=AF.Ln, bias=onec.a[:, 0:1], scale=1.0))
                    k.V([xs_, ex], [sp], lambda e: e.scalar_tensor_tensor(out=sp.a[:, 0:12], in0=xs_.a[:, 0:12], scalar=0.0, in1=ex.a[:, 0:12], op0=ALU.max, op1=ALU.add))
                    k.V([sp, nA], [G], lambda e: e.tensor_tensor(out=G.a[:, 0:12], in0=sp.a[:, 0:12], in1=nA.a, op=ALU.mult))
                    k.A([sm], [beta], lambda e: e.activation(out=beta.a[:, 0:4], in_=sm.a[:, 0:4], func=AF.Sigmoid))
                    pc = pq.get()
                    mm(pc, pc.a[:, 0:12], triU, triU.a, G, G.a[:, 0:12])
                    mm(pc, pc.a[:, 16:28], ones, ones.a, G, G.a[:, 0:12])
                    gc = sm12.get(); gl = sm12.get(); eg = sm12.get(); kdc = sm12.get(); dbc = sm12.get(); bg = sm12.get(); dif = sm12.get()
                    k.A([pc], [gc], lambda e: e.activation(out=gc.a[:, 0:12], in_=pc.a[:, 0:12], func=AF.Copy))
                    k.A([pc], [gl], lambda e: e.activation(out=gl.a[:, 0:12], in_=pc.a[:, 16:28], func=AF.Copy))
                    k.A([gc], [eg], lambda e: e.activation(out=eg.a[:, 0:12], in_=gc.a[:, 0:12], func=AF.Exp))
                    k.V([gl, gc], [dif], lambda e: e.tensor_tensor(out=dif.a[:, 0:12], in0=gl.a[:, 0:12], in1=gc.a[:, 0:12], op=ALU.subtract))
                    k.A([dif], [kdc], lambda e: e.activation(out=kdc.a[:, 0:12], in_=dif.a[:, 0:12], func=AF.Exp))
                    k.A([gl], [dbc], lambda e: e.activation(out=dbc.a[:, 0:12], in_=gl.a[:, 0:12], func=AF.Exp))
                    k.V([beta, eg], [bg], lambda e: e.tensor_tensor(out=bg.a[:, 0:4], in0=beta.a[:, 0:4], in1=eg.a[:, 0:4], op=ALU.mult))
                    DMP = (t == 0 and c4 < 2)
                    if DMP:
                        tg = "_c%d" % c4
                        dump("sm" + tg, sm, sm.a, [128, 16]); dump("sp" + tg, sp, sp.a[:, 0:12], [128, 12]); dump("G" + tg, G, G.a[:, 0:12], [128, 12])
                        dump("gc" + tg, gc, gc.a[:, 0:12], [128, 12]); dump("gl" + tg, gl, gl.a[:, 0:12], [128, 12]); dump("beta" + tg, beta, beta.a[:, 0:4], [128, 4])
                        dump("kdc" + tg, kdc, kdc.a[:, 0:12], [128, 12]); dump("dbc" + tg, dbc, dbc.a[:, 0:12], [128, 12])
                        if c4 == 0:
                            for i_ in range(22):
                                dump("slot%d" % i_, big, slot(i_), [128, TT])
                            dump("sstore", sstore, sstore.a, [128, 2, TT])

                    def gdn_head(hh):
                        mh, ml = mhp[hh], mlp[hh]
                        qT = slot(hh * 4 + 0)[:, cs]
                        kT = slot(hh * 4 + 1)[:, cs]
                        vT = slot(hh * 4 + 2)[:, cs]
                        p1 = pq.get()
                        k.P([big, ident], [p1], lambda e: e.transpose(p1.a, kT, ident.a))
                        yield
                        k.A([p1], [ktok[hh]], lambda e: e.activation(out=ktok[hh].a, in_=p1.a, func=AF.Copy))
                        p2 = pq.get()
                        k.P([big, ident], [p2], lambda e: e.transpose(p2.a, vT, ident.a))
                        yield
                        k.A([p2], [vtok[hh]], lambda e: e.activation(out=vtok[hh].a, in_=p2.a, func=AF.Copy))
                        gb = mh.get()
                        k.V([ones, G], [gb], lambda e: e.tensor_scalar(out=gb.a, in0=ones.a, scalar1=G.a[:, hh:hh + 1], scalar2=None, op0=ALU.mult))
                        prow = pq.get()
                        mm(prow, prow.a, gb, gb.a, triU, triU.a)
                        yield
                        X = mh.get(); GT = mh.get(); XL = mh.get(); GL = mh.get(); Eg = mh.get()
                        k.V([prow, gc, negU], [X], lambda e: e.scalar_tensor_tensor(out=X.a, in0=prow.a, scalar=gc.a[:, hh:hh + 1], in1=negU.a, op0=ALU.subtract, op1=ALU.min))
                        k.A([X], [GT], lambda e: e.activation(out=GT.a, in_=X.a, func=AF.Exp))
                        k.V([prow, gc, posL], [XL], lambda e: e.scalar_tensor_tensor(out=XL.a, in0=prow.a, scalar=gc.a[:, hh:hh + 1], in1=posL.a, op0=ALU.subtract, op1=ALU.max))
                        k.A([XL], [GL], lambda e: e.activation(out=GL.a, in_=XL.a, func=AF.Exp, scale=-1.0))
                        k.A([prow], [Eg], lambda e: e.activation(out=Eg.a, in_=prow.a, func=AF.Exp))
                        pqk = pq.get()
                        mm(pqk, pqk.a, big, kT, big, qT)
                        yield
                        attnT = ml.get()
                        k.V([pqk, GT], [attnT], lambda e: e.tensor_tensor(out=attnT.a, in0=pqk.a, in1=GT.a, op=ALU.mult))
                        vb = ml.get(); kbg = ml.get(); kdec = ml.get(); qdec = ml.get()
                        k.V([vtok[hh], beta], [vb], lambda e: e.tensor_scalar(out=vb.a, in0=vtok[hh].a, scalar1=beta.a[:, hh:hh + 1], scalar2=None, op0=ALU.mult))
                        k.V([ktok[hh], bg], [kbg], lambda e: e.tensor_scalar(out=kbg.a, in0=ktok[hh].a, scalar1=bg.a[:, hh:hh + 1], scalar2=None, op0=ALU.mult))
                        k.V([ktok[hh], kdc], [kdec], lambda e: e.tensor_scalar(out=kdec.a, in0=ktok[hh].a, scalar1=kdc.a[:, hh:hh + 1], scalar2=None, op0=ALU.mult))
                        k.V([big, Eg], [qdec], lambda e: e.tensor_tensor(out=qdec.a, in0=qT, in1=Eg.a, op=ALU.mult))
                        pkk = pq.get()
                        mm(pkk, pkk.a, big, kT, big, kT)
                        yield
                        Am = mh.get()
                        k.V([pkk, beta, GL], [Am], lambda e: e.scalar_tensor_tensor(out=Am.a, in0=pkk.a, scalar=beta.a[:, hh:hh + 1], in1=GL.a, op0=ALU.mult, op1=ALU.mult))
                        pN = pq.get()
                        k.P([Am, ident], [pN], lambda e: e.transpose(pN.a, Am.a, ident.a))
                        yield
                        Nm = mh.get()
                        k.A([pN], [Nm], lambda e: e.activation(out=Nm.a, in_=pN.a, func=AF.Copy))
                        R = mh.get()
                        k.V([ident, Nm], [R], lambda e: e.tensor_tensor(out=R.a, in0=ident.a, in1=Nm.a, op=ALU.subtract))
                        Pm, PTm = Nm, Am
                        for lvl in range(1, 7):
                            pPT = pq.get()
                            mm(pPT, pPT.a, Pm, Pm.a, PTm, PTm.a)
                            yield
                            nPT = mh.get()
                            k.A([pPT], [nPT], lambda e: e.activation(out=nPT.a, in_=pPT.a, func=AF.Copy))
                            if lvl < 6:
                                pP = pq.get()
                                mm(pP, pP.a, PTm, PTm.a, Pm, Pm.a)
                                yield
                                nP = mh.get()
                                k.V([pP], [nP], lambda e: e.tensor_copy(out=nP.a, in_=pP.a))
                            pR = pq.get()
                            mm(pR, pR.a, nPT, nPT.a, R, R.a)
                            yield
                            nR = mh.get()
                            k.V([pR, R], [nR], lambda e: e.tensor_tensor(out=nR.a, in0=pR.a, in1=R.a, op=ALU.add))
                            R = nR
                            PTm = nPT
                            if lvl < 6:
                                Pm = nP
                        pw = pq.get()
                        mm(pw, pw.a, kbg, kbg.a, R, R.a)
                        yield
                        wTn = mh.get()
                        k.A([pw], [wTn], lambda e: e.activation(out=wTn.a, in_=pw.a, func=AF.Copy, scale=-1.0))
                        pv = pq.get()
                        mm(pv, pv.a, R, R.a, vb, vb.a, start=True, stop=False)
                        mm(pv, pv.a, wTn, wTn.a, S_g[hh], S_g[hh].a, start=False, stop=True)
                        yield
                        vnew = mh.get()
                        k.V([pv], [vnew], lambda e: e.tensor_copy(out=vnew.a, in_=pv.a))
                        po = pq.get()
                        mm(po, po.a, S_g[hh], S_g[hh].a, qdec, qdec.a, start=True, stop=False)
                        mm(po, po.a, vnew, vnew.a, attnT, attnT.a, start=False, stop=True)
                        yield
                        k.A([po], [o_g], lambda e: e.activation(out=o_g.a[:, hh, cs], in_=po.a, func=AF.Copy))
                        pS = pq.get()
                        mm(pS, pS.a, kdec, kdec.a, vnew, vnew.a)
                        yield
                        k.V([S_g[hh], dbc, pS], [S_g[hh]], lambda e: e.scalar_tensor_tensor(out=S_g[hh].a, in0=S_g[hh].a, scalar=dbc.a[:, hh:hh + 1], in1=pS.a, op0=ALU.mult, op1=ALU.add))

                    lockstep([gdn_head(hh) for hh in range(4)])

                    for g2 in range(2):
                        transpose_f(btok[g2], btok[g2].a, big, slot(20 + g2)[:, cs])
                    for blk in range(4):
                        transpose_f(xtok, xtok.a[:, blk * 128:(blk + 1) * 128], big, slot(16 + blk)[:, cs])
                    for g2 in range(2):
                        p = pq.get()
                        mm(p, p.a, big, slot(20 + g2)[:, cs], sstore, sstore.a[:, g2, cs])
                        k.A([p], [mcb[g2]], lambda e: e.activation(out=mcb[g2].a, in_=p.a, func=AF.Copy))

                    def ssd_head(h, pos_):
                        g2 = h // 4
                        col = 4 + h
                        mh = mhp[h % 4]
                        gb = mh.get()
                        k.V([ones, G], [gb], lambda e: e.tensor_scalar(out=gb.a, in0=ones.a, scalar1=G.a[:, col:col + 1], scalar2=None, op0=ALU.mult))
                        prow = pq.get()
                        mm(prow, prow.a, gb, gb.a, triU, triU.a)
                        yield
                        X = mh.get(); GT = mh.get(); Eg = mh.get()
                        k.V([prow, gc, negU], [X], lambda e: e.scalar_tensor_tensor(out=X.a, in0=prow.a, scalar=gc.a[:, col:col + 1], in1=negU.a, op0=ALU.subtract, op1=ALU.min))
                        k.A([X], [GT], lambda e: e.activation(out=GT.a, in_=X.a, func=AF.Exp))
                        k.A([prow], [Eg], lambda e: e.activation(out=Eg.a, in_=prow.a, func=AF.Exp))
                        attnT = mh.get(); qdec = mh.get(); kdec = mh.get(); vh = mh.get()
                        k.V([mcb[g2], GT], [attnT], lambda e: e.tensor_tensor(out=attnT.a, in0=mcb[g2].a, in1=GT.a, op=ALU.mult))
                        k.V([sstore, Eg], [qdec], lambda e: e.tensor_tensor(out=qdec.a, in0=sstore.a[:, g2, cs], in1=Eg.a, op=ALU.mult))
                        k.V([btok[g2], kdc], [kdec], lambda e: e.tensor_scalar(out=kdec.a, in0=btok[g2].a, scalar1=kdc.a[:, col:col + 1], scalar2=None, op0=ALU.mult))
                        k.V([xtok, sp], [vh], lambda e: e.tensor_scalar(out=vh.a[:, 0:64], in0=xtok.a[:, h * 64:(h + 1) * 64], scalar1=sp.a[:, col:col + 1], scalar2=None, op0=ALU.mult))
                        half = slice((h % 2) * 64, (h % 2) * 64 + 64)
                        mm(pos_, pos_.a[half, :], S_s[h], S_s[h].a, qdec, qdec.a, start=True, stop=False)
                        mm(pos_, pos_.a[half, :], vh, vh.a[:, 0:64], attnT, attnT.a, start=False, stop=True)
                        pS = pq.get()
                        mm(pS, pS.a[:, 0:64], kdec, kdec.a, vh, vh.a[:, 0:64])
                        yield
                        k.V([S_s[h], dbc, pS], [S_s[h]], lambda e: e.scalar_tensor_tensor(out=S_s[h].a, in0=S_s[h].a, scalar=dbc.a[:, col:col + 1], in1=pS.a[:, 0:64], op0=ALU.mult, op1=ALU.add))

                    for g2 in range(2):
                        pp = [pq.get(), pq.get()]
                        lockstep([ssd_head(g2 * 4 + i, pp[i // 2]) for i in range(4)])
                        for i2 in range(2):
                            k.A([pp[i2]], [o_s], lambda e: e.activation(out=o_s.a[:, g2 * 2 + i2, cs], in_=pp[i2].a, func=AF.Copy))

                for hh in range(4):
                    sq = t512.get()
                    k.A([o_g], [sq], lambda e: e.activation(out=sq.a[:, 0:TT], in_=o_g.a[:, hh, :], func=AF.Square))
                    pb = pbig.get()
                    mm(pb, pb.a, ones, ones.a, sq, sq.a[:, 0:TT])
                    rt = t512.get()
                    k.A([pb, epsc], [rt], lambda e: e.activation(out=rt.a[:, 0:TT], in_=pb.a, func=AF.Sqrt, bias=epsc.a[:, 0:1], scale=1.0 / 128))
                    k.V([rt], [rt], lambda e: e.reciprocal(out=rt.a[:, 0:TT], in_=rt.a[:, 0:TT]))
                    k.V([o_g, rt], [rt], lambda e: e.tensor_tensor(out=rt.a[:, 0:TT], in0=o_g.a[:, hh, :], in1=rt.a[:, 0:TT], op=ALU.mult))
                    k.V([rt, gdnnw, big], [yT], lambda e: e.scalar_tensor_tensor(out=yT.a[:, hh, :], in0=rt.a[:, 0:TT], scalar=gdnnw.a[:, 0:1], in1=slot(hh * 4 + 3), op0=ALU.mult, op1=ALU.mult))
                for g2 in range(2):
                    ys = []
                    pb = pbig.get()
                    for bi in range(2):
                        blk = g2 * 2 + bi
                        y0 = t512.get()
                        k.V([big, dcol, o_s], [y0], lambda e: e.scalar_tensor_tensor(out=y0.a[:, 0:TT], in0=slot(16 + blk), scalar=dcol.a[:, blk:blk + 1], in1=o_s.a[:, blk, :], op0=ALU.mult, op1=ALU.add))
                        k.V([y0, zst], [y0], lambda e: e.tensor_tensor(out=y0.a[:, 0:TT], in0=y0.a[:, 0:TT], in1=zst.a[:, blk, :], op=ALU.mult))
                        sq = t512.get()
                        k.A([y0], [sq], lambda e: e.activation(out=sq.a[:, 0:TT], in_=y0.a[:, 0:TT], func=AF.Square))
                        mm(pb, pb.a, ones, ones.a, sq, sq.a[:, 0:TT], start=(bi == 0), stop=(bi == 1))
                        ys.append(y0)
                    rt = t512.get()
                    k.A([pb, epsc], [rt], lambda e: e.activation(out=rt.a[:, 0:TT], in_=pb.a, func=AF.Sqrt, bias=epsc.a[:, 0:1], scale=1.0 / 256))
                    k.V([rt], [rt], lambda e: e.reciprocal(out=rt.a[:, 0:TT], in_=rt.a[:, 0:TT]))
                    for bi in range(2):
                        blk = g2 * 2 + bi
                        y0 = ys[bi]
                        k.V([y0, ssmnw, rt], [yT], lambda e: e.scalar_tensor_tensor(out=yT.a[:, 4 + blk, :], in0=y0.a[:, 0:TT], scalar=ssmnw.a[:, blk:blk + 1], in1=rt.a[:, 0:TT], op0=ALU.mult, op1=ALU.mult))
                if t == 0:
                    dump("o_g", o_g, o_g.a, [128, 4, TT]); dump("o_s", o_s, o_s.a, [128, 4, TT])
                k.dma("sp", ybuf, ybuf.a[t].rearrange("(b p) c -> p b c", p=128), yT, yT.a)
                if dbg:
                    k.dma("sp", ydbg, ydbg.a[t].rearrange("(b p) c -> p b c", p=128), yT, yT.a)
                k.collective(ybuf, gath, ybuf.a[t], gath.a[t * 4096:(t + 1) * 4096, :], cfg["groups"], gath)
            emit_conv(len(conv_jobs))
            zt = t512.get()
            k.V([], [zt], lambda e: e.memset(zt.a, 0.0))
            zb16 = zt.a.bitcast(BF16)[:, 0:TT]
            for i in range(32):
                k.dma("sp", gath, gath.a[NT * 4096 + i * 128: NT * 4096 + (i + 1) * 128, :], zt, zb16, semtl=zt)
            k.barrier()
        k.es = es

        es3 = contextlib.ExitStack()
        if cfg.get("stop_after", 3) == 1:
            k.finish([ydbg, gath] + dumped)
            k.barrier()
            return nc
        with es3:
            k.es = es3
            rowb = k.sb("rowb", [128, D], F32)
            hb3 = k.sb("hb3", [128, D], BF16)
            halo2 = k.sb("halo2", [128, FB, 2], F32)
            piece = RR([k.sb("piece%d" % i, [128, 512], F32) for i in range(8)])
            ssq3 = k.sb("ssqf", [128, 4, NG], F32)
            actT = big.a.rearrange("p (f c) -> p f c", c=TT)
            y_sb = big.a[:, 0:KM * 640].rearrange("p (k c) -> p k c", k=KM)
            accs = [pbig_l[0], pbig_l[1], pacc[0], pacc[1]]
            pbB = RR([pbig_l[0], pbig_l[1], pacc[0], pacc[1]])
            WSL = WBE // 512
            for m in range(NT3):
                g0 = 0 if m == 0 else 1
                for kd in range(KM):
                    if m == 0:
                        k.dma("pool", big, y_sb[:, kd, 0:128], gath, gath.a.rearrange("r (q c) -> (r q) c", c=128), kind="ind",
                              in_off=bass.IndirectOffsetOnAxis(ap=idx.a[:, kd:kd + 1], axis=0))
                    k.dma("pool", big, y_sb[:, kd, 128:640], gath, gath.a[:, :], kind="ind",
                          in_off=bass.IndirectOffsetOnAxis(ap=idx.a[:, (m + 1) * KM + kd:(m + 1) * KM + kd + 1], axis=0))
                for g in range(NG):
                    for hf in range(2):
                        wb = wblk.get()
                        wv = wb.a.rearrange("p (k c) -> p k c", k=KM)
                        k.dma("sp", wb, wb.a[:, 0:KM * 256], wos, wos.a[g * 2 + hf])
                        for gi in range(g0, 5):
                            xp = piece.get()
                            r0 = (0 if gi == 0 else 128 + m * TT + (gi - 1) * 128)
                            c0 = g * 512 + hf * 256
                            k.dma("pool", xp, xp.a[:, 0:256], None, xq_in[r0:r0 + 128, c0:c0 + 256])
                            pb = pbig.get()
                            for kd in range(KM):
                                k.P([big, wb], [pb], lambda e: e.matmul(pb.a[:, 0:256], lhsT=y_sb[:, kd, gi * 128:(gi + 1) * 128], rhs=wv[:, kd, :], start=(kd == 0), stop=(kd == KM - 1)))
                            k.V([pb, xp], [xp], lambda e: e.tensor_tensor(out=xp.a[:, 0:256], in0=pb.a[:, 0:256], in1=xp.a[:, 0:256], op=ALU.add))
                            k.dma("pool", hbuf, hbuf.a[r0:r0 + 128, c0:c0 + 256], xp, xp.a[:, 0:256])
                            if dbg:
                                k.dma("pool", hdbg, hdbg.a[r0:r0 + 128, c0:c0 + 256], xp, xp.a[:, 0:256])
                for gi in range(g0, 5):
                    r0 = (0 if gi == 0 else 128 + m * TT + (gi - 1) * 128)
                    k.dma("sp", xrow, xrow.a, hbuf, hbuf.a[r0:r0 + 128, :])
                    if gi == 0:
                        norm_transpose(xrow, hb3, hb3.a, 0, n2c, hnT, ncols=2, srccol=126)
                    else:
                        norm_transpose(xrow, hb3, hb3.a, 2 + (gi - 1) * 128, n2c, hnT)
                k.V([], [ssq3], lambda e: e.memset(ssq3.a, 0.0))
                for hf in range(2):
                    for fl in range(FH):
                        fb = hf * FH + fl
                        wb = wblk.get()
                        wv = wb.a[:, 0:2 * KD * 128].rearrange("p (u k c) -> p u k c", u=2, k=KD)
                        k.dma("sp", wb, wb.a[:, 0:2 * KD * 128], wgs, wgs.a[fb])
                        pg = pbB.get()
                        for kd in range(KD):
                            k.P([wb, hnT], [pg], lambda e: e.matmul(pg.a, lhsT=wv[:, 0, kd, :], rhs=hnT.a[:, kd, 2:2 + TT], start=(kd == 0), stop=(kd == KD - 1)))
                        if m == 0:
                            ph = pq3.get()
                            for kd in range(KD):
                                k.P([wb, hnT], [ph], lambda e: e.matmul(ph.a[:, 0:2], lhsT=wv[:, 0, kd, :], rhs=hnT.a[:, kd, 0:2], start=(kd == 0), stop=(kd == KD - 1)))
                            k.A([ph], [halo2], lambda e: e.activation(out=halo2.a[:, fb, :], in_=ph.a[:, 0:2], func=AF.Copy))
                        pu = pbB.get()
                        for kd in range(KD):
                            k.P([wb, hnT], [pu], lambda e: e.matmul(pu.a, lhsT=wv[:, 1, kd, :], rhs=hnT.a[:, kd, 2:2 + TT], start=(kd == 0), stop=(kd == KD - 1)))
                        sg = t512.get()
                        conv_silu(pg, fb, sg, sg.a[:, 0:TT], 3, halo2, fcw, fcb)
                        k.V([sg, pu], [big], lambda e: e.tensor_tensor(out=actT[:, fl, :], in0=sg.a[:, 0:TT], in1=pu.a, op=ALU.mult))
                    for g in range(NG):
                        nsl = (FH + WSL - 1) // WSL
                        hps = []
                        for i in range(4):
                            r0 = m * TT + i * 128
                            hp = piece.get()
                            if hf == 0:
                                k.dma("pool", hp, hp.a, hbuf, hbuf.a[128 + r0:128 + r0 + 128, g * 512:(g + 1) * 512])
                            else:
                                k.dma("pool", hp, hp.a, out_d, out_d.a[r0:r0 + 128, g * 512:(g + 1) * 512])
                            hps.append(hp)
                        for sl in range(nsl):
                            f0 = sl * WSL
                            nf = min(WSL, FH - f0)
                            wb = wblk.get()
                            wv = wb.a.rearrange("p (f c) -> p f c", c=512)
                            r0 = (hf * FH + f0) * 128
                            si = (hf * NG + g) * NSL + sl
                            k.dma("sp", wb, wb.a[:, 0:nf * 512], wds, wds.a[si][:, 0:nf * 512])
                            for i in range(4):
                                for f in range(nf):
                                    k.P([big, wb], [accs[i]], lambda e: e.matmul(accs[i].a, lhsT=actT[:, f0 + f, i * 128:(i + 1) * 128], rhs=wv[:, f, :], start=(f0 + f == 0), stop=(f0 + f == FH - 1)))
                        for i in range(4):
                            r0 = m * TT + i * 128
                            hp = hps[i]
                            k.V([accs[i], hp], [hp], lambda e: e.tensor_tensor(out=hp.a, in0=accs[i].a, in1=hp.a, op=ALU.add))
                            if hf == 1:
                                jk = t512.get()
                                k.A([hp, ssq3], [jk, ssq3], lambda e: e.activation(out=jk.a[:, 0:512], in_=hp.a, func=AF.Square, accum_out=ssq3.a[:, i, g:g + 1]))
                            k.dma("pool", out_d, out_d.a[r0:r0 + 128, g * 512:(g + 1) * 512], hp, hp.a)
                k.dma("sp", xrow, xrow.a, None, nfb_in.partition_broadcast(128))
                for i in range(4):
                    r0 = m * TT + i * 128
                    tot = ssq.get(); rs = rstd.get()
                    k.V([ssq3], [tot], lambda e: e.tensor_reduce(out=tot.a, in_=ssq3.a[:, i, :], axis=mybir.AxisListType.X, op=ALU.add))
                    k.A([tot, epsc], [rs], lambda e: e.activation(out=rs.a, in_=tot.a, func=AF.Sqrt, bias=epsc.a[:, 0:1], scale=1.0 / D))
                    k.V([rs], [rs], lambda e: e.reciprocal(out=rs.a, in_=rs.a))
                    k.dma("pool", rowb, rowb.a, out_d, out_d.a[r0:r0 + 128, :])
                    k.V([rowb, rs, xrow], [rowb], lambda e: e.scalar_tensor_tensor(out=rowb.a, in0=rowb.a, scalar=rs.a[:, 0:1], in1=xrow.a, op0=ALU.mult, op1=ALU.mult))
                    k.dma("pool", out_d, out_d.a[r0:r0 + 128, :], rowb, rowb.a)
            fin = [out_d]
            if dbg:
                fin += [ydbg, hdbg]
            k.finish(fin)
            k.barrier()
        k.es = es
    return nc


Q0, K0, V0, GATE0, BETA0, ALPHA0, Z0, XBC0, DT0 = 0, 2048, 4096, 6144, 8192, 8208, 8224, 10272, 14368
GROUPS = [[0, 1, 2, 3], [4, 5, 6, 7]]


def _masks():
    p = np.arange(128)[:, None]
    f = np.arange(128)[None, :]
    ident = (p == f).astype(np.float32)
    triU = (p <= f).astype(np.float32)
    negU = np.where(f >= p, 0.0, NEG).astype(np.float32)
    posL = np.where(p > f, 0.0, -NEG).astype(np.float32)
    return np.ascontiguousarray(np.stack([ident, triU, negU, posL]))


def prep_inputs(inp, cfg):
    L, D, DFF = cfg["L"], cfg["D"], cfg["DFF"]
    KD, FB, NG = D // 128, DFF // 128, D // 512
    LQ = L // 4
    NT, NT3 = L // TT, LQ // TT
    f32 = np.float32
    x = np.asarray(inp["x"], f32)
    w_in = np.asarray(inp["w_in"], f32)[0]
    w_out = np.asarray(inp["w_out"], f32)[0]
    gcw = np.asarray(inp["gdn_conv_w"], f32)[0]
    scw = np.asarray(inp["ssm_conv_w"], f32)[0]
    scb = np.asarray(inp["ssm_conv_b"], f32)[0]
    ar = np.arange(128)
    masks = _masks()
    wgate = np.asarray(inp["ffn_w_gate"], f32)[0].reshape(KD, 128, FB, 128).transpose(2, 1, 0, 3)
    wup = np.asarray(inp["ffn_w_up"], f32)[0].reshape(KD, 128, FB, 128).transpose(2, 1, 0, 3)
    wg = np.ascontiguousarray(np.stack([wgate, wup], axis=2))
    wd = np.ascontiguousarray(np.asarray(inp["ffn_w_down"], f32)[0])
    fcw = np.ascontiguousarray(np.asarray(inp["ffn_conv_w"], f32)[0].reshape(3, FB, 128).transpose(2, 1, 0))
    fcb = np.ascontiguousarray(np.asarray(inp["ffn_conv_b"], f32)[0].reshape(FB, 128).T)
    n1c = np.ascontiguousarray(np.asarray(inp["norm1_w"], f32)[0].reshape(KD, 128).T)
    n2c = np.ascontiguousarray(np.asarray(inp["norm2_w"], f32)[0].reshape(KD, 128).T)
    nfb = np.ascontiguousarray(np.asarray(inp["norm_f_w"], f32))
    gdnnw = np.ascontiguousarray(np.asarray(inp["gdn_norm_w"], f32)[0].reshape(128, 1))
    perm = []
    for r in range(4):
        for hh in range(4):
            perm.append(128 * (4 * r + hh) + ar)
        perm.append(2048 + 512 * r + np.arange(512))
    perm = np.concatenate(perm)
    wo = np.ascontiguousarray(w_out[perm].reshape(KM, 128, NG, 512).transpose(2, 1, 0, 3))
    maps = []
    for c in range(8):
        b, j = c // 4, c % 4
        cols = []
        for hh in range(4):
            H = 4 * j + hh
            cols += [Q0 + 128 * H + ar, K0 + 128 * H + ar, V0 + 128 * H + ar, GATE0 + 128 * H + ar]
        for zb in range(4):
            cols.append(Z0 + 512 * j + zb * 128 + ar)
        for blk in range(4):
            cols.append(XBC0 + 512 * j + blk * 128 + ar)
        for g in range(2):
            cols.append(XBC0 + 2048 + 128 * (2 * j + g) + ar)
        for g in range(2):
            cols.append(XBC0 + 3072 + 128 * (2 * j + g) + ar)
        cols = np.concatenate(cols)
        w1 = np.ascontiguousarray(w_in[:, cols].reshape(KD, 128, 28, 128).transpose(2, 1, 0, 3))
        small = np.concatenate([BETA0 + 4 * j + np.arange(4), ALPHA0 + 4 * j + np.arange(4), DT0 + 8 * j + np.arange(8)])
        wsm = np.ascontiguousarray(w_in[:, small].reshape(KD, 128, 16).transpose(1, 0, 2))
        cw = np.zeros((128, 20, 4), f32)
        cb = np.zeros((128, 20), f32)
        for hh in range(4):
            H = 4 * j + hh
            for r in range(3):
                cw[:, hh * 3 + r, :] = gcw[:, r * 2048 + 128 * H + ar].T
        for i in range(8):
            if i < 4:
                ch = 512 * j + i * 128 + ar
            elif i < 6:
                ch = 2048 + 128 * (2 * j + (i - 4)) + ar
            else:
                ch = 3072 + 128 * (2 * j + (i - 6)) + ar
            cw[:, 12 + i, :] = scw[:, ch].T
            cb[:, 12 + i] = scb[ch]
        dtb = np.concatenate([np.asarray(inp["gdn_dt_bias"], f32)[0][4 * j:4 * j + 4], np.asarray(inp["ssm_dt_bias"], f32)[0][8 * j:8 * j + 8]])
        alog = np.concatenate([np.asarray(inp["gdn_A_log"], f32)[0][4 * j:4 * j + 4], np.asarray(inp["ssm_A_log"], f32)[0][8 * j:8 * j + 8]])
        ssd_D = np.asarray(inp["ssm_D"], f32)[0]
        dcol = np.zeros((128, 4), f32)
        for blk in range(4):
            dcol[:64, blk] = ssd_D[8 * j + 2 * blk]
            dcol[64:, blk] = ssd_D[8 * j + 2 * blk + 1]
        ssmnw = np.ascontiguousarray(np.asarray(inp["ssm_norm_w"], f32)[0][512 * j:512 * (j + 1)].reshape(4, 128).T)
        xq = np.zeros((128 + LQ, D), f32)
        if j > 0:
            xq[:] = x[b, j * LQ - 128:(j + 1) * LQ]
        else:
            xq[128:] = x[b, 0:LQ]
        idx = np.zeros((128, (NT3 + 1) * KM), np.int32)
        for blk in range(NT3 + 1):
            if blk == 0:
                tile = j * NT3 - 1 if j > 0 else NT
            else:
                tile = j * NT3 + (blk - 1)
            for kd in range(KM):
                idx[:, blk * KM + kd] = tile * 4096 + kd * 128 + ar
                if blk == 0:
                    idx[:, kd] = idx[:, kd] * 4 + 3
        maps.append({
            "x": np.ascontiguousarray(x[b]), "xq": xq, "w1": w1, "wsm": wsm, "n1c": n1c, "n2c": n2c, "nfb": nfb,
            "cw": cw, "cb": cb, "dtb": np.ascontiguousarray(np.tile(dtb[None, :], (128, 1))),
            "alog": np.ascontiguousarray(np.tile(alog[None, :], (128, 1))), "dcol": dcol, "ssmnw": ssmnw, "gdnnw": gdnnw,
            "masks": masks, "wo": wo, "wg": wg, "wd": wd, "fcw": fcw, "fcb": fcb, "idx": idx,
        })
    return maps


def run(inp, cfg):
    nc = build_program(cfg)
    maps = prep_inputs(inp, cfg)
    res = run_bass_kernel_spmd(nc, maps, core_ids=list(range(8)))
    return res


def kernel(**inputs):
    cfg = {"L": 8192, "D": 4096, "DFF": 11008, "groups": GROUPS}
    res = run(inputs, cfg)
    L, D = cfg["L"], cfg["D"]
    LQ = L // 4
    out = np.zeros((2, L, D), np.float32)
    for c in range(8):
        b, j = c // 4, c % 4
        out[b, j * LQ:(j + 1) * LQ] = res.results[c]["out"]
    return out
```

```python
import bisect
import contextlib
import numpy as np
import concourse.bass as bass
import concourse.mybir as mybir
from concourse.bass_utils import run_bass_kernel_spmd

F32 = mybir.dt.float32
BF16 = mybir.dt.bfloat16
I32 = mybir.dt.int32
AF = mybir.ActivationFunctionType
ALU = mybir.AluOpType
EPS = 1e-6
NEG = -30000.0


class Tl:
    __slots__ = ("a", "name", "w", "r", "dsem", "dn", "multi")

    def __init__(self, a, name, multi=False):
        self.a = a
        self.name = name
        self.w = [] if multi else None
        self.r = {}
        self.dsem = {}
        self.dn = {}
        self.multi = multi


class TlView:
    def __init__(self, parent, a):
        object.__setattr__(self, "p", parent)
        object.__setattr__(self, "a", a)

    def __getattr__(self, n):
        return getattr(object.__getattribute__(self, "p"), n)

    def __setattr__(self, n, v):
        setattr(object.__getattribute__(self, "p"), n, v)


class KB:
    ENG = ("pe", "dve", "act", "pool", "sp")

    def __init__(self, nc, es):
        self.nc = nc
        self.es = es
        self.es_sem = es
        self.dma_tiles = []
        self.q = {"pe": nc.tensor, "dve": nc.vector, "act": nc.scalar, "pool": nc.gpsimd, "sp": nc.sync}
        self.sem = {e: es.enter_context(nc.semaphore("s_" + e)) for e in self.ENG}
        self.emitted = {e: 0 for e in self.ENG}
        self.last = {e: None for e in self.ENG}
        self.sigs = {e: [] for e in self.ENG}
        self.seen = {e: {} for e in self.ENG}
        self.nsem = 0
        self.uid = 0

    def sb(self, name, shape, dt):
        t = self.es.enter_context(self.nc.sbuf_tensor("sb_" + name, list(shape), dt))
        return Tl(t[:], name)

    def ps(self, name, shape, dt):
        t = self.es.enter_context(self.nc.psum_tensor("ps_" + name, list(shape), dt))
        return Tl(t[:], name)

    def dram(self, name, shape, dt, kind="Internal", multi=False):
        t = self.nc.dram_tensor(name, list(shape), dt, kind=kind)
        return Tl(t.ap(), name, multi=multi)

    def view(self, ap, name):
        return Tl(ap, name)

    def newsem(self, name):
        self.nsem += 1
        return self.es_sem.enter_context(self.nc.semaphore(name))

    def _setw(self, t, dep):
        if t.multi:
            t.w = [d for d in t.w if d[1] is not dep[1]] + [dep]
        else:
            t.w = dep
        t.r = {}

    def barrier(self):
        engs = ["pe", "dve", "act"]
        for e in engs:
            if self.last[e] is not None and (not self.sigs[e] or self.sigs[e][-1] < self.emitted[e]):
                self.last[e].then_inc(self.sem[e], 1)
                self.sigs[e].append(self.emitted[e])
        for e in self.ENG:
            for o in engs:
                if o != e and self.sigs[o]:
                    self._wait_sem(e, self.sem[o], len(self.sigs[o]))
            for tl in self.dma_tiles:
                for qc in tl.dsem:
                    self._wait_sem(e, tl.dsem[qc], tl.dn[qc])

    def _wait_sem(self, eng, sem, val):
        d = self.seen[eng]
        if d.get(sem.name, 0) >= val:
            return
        d[sem.name] = val
        self.q[eng].wait_ge(sem, val)

    def _need(self, eng, dep):
        if dep is None:
            return
        if isinstance(dep, list):
            for d in dep:
                self._need(eng, d)
            return
        if dep[0] == "e":
            _, e, idx = dep
            sg = self.sigs[e]
            if not sg or sg[-1] < idx:
                self.last[e].then_inc(self.sem[e], 1)
                sg.append(self.emitted[e])
            pos = bisect.bisect_left(sg, idx)
            self._wait_sem(eng, self.sem[e], pos + 1)
        else:
            tl = dep[1]
            for qc in tl.dsem:
                self._wait_sem(eng, tl.dsem[qc], tl.dn[qc])

    def _deps(self, eng, reads, writes, is_dma=False):
        for t in reads:
            w = t.w
            if w is not None:
                if not isinstance(w, list) and w[0] == "e" and w[1] == eng and eng == "pe":
                    continue
                self._need(eng, w)
        for t in writes:
            w = t.w
            if w is not None and (isinstance(w, list) or not (w[0] == "e" and w[1] == eng and eng == "pe")):
                self._need(eng, w)
            for key, dep in t.r.items():
                if dep[0] == "e" and dep[1] == eng and eng == "pe":
                    continue
                self._need(eng, dep)

    def op(self, eng, reads, writes, fn):
        self._deps(eng, reads, writes)
        ins = fn(self.q[eng])
        self.emitted[eng] += 1
        idx = self.emitted[eng]
        self.last[eng] = ins
        for t in reads:
            t.r[eng] = ("e", eng, idx)
        for t in writes:
            self._setw(t, ("e", eng, idx))
        return ins

    def V(self, reads, writes, fn):
        return self.op("dve", reads, writes, fn)

    def A(self, reads, writes, fn):
        return self.op("act", reads, writes, fn)

    def P(self, reads, writes, fn):
        return self.op("pe", reads, writes, fn)

    def dma(self, qe, dst, dst_ap, src, src_ap, semtl=None, kind="dma", in_off=None):
        reads = [src] if src is not None else []
        writes = [dst] if dst is not None else []
        self._deps(qe, reads, writes, is_dma=True)
        if semtl is None:
            semtl = dst if dst is not None else src
        qc = "sw" if qe == "pool" else "hw"
        if qc not in semtl.dsem:
            if not semtl.dsem:
                self.dma_tiles.append(semtl)
            semtl.dsem[qc] = self.newsem("d%s_%s" % (qc, semtl.name))
            semtl.dn[qc] = 0
        e = self.q[qe]
        if kind == "dma":
            ins = e.dma_start(out=dst_ap, in_=src_ap)
            inc = 16
        elif kind == "ind":
            ins = e.indirect_dma_start(out=dst_ap, out_offset=None, in_=src_ap, in_offset=in_off)
            inc = 16
        else:
            raise ValueError(kind)
        semtl.dn[qc] += inc
        ins.then_inc(semtl.dsem[qc], inc)
        self.emitted[qe] += 1
        self.last[qe] = None
        dep = ("d", semtl)
        if src is not None:
            src.r["d_" + semtl.name] = dep
        if dst is not None:
            self._setw(dst, dep)
        return ins

    def collective(self, src, dst, src_ap, dst_ap, groups, cctl):
        self._deps("pool", [src], [dst], is_dma=True)
        if "cc" not in cctl.dsem:
            if not cctl.dsem:
                self.dma_tiles.append(cctl)
            cctl.dsem["cc"] = self.newsem("dcc_" + cctl.name)
            cctl.dn["cc"] = 0
        ins = self.q["pool"].collective_compute("AllGather", ALU.bypass, replica_groups=groups, ins=[src_ap], outs=[dst_ap])
        cctl.dn["cc"] += 1
        ins.then_inc(cctl.dsem["cc"], 1)
        self.emitted["pool"] += 1
        self.last["pool"] = None
        dep = ("d", cctl)
        src.r["d_" + cctl.name] = dep
        self._setw(dst, dep)

    def finish(self, tls):
        for t in tls:
            if t.w is not None:
                self._need("sp", t.w)


class RR:
    def __init__(self, items):
        self.items = items
        self.i = 0

    def get(self):
        x = self.items[self.i % len(self.items)]
        self.i += 1
        return x


KM = 32
TT = 512
WBE = 8192


LOCK = [True]


def lockstep(gens):
    gens = list(gens)
    if not LOCK[0]:
        for g in gens:
            for _ in g:
                pass
        return
    while gens:
        nxt = []
        for g in gens:
            try:
                next(g)
                nxt.append(g)
            except StopIteration:
                pass
        gens = nxt


def build_program(cfg):
    L, D, DFF = cfg["L"], cfg["D"], cfg["DFF"]
    dbg = cfg.get("debug", False)
    LOCK[0] = cfg.get("lock", True)
    KD = D // 128
    NT = L // TT
    LQ = L // 4
    NT3 = LQ // TT
    FB = DFF // 128
    FH = FB // 2
    NG = D // 512
    assert FB % 2 == 0 and LQ % TT == 0 and D % 512 == 0 and FH <= 44 and 2 * KD * 128 <= WBE

    nc = bass.Bass("TRN2", target_bir_lowering=False)
    es = contextlib.ExitStack()
    with es:
        k = KB(nc, es)

        def din(name, shape, dt=F32):
            return nc.dram_tensor(name, list(shape), dt, kind="ExternalInput").ap()

        dump_on = cfg.get("dump", False)
        dumped = []

        def dump(name, tl, ap, shape, dt=F32):
            if not dump_on:
                return
            d = k.dram("D_" + name, list(shape), dt, kind="ExternalOutput")
            k.dma("sp", d, d.a, tl, ap)
            dumped.append(d)

        x_in = din("x", [L, D])
        xq_in = din("xq", [128 + LQ, D])
        w1_in = din("w1", [28, 128, KD, 128])
        wsm_in = din("wsm", [128, KD, 16])
        n1c_in = din("n1c", [128, KD])
        n2c_in = din("n2c", [128, KD])
        nfb_in = din("nfb", [D])
        cw_in = din("cw", [128, 20, 4])
        cb_in = din("cb", [128, 20])
        dtb_in = din("dtb", [128, 12])
        alog_in = din("alog", [128, 12])
        dcol_in = din("dcol", [128, 4])
        ssmnw_in = din("ssmnw", [128, 4])
        gdnnw_in = din("gdnnw", [128, 1])
        masks_in = din("masks", [4, 128, 128])
        wo_in = din("wo", [NG, 128, KM, 512])
        wg_in = din("wg", [FB, 128, 2, KD, 128])
        wd_in = din("wd", [DFF, D])
        fcw_in = din("fcw", [128, FB, 3])
        fcb_in = din("fcb", [128, FB])
        idx_in = din("idx", [128, (NT3 + 1) * KM], I32)
        out_d = k.dram("out", [LQ, D], F32, kind="ExternalOutput")
        ybuf = k.dram("ybuf", [NT, 1024, TT], BF16)
        gath = k.dram("gath", [(NT + 1) * 4096, TT], BF16, multi=True)
        hbuf = k.dram("hbuf", [128 + LQ, D], F32)
        WSLc = WBE // 512
        NSL = (FH + WSLc - 1) // WSLc
        w1s = k.dram("w1s", [28, 128, KD * 128], BF16)
        wos = k.dram("wos", [NG * 2, 128, KM * 256], BF16)
        wgs = k.dram("wgs", [FB, 128, 2 * KD * 128], BF16)
        wds = k.dram("wds", [2 * NG * NSL, 128, WBE], BF16)
        if dbg:
            ydbg = k.dram("ydbg", [NT, 1024, TT], BF16, kind="ExternalOutput")
            hdbg = k.dram("hdbg", [128 + LQ, D], F32, kind="ExternalOutput")

        cst = {"sp": Tl(None, "cst_sp"), "pool": Tl(None, "cst_pool")}

        HB = WBE // 2
        conv_jobs = []
        for g in range(NG):
            for hf in range(2):
                for q2 in range(2):
                    ks = slice(q2 * (KM // 2), (q2 + 1) * (KM // 2))
                    conv_jobs.append((wos, wos.a[g * 2 + hf].rearrange("p (k c) -> p k c", k=KM)[:, ks, :],
                                      wo_in[g][:, ks, hf * 256:(hf + 1) * 256], (KM // 2, 256)))
        for fb in range(FB):
            for u in range(2):
                conv_jobs.append((wgs, wgs.a[fb].rearrange("p (u k c) -> p u k c", u=2, k=KD)[:, u], wg_in[fb][:, u], (KD, 128)))
        for hf in range(2):
            for g in range(NG):
                for sl in range(NSL):
                    f0 = sl * WSLc
                    nf = min(WSLc, FH - f0)
                    si = (hf * NG + g) * NSL + sl
                    for q2 in range(2):
                        fa = q2 * (WSLc // 2)
                        fn_ = min(WSLc // 2, nf - fa)
                        if fn_ <= 0:
                            continue
                        r0 = (hf * FH + f0 + fa) * 128
                        conv_jobs.append((wds, wds.a[si][:, fa * 512:(fa + fn_) * 512].rearrange("p (f c) -> p f c", c=512),
                                          wd_in[r0:r0 + fn_ * 128, g * 512:(g + 1) * 512].rearrange("(f p) c -> p f c", p=128), (fn_, 512)))
        conv_pos = [0]
        bnc = []

        def emit_conv(n):
            for _ in range(n):
                if conv_pos[0] < len(conv_jobs):
                    d, dap, sap, (a_, b_) = conv_jobs[conv_pos[0]]
                    bt = bnc[conv_pos[0] % 2]
                    bv = bt.a[:, 0:a_ * b_].rearrange("p (a b) -> p a b", a=a_)
                    k.dma("pool", bt, bv, None, sap)
                    k.dma("pool", d, dap, bt, bv)
                    conv_pos[0] += 1

        def cload(name, shape, src_ap, dt=F32, q="sp"):
            t = k.sb(name, shape, dt)
            k.dma(q, t, t.a, None, src_ap, semtl=cst[q])
            return t

        ident = cload("ident", [128, 128], masks_in[0])
        triU = cload("triU", [128, 128], masks_in[1])
        negU = cload("negU", [128, 128], masks_in[2])
        posL = cload("posL", [128, 128], masks_in[3])
        n1c = cload("n1c", [128, KD], n1c_in)
        n2c = cload("n2c", [128, KD], n2c_in)
        cw = cload("cw", [128, 20, 4], cw_in)
        cbias = cload("cbias", [128, 20], cb_in)
        dtb = cload("dtb", [128, 12], dtb_in)
        alog = cload("alog", [128, 12], alog_in)
        dcol = cload("dcol", [128, 4], dcol_in)
        ssmnw = cload("ssmnw", [128, 4], ssmnw_in)
        gdnnw = cload("gdnnw", [128, 1], gdnnw_in)
        fcw = cload("fcw", [128, FB, 3], fcw_in)
        fcb = cload("fcb", [128, FB], fcb_in)
        idx = cload("idx", [128, (NT3 + 1) * KM], idx_in, dt=I32, q="pool")
        wsm = cload("wsm", [128, KD, 16], wsm_in, dt=BF16, q="pool")
        identb = k.sb("identb", [128, 128], BF16)
        ones = k.sb("ones", [128, 128], F32)
        epsc = k.sb("epsc", [128, 1], F32)
        eps128 = k.sb("eps128", [128, 1], F32)
        onec = k.sb("onec", [128, 1], F32)
        nA = k.sb("nA", [128, 12], F32)
        k.V([ident], [identb], lambda e: e.tensor_copy(out=identb.a, in_=ident.a))
        k.V([], [ones], lambda e: e.memset(ones.a, 1.0))
        k.V([], [epsc], lambda e: e.memset(epsc.a, EPS))
        k.V([], [eps128], lambda e: e.memset(eps128.a, EPS * 128.0))
        k.V([], [onec], lambda e: e.memset(onec.a, 1.0))
        k.A([alog], [nA], lambda e: e.activation(out=nA.a, in_=alog.a, func=AF.Exp))
        k.V([nA], [nA], lambda e: e.tensor_scalar(out=nA.a, in0=nA.a, scalar1=-1.0, scalar2=None, op0=ALU.mult))

        pbig_l = [k.ps("pb%d" % i, [128, 512], F32) for i in range(2)]
        pbig = RR(pbig_l)
        ptr = RR([k.ps("ptr%d" % i, [128, 1024], BF16) for i in range(2)])
        pqb = [k.ps("pqb%d" % i, [128, 512], F32) for i in range(4)]
        pq = RR([TlView(pqb[i], pqb[i].a[:, qd * 128:(qd + 1) * 128]) for qd in range(4) for i in range(4)])
        pq3 = RR([TlView(pqb[i], pqb[i].a[:, qd * 128:(qd + 1) * 128]) for qd in range(4) for i in range(2)])
        pacc = [pqb[2], pqb[3]]

        xrow = k.sb("xrow", [128, D], F32)
        hnT = k.sb("hnT", [128, KD, TT + 2], BF16)
        wblk_l = [k.sb("wblk%d" % i, [128, WBE], BF16) for i in range(2)]
        wblk = RR(wblk_l)
        wlo = RR([Tl(wblk_l[i].a[:, 0:WBE // 2], "wlo%d" % i) for i in range(2)])
        bnc.extend([Tl(wblk_l[i].a[:, WBE // 2:WBE], "bnc%d" % i) for i in range(2)])
        ssq = RR([k.sb("ssq%d" % i, [128, 1], F32) for i in range(4)])
        rstd = RR([k.sb("rstd%d" % i, [128, 1], F32) for i in range(4)])
        big = k.sb("big", [128, 22528], BF16)
        t512 = RR([k.sb("t512_%d" % i, [128, 516], F32) for i in range(5)])

        def norm_transpose(xr, hb, hba, col0, nwc, tgt, ncols=128, srccol=0):
            sq = ssq.get()
            rs = rstd.get()
            k.V([], [sq], lambda e: e.memset(sq.a, 0.0))
            k.A([xr, sq], [hb, sq], lambda e: e.activation(out=hba, in_=xr.a, func=AF.Square, accum_out=sq.a))
            k.A([sq, epsc], [rs], lambda e: e.activation(out=rs.a, in_=sq.a, func=AF.Sqrt, bias=epsc.a[:, 0:1], scale=1.0 / D))
            k.V([rs], [rs], lambda e: e.reciprocal(out=rs.a, in_=rs.a))
            k.A([xr, rs], [hb], lambda e: e.activation(out=hba, in_=xr.a, func=AF.Copy, scale=rs.a[:, 0:1]))
            for kg in range(KD // 4):
                pt = ptr.get()
                for qd in range(4):
                    kd = kg * 4 + qd
                    k.P([hb, identb], [pt], lambda e: e.transpose(pt.a[:, qd * 128:(qd + 1) * 128], hba[:, kd * 128:(kd + 1) * 128], identb.a))
                src = pt.a[:, 0:512].rearrange("p (q c) -> p q c", q=4)[:, :, srccol:srccol + ncols]
                dstv = tgt.a[:, kg * 4:kg * 4 + 4, col0:col0 + ncols]
                sc = nwc.a[:, kg * 4:kg * 4 + 4].unsqueeze(2).to_broadcast([128, 4, ncols])
                k.V([pt, nwc], [tgt], lambda e: e.tensor_tensor(out=dstv, in0=src, in1=sc, op=ALU.mult))

        def conv_silu(pb, ci, dst_tl, dst_ap, K, hal, cwt, cbt):
            pre = t512.get()
            acc = t512.get()
            H = K - 1
            W = TT
            k.A([hal], [pre], lambda e: e.activation(out=pre.a[:, 0:H], in_=hal.a[:, ci, 0:H], func=AF.Copy))
            k.A([pb], [pre], lambda e: e.activation(out=pre.a[:, H:H + W], in_=pb.a[:, 0:W], func=AF.Copy))
            k.V([pre, cwt, cbt], [acc], lambda e: e.tensor_scalar(out=acc.a[:, 0:W], in0=pre.a[:, H:H + W], scalar1=cwt.a[:, ci, K - 1:K], scalar2=cbt.a[:, ci:ci + 1], op0=ALU.mult, op1=ALU.add))
            for j in range(K - 1):
                k.V([pre, cwt, acc], [acc], lambda e: e.scalar_tensor_tensor(out=acc.a[:, 0:W], in0=pre.a[:, j:j + W], scalar=cwt.a[:, ci, j:j + 1], in1=acc.a[:, 0:W], op0=ALU.mult, op1=ALU.add))
            k.A([pre], [hal], lambda e: e.activation(out=hal.a[:, ci, 0:H], in_=pre.a[:, W:W + H], func=AF.Copy))
            k.A([acc], [dst_tl], lambda e: e.activation(out=dst_ap, in_=acc.a[:, 0:W], func=AF.Silu))

        def mm(out_tl, out_ap, l_tl, l_ap, r_tl, r_ap, start=True, stop=True):
            k.P([l_tl, r_tl], [out_tl], lambda e: e.matmul(out_ap, lhsT=l_ap, rhs=r_ap, start=start, stop=stop))

        es1 = contextlib.ExitStack()
        with es1:
            k.es = es1
            sv = big.a.bitcast(F32)

            def slot(i):
                return sv[:, i * TT:(i + 1) * TT]
            sstore = k.sb("sstore", [128, 2, TT], F32)
            zst = k.sb("zst", [128, 4, TT], BF16)
            halo = k.sb("halo", [128, 20, 3], F32)
            k.V([], [halo], lambda e: e.memset(halo.a, 0.0))
            o_g = k.sb("o_g", [128, 4, TT], BF16)
            o_s = k.sb("o_s", [128, 4, TT], BF16)
            yT = k.sb("yT", [128, 8, TT], BF16)
            hb1_ap = yT.a.rearrange("p b c -> p (b c)")[:, 0:D]
            S_g = [k.sb("S_g%d" % h, [128, 128], F32) for h in range(4)]
            S_s = [k.sb("S_s%d" % h, [128, 64], F32) for h in range(8)]
            for s_ in S_g + S_s:
                k.V([], [s_], lambda e: e.memset(s_.a, 0.0))
            mhp = [RR([k.sb("mh%d_%d" % (h, i), [128, 128], F32) for i in range(8)]) for h in range(4)]
            mlp = [RR([k.sb("ml%d_%d" % (h, i), [128, 128], F32) for i in range(5)]) for h in range(4)]
            mcb = [k.sb("mcb%d" % i, [128, 128], F32) for i in range(2)]
            sm12 = RR([k.sb("sm%d" % i, [128, 16], F32) for i in range(24)])
            ktok = [k.sb("ktok%d" % h, [128, 128], F32) for h in range(4)]
            vtok = [k.sb("vtok%d" % h, [128, 128], F32) for h in range(4)]
            btok = [k.sb("btok%d" % g, [128, 128], F32) for g in range(2)]
            xtok = k.sb("xtok", [128, 512], F32)

            def proj_block(cb, pb):
                wb = wlo.get()
                wv = wb.a[:, 0:KD * 128].rearrange("p (k c) -> p k c", k=KD)
                if t == 0:
                    k.dma("pool", wb, wv, None, w1_in[cb])
                    k.dma("sp", w1s, w1s.a[cb], wb, wb.a[:, 0:KD * 128])
                else:
                    k.dma("sp", wb, wb.a[:, 0:KD * 128], w1s, w1s.a[cb])
                for kd in range(KD):
                    k.P([wb, hnT], [pb], lambda e: e.matmul(pb.a, lhsT=wv[:, kd, :], rhs=hnT.a[:, kd, 0:TT], start=(kd == 0), stop=(kd == KD - 1)))
                return pb

            def l2norm(src_tl, src_ap, dst_tl, dst_ap, scale, bias_tl, pb):
                sq = t512.get()
                k.A([src_tl], [sq], lambda e: e.activation(out=sq.a[:, 0:TT], in_=src_ap, func=AF.Square))
                mm(pb, pb.a, ones, ones.a, sq, sq.a[:, 0:TT])
                rt = t512.get()
                k.A([pb, bias_tl], [rt], lambda e: e.activation(out=rt.a[:, 0:TT], in_=pb.a, func=AF.Sqrt, bias=bias_tl.a[:, 0:1], scale=scale))
                k.V([rt], [rt], lambda e: e.reciprocal(out=rt.a[:, 0:TT], in_=rt.a[:, 0:TT]))
                k.V([src_tl, rt], [dst_tl], lambda e: e.tensor_tensor(out=dst_ap, in0=src_ap, in1=rt.a[:, 0:TT], op=ALU.mult))

            def transpose_f(dst_tl, dst_ap, src_tl, src_ap):
                p = pq.get()
                k.P([src_tl, ident], [p], lambda e: e.transpose(p.a, src_ap, ident.a))
                k.A([p], [dst_tl], lambda e: e.activation(out=dst_ap, in_=p.a, func=AF.Copy))

            for t in range(NT):
                xr2 = TlView(big, big.a.bitcast(F32)[:, 0:D])
                xbufs = [xrow, xr2]

                def xload(s_):
                    xb = xbufs[s_ % 2]
                    k.dma("sp", xb, xb.a, None, x_in[t * TT + s_ * 128: t * TT + (s_ + 1) * 128, :])
                xload(0)
                xload(1)
                for s in range(4):
                    norm_transpose(xbufs[s % 2], yT, hb1_ap, s * 128, n1c, hnT)
                    if s + 2 < 4:
                        xload(s + 2)
                jobs = []
                for hh in range(4):
                    for r in range(4):
                        jobs.append(("g", hh, r))
                for sbk in range(12):
                    jobs.append(("s", sbk, 0))

                def post(job, pb, ji):
                    kind, a_, r = job
                    if kind == "g":
                        hh = a_
                        if r == 3:
                            k.A([pb], [big], lambda e: e.activation(out=slot(hh * 4 + 3), in_=pb.a, func=AF.Silu))
                        elif r == 2:
                            conv_silu(pb, hh * 3 + r, big, slot(hh * 4 + 2), 4, halo, cw, cbias)
                        else:
                            tmp = t512.get()
                            conv_silu(pb, hh * 3 + r, tmp, tmp.a[:, 0:TT], 4, halo, cw, cbias)
                            if r == 0:
                                l2norm(tmp, tmp.a[:, 0:TT], big, slot(hh * 4 + 0), 128.0, eps128, pqb[ji % 2])
                            else:
                                l2norm(tmp, tmp.a[:, 0:TT], big, slot(hh * 4 + 1), 1.0, epsc, pqb[ji % 2])
                    else:
                        sbk = a_
                        if sbk < 4:
                            k.A([pb], [zst], lambda e: e.activation(out=zst.a[:, sbk, :], in_=pb.a, func=AF.Silu))
                        elif sbk < 8:
                            conv_silu(pb, 12 + (sbk - 4), big, slot(16 + sbk - 4), 4, halo, cw, cbias)
                        elif sbk < 10:
                            conv_silu(pb, 12 + (sbk - 4), big, slot(20 + sbk - 8), 4, halo, cw, cbias)
                        else:
                            conv_silu(pb, 12 + (sbk - 4), sstore, sstore.a[:, sbk - 10, :], 4, halo, cw, cbias)

                prev = None
                for ji, job in enumerate(jobs):
                    cb = (job[1] * 4 + job[2]) if job[0] == "g" else 16 + job[1]
                    pb = proj_block(cb, pbig_l[ji % 2])
                    if prev is not None:
                        post(*prev)
                    prev = (job, pb, ji)
                post(*prev)
                if t >= 1:
                    emit_conv((len(conv_jobs) + NT - 2) // max(NT - 1, 1))
                for c4 in range(4):
                    cs = slice(c4 * 128, (c4 + 1) * 128)
                    psm = pq.get()
                    for kd in range(KD):
                        k.P([hnT, wsm], [psm], lambda e: e.matmul(psm.a[:, 0:16], lhsT=hnT.a[:, kd, cs], rhs=wsm.a[:, kd, :], start=(kd == 0), stop=(kd == KD - 1)))
                    sm = sm12.get()
                    k.A([psm], [sm], lambda e: e.activation(out=sm.a, in_=psm.a[:, 0:16], func=AF.Copy))
                    xs_ = sm12.get(); ax = sm12.get(); ex = sm12.get(); sp = sm12.get(); G = sm12.get(); beta = sm12.get()
                    k.V([sm, dtb], [xs_], lambda e: e.tensor_tensor(out=xs_.a[:, 0:12], in0=sm.a[:, 4:16], in1=dtb.a, op=ALU.add))
                    k.A([xs_], [ax], lambda e: e.activation(out=ax.a[:, 0:12], in_=xs_.a[:, 0:12], func=AF.Abs))
                    k.A([ax], [ex], lambda e: e.activation(out=ex.a[:, 0:12], in_=ax.a[:, 0:12], func=AF.Exp, scale=-1.0))
                    k.A([ex, onec], [ex], lambda e: e.activation(out=ex.a[:, 0:12], in_=ex.a[:, 0:12], func=AF.Ln, bias=onec.a[:, 0:1], scale=1.0))
                    k.V([xs_, ex], [sp], lambda e: e.scalar_tensor_tensor(out=sp.a[:, 0:12], in0=xs_.a[:, 0:12], scalar=0.0, in1=ex.a[:, 0:12], op0=ALU.max, op1=ALU.add))
                    k.V([sp, nA], [G], lambda e: e.tensor_tensor(out=G.a[:, 0:12], in0=sp.a[:, 0:12], in1=nA.a, op=ALU.mult))
                    k.A([sm], [beta], lambda e: e.activation(out=beta.a[:, 0:4], in_=sm.a[:, 0:4], func=AF.Sigmoid))
                    pc = pq.get()
                    mm(pc, pc.a[:, 0:12], triU, triU.a, G, G.a[:, 0:12])
                    mm(pc, pc.a[:, 16:28], ones, ones.a, G, G.a[:, 0:12])
                    gc = sm12.get(); gl = sm12.get(); eg = sm12.get(); kdc = sm12.get(); dbc = sm12.get(); bg = sm12.get(); dif = sm12.get()
                    k.A([pc], [gc], lambda e: e.activation(out=gc.a[:, 0:12], in_=pc.a[:, 0:12], func=AF.Copy))
                    k.A([pc], [gl], lambda e: e.activation(out=gl.a[:, 0:12], in_=pc.a[:, 16:28], func=AF.Copy))
                    k.A([gc], [eg], lambda e: e.activation(out=eg.a[:, 0:12], in_=gc.a[:, 0:12], func=AF.Exp))
                    k.V([gl, gc], [dif], lambda e: e.tensor_tensor(out=dif.a[:, 0:12], in0=gl.a[:, 0:12], in1=gc.a[:, 0:12], op=ALU.subtract))
                    k.A([dif], [kdc], lambda e: e.activation(out=kdc.a[:, 0:12], in_=dif.a[:, 0:12], func=AF.Exp))
                    k.A([gl], [dbc], lambda e: e.activation(out=dbc.a[:, 0:12], in_=gl.a[:, 0:12], func=AF.Exp))
                    k.V([beta, eg], [bg], lambda e: e.tensor_tensor(out=bg.a[:, 0:4], in0=beta.a[:, 0:4], in1=eg.a[:, 0:4], op=ALU.mult))
                    DMP = (t == 0 and c4 < 2)
                    if DMP:
                        tg = "_c%d" % c4
                        dump("sm" + tg, sm, sm.a, [128, 16]); dump("sp" + tg, sp, sp.a[:, 0:12], [128, 12]); dump("G" + tg, G, G.a[:, 0:12], [128, 12])
                        dump("gc" + tg, gc, gc.a[:, 0:12], [128, 12]); dump("gl" + tg, gl, gl.a[:, 0:12], [128, 12]); dump("beta" + tg, beta, beta.a[:, 0:4], [128, 4])
                        dump("kdc" + tg, kdc, kdc.a[:, 0:12], [128, 12]); dump("dbc" + tg, dbc, dbc.a[:, 0:12], [128, 12])
                        if c4 == 0:
                            for i_ in range(22):
                                dump("slot%d" % i_, big, slot(i_), [128, TT])
                            dump("sstore", sstore, sstore.a, [128, 2, TT])

                    def gdn_head(hh):
                        mh, ml = mhp[hh], mlp[hh]
                        qT = slot(hh * 4 + 0)[:, cs]
                        kT = slot(hh * 4 + 1)[:, cs]
                        vT = slot(hh * 4 + 2)[:, cs]
                        p1 = pq.get()
                        k.P([big, ident], [p1], lambda e: e.transpose(p1.a, kT, ident.a))
                        yield
                        k.A([p1], [ktok[hh]], lambda e: e.activation(out=ktok[hh].a, in_=p1.a, func=AF.Copy))
                        p2 = pq.get()
                        k.P([big, ident], [p2], lambda e: e.transpose(p2.a, vT, ident.a))
                        yield
                        k.A([p2], [vtok[hh]], lambda e: e.activation(out=vtok[hh].a, in_=p2.a, func=AF.Copy))
                        gb = mh.get()
                        k.V([ones, G], [gb], lambda e: e.tensor_scalar(out=gb.a, in0=ones.a, scalar1=G.a[:, hh:hh + 1], scalar2=None, op0=ALU.mult))
                        prow = pq.get()
                        mm(prow, prow.a, gb, gb.a, triU, triU.a)
                        yield
                        X = mh.get(); GT = mh.get(); XL = mh.get(); GL = mh.get(); Eg = mh.get()
                        k.V([prow, gc, negU], [X], lambda e: e.scalar_tensor_tensor(out=X.a, in0=prow.a, scalar=gc.a[:, hh:hh + 1], in1=negU.a, op0=ALU.subtract, op1=ALU.min))
                        k.A([X], [GT], lambda e: e.activation(out=GT.a, in_=X.a, func=AF.Exp))
                        k.V([prow, gc, posL], [XL], lambda e: e.scalar_tensor_tensor(out=XL.a, in0=prow.a, scalar=gc.a[:, hh:hh + 1], in1=posL.a, op0=ALU.subtract, op1=ALU.max))
                        k.A([XL], [GL], lambda e: e.activation(out=GL.a, in_=XL.a, func=AF.Exp, scale=-1.0))
                        k.A([prow], [Eg], lambda e: e.activation(out=Eg.a, in_=prow.a, func=AF.Exp))
                        pqk = pq.get()
                        mm(pqk, pqk.a, big, kT, big, qT)
                        yield
                        attnT = ml.get()
                        k.V([pqk, GT], [attnT], lambda e: e.tensor_tensor(out=attnT.a, in0=pqk.a, in1=GT.a, op=ALU.mult))
                        vb = ml.get(); kbg = ml.get(); kdec = ml.get(); qdec = ml.get()
                        k.V([vtok[hh], beta], [vb], lambda e: e.tensor_scalar(out=vb.a, in0=vtok[hh].a, scalar1=beta.a[:, hh:hh + 1], scalar2=None, op0=ALU.mult))
                        k.V([ktok[hh], bg], [kbg], lambda e: e.tensor_scalar(out=kbg.a, in0=ktok[hh].a, scalar1=bg.a[:, hh:hh + 1], scalar2=None, op0=ALU.mult))
                        k.V([ktok[hh], kdc], [kdec], lambda e: e.tensor_scalar(out=kdec.a, in0=ktok[hh].a, scalar1=kdc.a[:, hh:hh + 1], scalar2=None, op0=ALU.mult))
                        k.V([big, Eg], [qdec], lambda e: e.tensor_tensor(out=qdec.a, in0=qT, in1=Eg.a, op=ALU.mult))
                        pkk = pq.get()
                        mm(pkk, pkk.a, big, kT, big, kT)
                        yield
                        Am = mh.get()
                        k.V([pkk, beta, GL], [Am], lambda e: e.scalar_tensor_tensor(out=Am.a, in0=pkk.a, scalar=beta.a[:, hh:hh + 1], in1=GL.a, op0=ALU.mult, op1=ALU.mult))
                        pN = pq.get()
                        k.P([Am, ident], [pN], lambda e: e.transpose(pN.a, Am.a, ident.a))
                        yield
                        Nm = mh.get()
                        k.A([pN], [Nm], lambda e: e.activation(out=Nm.a, in_=pN.a, func=AF.Copy))
                        R = mh.get()
                        k.V([ident, Nm], [R], lambda e: e.tensor_tensor(out=R.a, in0=ident.a, in1=Nm.a, op=ALU.subtract))
                        Pm, PTm = Nm, Am
                        for lvl in range(1, 7):
                            pPT = pq.get()
                            mm(pPT, pPT.a, Pm, Pm.a, PTm, PTm.a)
                            yield
                            nPT = mh.get()
                            k.A([pPT], [nPT], lambda e: e.activation(out=nPT.a, in_=pPT.a, func=AF.Copy))
                            if lvl < 6:
                                pP = pq.get()
                                mm(pP, pP.a, PTm, PTm.a, Pm, Pm.a)
                                yield
                                nP = mh.get()
                                k.V([pP], [nP], lambda e: e.tensor_copy(out=nP.a, in_=pP.a))
                            pR = pq.get()
                            mm(pR, pR.a, nPT, nPT.a, R, R.a)
                            yield
                            nR = mh.get()
                            k.V([pR, R], [nR], lambda e: e.tensor_tensor(out=nR.a, in0=pR.a, in1=R.a, op=ALU.add))
                            R = nR
                            PTm = nPT
                            if lvl < 6:
                                Pm = nP
                        pw = pq.get()
                        mm(pw, pw.a, kbg, kbg.a, R, R.a)
                        yield
                        wTn = mh.get()
                        k.A([pw], [wTn], lambda e: e.activation(out=wTn.a, in_=pw.a, func=AF.Copy, scale=-1.0))
                        pv = pq.get()
                        mm(pv, pv.a, R, R.a, vb, vb.a, start=True, stop=False)
                        mm(pv, pv.a, wTn, wTn.a, S_g[hh], S_g[hh].a, start=False, stop=True)
                        yield
                        vnew = mh.get()
                        k.V([pv], [vnew], lambda e: e.tensor_copy(out=vnew.a, in_=pv.a))
                        po = pq.get()
                        mm(po, po.a, S_g[hh], S_g[hh].a, qdec, qdec.a, start=True, stop=False)
                        mm(po, po.a, vnew, vnew.a, attnT, attnT.a, start=False, stop=True)
                        yield
                        k.A([po], [o_g], lambda e: e.activation(out=o_g.a[:, hh, cs], in_=po.a, func=AF.Copy))
                        pS = pq.get()
                        mm(pS, pS.a, kdec, kdec.a, vnew, vnew.a)
                        yield
                        k.V([S_g[hh], dbc, pS], [S_g[hh]], lambda e: e.scalar_tensor_tensor(out=S_g[hh].a, in0=S_g[hh].a, scalar=dbc.a[:, hh:hh + 1], in1=pS.a, op0=ALU.mult, op1=ALU.add))

                    lockstep([gdn_head(hh) for hh in range(4)])

                    for g2 in range(2):
                        transpose_f(btok[g2], btok[g2].a, big, slot(20 + g2)[:, cs])
                    for blk in range(4):
                        transpose_f(xtok, xtok.a[:, blk * 128:(blk + 1) * 128], big, slot(16 + blk)[:, cs])
                    for g2 in range(2):
                        p = pq.get()
                        mm(p, p.a, big, slot(20 + g2)[:, cs], sstore, sstore.a[:, g2, cs])
                        k.A([p], [mcb[g2]], lambda e: e.activation(out=mcb[g2].a, in_=p.a, func=AF.Copy))

                    def ssd_head(h, pos_):
                        g2 = h // 4
                        col = 4 + h
                        mh = mhp[h % 4]
                        gb = mh.get()
                        k.V([ones, G], [gb], lambda e: e.tensor_scalar(out=gb.a, in0=ones.a, scalar1=G.a[:, col:col + 1], scalar2=None, op0=ALU.mult))
                        prow = pq.get()
                        mm(prow, prow.a, gb, gb.a, triU, triU.a)
                        yield
                        X = mh.get(); GT = mh.get(); Eg = mh.get()
                        k.V([prow, gc, negU], [X], lambda e: e.scalar_tensor_tensor(out=X.a, in0=prow.a, scalar=gc.a[:, col:col + 1], in1=negU.a, op0=ALU.subtract, op1=ALU.min))
                        k.A([X], [GT], lambda e: e.activation(out=GT.a, in_=X.a, func=AF.Exp))
                        k.A([prow], [Eg], lambda e: e.activation(out=Eg.a, in_=prow.a, func=AF.Exp))
                        attnT = mh.get(); qdec = mh.get(); kdec = mh.get(); vh = mh.get()
                        k.V([mcb[g2], GT], [attnT], lambda e: e.tensor_tensor(out=attnT.a, in0=mcb[g2].a, in1=GT.a, op=ALU.mult))
                        k.V([sstore, Eg], [qdec], lambda e: e.tensor_tensor(out=qdec.a, in0=sstore.a[:, g2, cs], in1=Eg.a, op=ALU.mult))
                        k.V([btok[g2], kdc], [kdec], lambda e: e.tensor_scalar(out=kdec.a, in0=btok[g2].a, scalar1=kdc.a[:, col:col + 1], scalar2=None, op0=ALU.mult))
                        k.V([xtok, sp], [vh], lambda e: e.tensor_scalar(out=vh.a[:, 0:64], in0=xtok.a[:, h * 64:(h + 1) * 64], scalar1=sp.a[:, col:col + 1], scalar2=None, op0=ALU.mult))
                        half = slice((h % 2) * 64, (h % 2) * 64 + 64)
                        mm(pos_, pos_.a[half, :], S_s[h], S_s[h].a, qdec, qdec.a, start=True, stop=False)
                        mm(pos_, pos_.a[half, :], vh, vh.a[:, 0:64], attnT, attnT.a, start=False, stop=True)
                        pS = pq.get()
                        mm(pS, pS.a[:, 0:64], kdec, kdec.a, vh, vh.a[:, 0:64])
                        yield
                        k.V([S_s[h], dbc, pS], [S_s[h]], lambda e: e.scalar_tensor_tensor(out=S_s[h].a, in0=S_s[h].a, scalar=dbc.a[:, col:col + 1], in1=pS.a[:, 0:64], op0=ALU.mult, op1=ALU.add))

                    for g2 in range(2):
                        pp = [pq.get(), pq.get()]
                        lockstep([ssd_head(g2 * 4 + i, pp[i // 2]) for i in range(4)])
                        for i2 in range(2):
                            k.A([pp[i2]], [o_s], lambda e: e.activation(out=o_s.a[:, g2 * 2 + i2, cs], in_=pp[i2].a, func=AF.Copy))

                for hh in range(4):
                    sq = t512.get()
                    k.A([o_g], [sq], lambda e: e.activation(out=sq.a[:, 0:TT], in_=o_g.a[:, hh, :], func=AF.Square))
                    pb = pbig.get()
                    mm(pb, pb.a, ones, ones.a, sq, sq.a[:, 0:TT])
                    rt = t512.get()
                    k.A([pb, epsc], [rt], lambda e: e.activation(out=rt.a[:, 0:TT], in_=pb.a, func=AF.Sqrt, bias=epsc.a[:, 0:1], scale=1.0 / 128))
                    k.V([rt], [rt], lambda e: e.reciprocal(out=rt.a[:, 0:TT], in_=rt.a[:, 0:TT]))
                    k.V([o_g, rt], [rt], lambda e: e.tensor_tensor(out=rt.a[:, 0:TT], in0=o_g.a[:, hh, :], in1=rt.a[:, 0:TT], op=ALU.mult))
                    k.V([rt, gdnnw, big], [yT], lambda e: e.scalar_tensor_tensor(out=yT.a[:, hh, :], in0=rt.a[:, 0:TT], scalar=gdnnw.a[:, 0:1], in1=slot(hh * 4 + 3), op0=ALU.mult, op1=ALU.mult))
                for g2 in range(2):
                    ys = []
                    pb = pbig.get()
                    for bi in range(2):
                        blk = g2 * 2 + bi
                        y0 = t512.get()
                        k.V([big, dcol, o_s], [y0], lambda e: e.scalar_tensor_tensor(out=y0.a[:, 0:TT], in0=slot(16 + blk), scalar=dcol.a[:, blk:blk + 1], in1=o_s.a[:, blk, :], op0=ALU.mult, op1=ALU.add))
                        k.V([y0, zst], [y0], lambda e: e.tensor_tensor(out=y0.a[:, 0:TT], in0=y0.a[:, 0:TT], in1=zst.a[:, blk, :], op=ALU.mult))
                        sq = t512.get()
                        k.A([y0], [sq], lambda e: e.activation(out=sq.a[:, 0:TT], in_=y0.a[:, 0:TT], func=AF.Square))
                        mm(pb, pb.a, ones, ones.a, sq, sq.a[:, 0:TT], start=(bi == 0), stop=(bi == 1))
                        ys.append(y0)
                    rt = t512.get()
                    k.A([pb, epsc], [rt], lambda e: e.activation(out=rt.a[:, 0:TT], in_=pb.a, func=AF.Sqrt, bias=epsc.a[:, 0:1], scale=1.0 / 256))
                    k.V([rt], [rt], lambda e: e.reciprocal(out=rt.a[:, 0:TT], in_=rt.a[:, 0:TT]))
                    for bi in range(2):
                        blk = g2 * 2 + bi
                        y0 = ys[bi]
                        k.V([y0, ssmnw, rt], [yT], lambda e: e.scalar_tensor_tensor(out=yT.a[:, 4 + blk, :], in0=y0.a[:, 0:TT], scalar=ssmnw.a[:, blk:blk + 1], in1=rt.a[:, 0:TT], op0=ALU.mult, op1=ALU.mult))
                if t == 0:
                    dump("o_g", o_g, o_g.a, [128, 4, TT]); dump("o_s", o_s, o_s.a, [128, 4, TT])
                k.dma("sp", ybuf, ybuf.a[t].rearrange("(b p) c -> p b c", p=128), yT, yT.a)
                if dbg:
                    k.dma("sp", ydbg, ydbg.a[t].rearrange("(b p) c -> p b c", p=128), yT, yT.a)
                k.collective(ybuf, gath, ybuf.a[t], gath.a[t * 4096:(t + 1) * 4096, :], cfg["groups"], gath)
            emit_conv(len(conv_jobs))
            zt = t512.get()
            k.V([], [zt], lambda e: e.memset(zt.a, 0.0))
            zb16 = zt.a.bitcast(BF16)[:, 0:TT]
            for i in range(32):
                k.dma("sp", gath, gath.a[NT * 4096 + i * 128: NT * 4096 + (i + 1) * 128, :], zt, zb16, semtl=zt)
            k.barrier()
        k.es = es

        es3 = contextlib.ExitStack()
        if cfg.get("stop_after", 3) == 1:
            k.finish([ydbg, gath] + dumped)
            k.barrier()
            return nc
        with es3:
            k.es = es3
            rowb = k.sb("rowb", [128, D], F32)
            wblk3 = RR([wblk_l[0], wblk_l[1], k.sb("wblk2", [128, WBE], BF16)])
            hb3 = k.sb("hb3", [128, D], BF16)
            halo2 = k.sb("halo2", [128, FB, 2], F32)
            piece = RR([k.sb("piece%d" % i, [128, 512], F32) for i in range(8)])
            ssq3 = k.sb("ssqf", [128, 4, NG], F32)
            actT = big.a.rearrange("p (f c) -> p f c", c=TT)
            y_sb = big.a[:, 0:KM * 640].rearrange("p (k c) -> p k c", k=KM)
            accs = [pbig_l[0], pbig_l[1], pacc[0], pacc[1]]
            pbB = RR([pbig_l[0], pbig_l[1], pacc[0], pacc[1]])
            WSL = WBE // 512
            for m in range(NT3):
                g0 = 0 if m == 0 else 1
                for kd in range(KM):
                    if m == 0:
                        k.dma("pool", big, y_sb[:, kd, 0:128], gath, gath.a.rearrange("r (q c) -> (r q) c", c=128), kind="ind",
                              in_off=bass.IndirectOffsetOnAxis(ap=idx.a[:, kd:kd + 1], axis=0))
                    k.dma("pool", big, y_sb[:, kd, 128:640], gath, gath.a[:, :], kind="ind",
                          in_off=bass.IndirectOffsetOnAxis(ap=idx.a[:, (m + 1) * KM + kd:(m + 1) * KM + kd + 1], axis=0))
                for g in range(NG):
                    for hf in range(2):
                        wb = wblk3.get()
                        wv = wb.a.rearrange("p (k c) -> p k c", k=KM)
                        k.dma("sp", wb, wb.a[:, 0:KM * 256], wos, wos.a[g * 2 + hf])
                        for gi in range(g0, 5):
                            xp = piece.get()
                            r0 = (0 if gi == 0 else 128 + m * TT + (gi - 1) * 128)
                            c0 = g * 512 + hf * 256
                            k.dma("pool", xp, xp.a[:, 0:256], None, xq_in[r0:r0 + 128, c0:c0 + 256])
                            pb = pbig.get()
                            for kd in range(KM):
                                k.P([big, wb], [pb], lambda e: e.matmul(pb.a[:, 0:256], lhsT=y_sb[:, kd, gi * 128:(gi + 1) * 128], rhs=wv[:, kd, :], start=(kd == 0), stop=(kd == KM - 1)))
                            k.V([pb, xp], [xp], lambda e: e.tensor_tensor(out=xp.a[:, 0:256], in0=pb.a[:, 0:256], in1=xp.a[:, 0:256], op=ALU.add))
                            k.dma("pool", hbuf, hbuf.a[r0:r0 + 128, c0:c0 + 256], xp, xp.a[:, 0:256])
                            if dbg:
                                k.dma("pool", hdbg, hdbg.a[r0:r0 + 128, c0:c0 + 256], xp, xp.a[:, 0:256])
                for gi in range(g0, 5):
                    r0 = (0 if gi == 0 else 128 + m * TT + (gi - 1) * 128)
                    xb = xrow if gi % 2 == 0 else rowb
                    k.dma("sp", xb, xb.a, hbuf, hbuf.a[r0:r0 + 128, :])
                    if gi == 0:
                        norm_transpose(xb, hb3, hb3.a, 0, n2c, hnT, ncols=2, srccol=126)
                    else:
                        norm_transpose(xb, hb3, hb3.a, 2 + (gi - 1) * 128, n2c, hnT)
                k.V([], [ssq3], lambda e: e.memset(ssq3.a, 0.0))
                for hf in range(2):
                    for fl in range(FH):
                        fb = hf * FH + fl
                        wb = wblk3.get()
                        wv = wb.a[:, 0:2 * KD * 128].rearrange("p (u k c) -> p u k c", u=2, k=KD)
                        k.dma("sp", wb, wb.a[:, 0:2 * KD * 128], wgs, wgs.a[fb])
                        pg = pbB.get()
                        for kd in range(KD):
                            k.P([wb, hnT], [pg], lambda e: e.matmul(pg.a, lhsT=wv[:, 0, kd, :], rhs=hnT.a[:, kd, 2:2 + TT], start=(kd == 0), stop=(kd == KD - 1)))
                        if m == 0:
                            ph = pq3.get()
                            for kd in range(KD):
                                k.P([wb, hnT], [ph], lambda e: e.matmul(ph.a[:, 0:2], lhsT=wv[:, 0, kd, :], rhs=hnT.a[:, kd, 0:2], start=(kd == 0), stop=(kd == KD - 1)))
                            k.A([ph], [halo2], lambda e: e.activation(out=halo2.a[:, fb, :], in_=ph.a[:, 0:2], func=AF.Copy))
                        pu = pbB.get()
                        for kd in range(KD):
                            k.P([wb, hnT], [pu], lambda e: e.matmul(pu.a, lhsT=wv[:, 1, kd, :], rhs=hnT.a[:, kd, 2:2 + TT], start=(kd == 0), stop=(kd == KD - 1)))
                        sg = t512.get()
                        conv_silu(pg, fb, sg, sg.a[:, 0:TT], 3, halo2, fcw, fcb)
                        k.V([sg, pu], [big], lambda e: e.tensor_tensor(out=actT[:, fl, :], in0=sg.a[:, 0:TT], in1=pu.a, op=ALU.mult))
                    for g in range(NG):
                        nsl = (FH + WSL - 1) // WSL
                        hps = []
                        for i in range(4):
                            r0 = m * TT + i * 128
                            hp = piece.get()
                            if hf == 0:
                                k.dma("pool", hp, hp.a, hbuf, hbuf.a[128 + r0:128 + r0 + 128, g * 512:(g + 1) * 512])
                            else:
                                k.dma("pool", hp, hp.a, out_d, out_d.a[r0:r0 + 128, g * 512:(g + 1) * 512])
                            hps.append(hp)
                        for sl in range(nsl):
                            f0 = sl * WSL
                            nf = min(WSL, FH - f0)
                            wb = wblk3.get()
                            wv = wb.a.rearrange("p (f c) -> p f c", c=512)
                            r0 = (hf * FH + f0) * 128
                            si = (hf * NG + g) * NSL + sl
                            k.dma("sp", wb, wb.a[:, 0:nf * 512], wds, wds.a[si][:, 0:nf * 512])
                            for i in range(4):
                                for f in range(nf):
                                    k.P([big, wb], [accs[i]], lambda e: e.matmul(accs[i].a, lhsT=actT[:, f0 + f, i * 128:(i + 1) * 128], rhs=wv[:, f, :], start=(f0 + f == 0), stop=(f0 + f == FH - 1)))
                        for i in range(4):
                            r0 = m * TT + i * 128
                            hp = hps[i]
                            k.V([accs[i], hp], [hp], lambda e: e.tensor_tensor(out=hp.a, in0=accs[i].a, in1=hp.a, op=ALU.add))
                            if hf == 1:
                                jk = t512.get()
                                k.A([hp, ssq3], [jk, ssq3], lambda e: e.activation(out=jk.a[:, 0:512], in_=hp.a, func=AF.Square, accum_out=ssq3.a[:, i, g:g + 1]))
                            k.dma("pool", out_d, out_d.a[r0:r0 + 128, g * 512:(g + 1) * 512], hp, hp.a)
                k.dma("sp", xrow, xrow.a, None, nfb_in.partition_broadcast(128))
                for i in range(4):
                    r0 = m * TT + i * 128
                    tot = ssq.get(); rs = rstd.get()
                    k.V([ssq3], [tot], lambda e: e.tensor_reduce(out=tot.a, in_=ssq3.a[:, i, :], axis=mybir.AxisListType.X, op=ALU.add))
                    k.A([tot, epsc], [rs], lambda e: e.activation(out=rs.a, in_=tot.a, func=AF.Sqrt, bias=epsc.a[:, 0:1], scale=1.0 / D))
                    k.V([rs], [rs], lambda e: e.reciprocal(out=rs.a, in_=rs.a))
                    k.dma("pool", rowb, rowb.a, out_d, out_d.a[r0:r0 + 128, :])
                    k.V([rowb, rs, xrow], [rowb], lambda e: e.scalar_tensor_tensor(out=rowb.a, in0=rowb.a, scalar=rs.a[:, 0:1], in1=xrow.a, op0=ALU.mult, op1=ALU.mult))
                    k.dma("pool", out_d, out_d.a[r0:r0 + 128, :], rowb, rowb.a)
            fin = [out_d]
            if dbg:
                fin += [ydbg, hdbg]
            k.finish(fin)
            k.barrier()
        k.es = es
    return nc


Q0, K0, V0, GATE0, BETA0, ALPHA0, Z0, XBC0, DT0 = 0, 2048, 4096, 6144, 8192, 8208, 8224, 10272, 14368
GROUPS = [[0, 1, 2, 3], [4, 5, 6, 7]]


def _masks():
    p = np.arange(128)[:, None]
    f = np.arange(128)[None, :]
    ident = (p == f).astype(np.float32)
    triU = (p <= f).astype(np.float32)
    negU = np.where(f >= p, 0.0, NEG).astype(np.float32)
    posL = np.where(p > f, 0.0, -NEG).astype(np.float32)
    return np.ascontiguousarray(np.stack([ident, triU, negU, posL]))


def prep_inputs(inp, cfg):
    L, D, DFF = cfg["L"], cfg["D"], cfg["DFF"]
    KD, FB, NG = D // 128, DFF // 128, D // 512
    LQ = L // 4
    NT, NT3 = L // TT, LQ // TT
    f32 = np.float32
    x = np.asarray(inp["x"], f32)
    w_in = np.asarray(inp["w_in"], f32)[0]
    w_out = np.asarray(inp["w_out"], f32)[0]
    gcw = np.asarray(inp["gdn_conv_w"], f32)[0]
    scw = np.asarray(inp["ssm_conv_w"], f32)[0]
    scb = np.asarray(inp["ssm_conv_b"], f32)[0]
    ar = np.arange(128)
    masks = _masks()
    wgate = np.asarray(inp["ffn_w_gate"], f32)[0].reshape(KD, 128, FB, 128).transpose(2, 1, 0, 3)
    wup = np.asarray(inp["ffn_w_up"], f32)[0].reshape(KD, 128, FB, 128).transpose(2, 1, 0, 3)
    wg = np.ascontiguousarray(np.stack([wgate, wup], axis=2))
    wd = np.ascontiguousarray(np.asarray(inp["ffn_w_down"], f32)[0])
    fcw = np.ascontiguousarray(np.asarray(inp["ffn_conv_w"], f32)[0].reshape(3, FB, 128).transpose(2, 1, 0))
    fcb = np.ascontiguousarray(np.asarray(inp["ffn_conv_b"], f32)[0].reshape(FB, 128).T)
    n1c = np.ascontiguousarray(np.asarray(inp["norm1_w"], f32)[0].reshape(KD, 128).T)
    n2c = np.ascontiguousarray(np.asarray(inp["norm2_w"], f32)[0].reshape(KD, 128).T)
    nfb = np.ascontiguousarray(np.asarray(inp["norm_f_w"], f32))
    gdnnw = np.ascontiguousarray(np.asarray(inp["gdn_norm_w"], f32)[0].reshape(128, 1))
    perm = []
    for r in range(4):
        for hh in range(4):
            perm.append(128 * (4 * r + hh) + ar)
        perm.append(2048 + 512 * r + np.arange(512))
    perm = np.concatenate(perm)
    wo = np.ascontiguousarray(w_out[perm].reshape(KM, 128, NG, 512).transpose(2, 1, 0, 3))
    maps = []
    for c in range(8):
        b, j = c // 4, c % 4
        cols = []
        for hh in range(4):
            H = 4 * j + hh
            cols += [Q0 + 128 * H + ar, K0 + 128 * H + ar, V0 + 128 * H + ar, GATE0 + 128 * H + ar]
        for zb in range(4):
            cols.append(Z0 + 512 * j + zb * 128 + ar)
        for blk in range(4):
            cols.append(XBC0 + 512 * j + blk * 128 + ar)
        for g in range(2):
            cols.append(XBC0 + 2048 + 128 * (2 * j + g) + ar)
        for g in range(2):
            cols.append(XBC0 + 3072 + 128 * (2 * j + g) + ar)
        cols = np.concatenate(cols)
        w1 = np.ascontiguousarray(w_in[:, cols].reshape(KD, 128, 28, 128).transpose(2, 1, 0, 3))
        small = np.concatenate([BETA0 + 4 * j + np.arange(4), ALPHA0 + 4 * j + np.arange(4), DT0 + 8 * j + np.arange(8)])
        wsm = np.ascontiguousarray(w_in[:, small].reshape(KD, 128, 16).transpose(1, 0, 2))
        cw = np.zeros((128, 20, 4), f32)
        cb = np.zeros((128, 20), f32)
        for hh in range(4):
            H = 4 * j + hh
            for r in range(3):
                cw[:, hh * 3 + r, :] = gcw[:, r * 2048 + 128 * H + ar].T
        for i in range(8):
            if i < 4:
                ch = 512 * j + i * 128 + ar
            elif i < 6:
                ch = 2048 + 128 * (2 * j + (i - 4)) + ar
            else:
                ch = 3072 + 128 * (2 * j + (i - 6)) + ar
            cw[:, 12 + i, :] = scw[:, ch].T
            cb[:, 12 + i] = scb[ch]
        dtb = np.concatenate([np.asarray(inp["gdn_dt_bias"], f32)[0][4 * j:4 * j + 4], np.asarray(inp["ssm_dt_bias"], f32)[0][8 * j:8 * j + 8]])
        alog = np.concatenate([np.asarray(inp["gdn_A_log"], f32)[0][4 * j:4 * j + 4], np.asarray(inp["ssm_A_log"], f32)[0][8 * j:8 * j + 8]])
        ssd_D = np.asarray(inp["ssm_D"], f32)[0]
        dcol = np.zeros((128, 4), f32)
        for blk in range(4):
            dcol[:64, blk] = ssd_D[8 * j + 2 * blk]
            dcol[64:, blk] = ssd_D[8 * j + 2 * blk + 1]
        ssmnw = np.ascontiguousarray(np.asarray(inp["ssm_norm_w"], f32)[0][512 * j:512 * (j + 1)].reshape(4, 128).T)
        xq = np.zeros((128 + LQ, D), f32)
        if j > 0:
            xq[:] = x[b, j * LQ - 128:(j + 1) * LQ]
        else:
            xq[128:] = x[b, 0:LQ]
        idx = np.zeros((128, (NT3 + 1) * KM), np.int32)
        for blk in range(NT3 + 1):
            if blk == 0:
                tile = j * NT3 - 1 if j > 0 else NT
            else:
                tile = j * NT3 + (blk - 1)
            for kd in range(KM):
                idx[:, blk * KM + kd] = tile * 4096 + kd * 128 + ar
                if blk == 0:
                    idx[:, kd] = idx[:, kd] * 4 + 3
        maps.append({
            "x": np.ascontiguousarray(x[b]), "xq": xq, "w1": w1, "wsm": wsm, "n1c": n1c, "n2c": n2c, "nfb": nfb,
            "cw": cw, "cb": cb, "dtb": np.ascontiguousarray(np.tile(dtb[None, :], (128, 1))),
            "alog": np.ascontiguousarray(np.tile(alog[None, :], (128, 1))), "dcol": dcol, "ssmnw": ssmnw, "gdnnw": gdnnw,
            "masks": masks, "wo": wo, "wg": wg, "wd": wd, "fcw": fcw, "fcb": fcb, "idx": idx,
        })
    return maps


def run(inp, cfg):
    nc = build_program(cfg)
    maps = prep_inputs(inp, cfg)
    res = run_bass_kernel_spmd(nc, maps, core_ids=list(range(8)))
    return res


def kernel(**inputs):
    cfg = {"L": 8192, "D": 4096, "DFF": 11008, "groups": GROUPS}
    res = run(inputs, cfg)
    L, D = cfg["L"], cfg["D"]
    LQ = L // 4
    out = np.zeros((2, L, D), np.float32)
    for c in range(8):
        b, j = c // 4, c % 4
        out[b, j * LQ:(j + 1) * LQ] = res.results[c]["out"]
    return out
```
